# Optimizing a Trainium2 kernel written in Bass

```python
import jax, jax.numpy as jnp
from jax import lax
import numpy as np

D_MODEL = 2048
BATCH = 2
SEQ = 8192
DEPTH = 1

CHUNK = 64
HEAD_DIM = 64
D_SSD = D_MODEL
D_RWKV = D_MODEL
D_MIX = D_SSD + D_RWKV
SSD_HEADS = D_SSD // HEAD_DIM
SSD_GROUPS = 4
SSD_HPG = SSD_HEADS // SSD_GROUPS
SSD_STATE = 128
CONV_WIDTH = 4
D_XBC = D_SSD + 2 * SSD_GROUPS * SSD_STATE
RWKV_HEADS = D_RWKV // HEAD_DIM
DECAY_LORA = 96
AAA_LORA = 96
GATE_LORA = 256
D_RWKV_IN = 3 * D_RWKV + DECAY_LORA + AAA_LORA + GATE_LORA
D_IN = D_SSD + D_XBC + SSD_HEADS + D_RWKV_IN
D_FF = -(-8 * D_MODEL // (3 * 256)) * 256
RMS_EPS = 1e-6
GATED_NORM_EPS = 1e-5
GN_EPS = 64e-5

kernel_name = "hymba_style_ssd_rwkv7_hybrid_block"


def rms_norm(x, g, eps=RMS_EPS):
    xf = x.astype(jnp.float32)
    y = xf * lax.rsqrt(jnp.mean(xf * xf, axis=-1, keepdims=True) + eps)
    return (y * g.astype(jnp.float32)).astype(x.dtype)


def causal_depthwise_conv(u, w, b):
    c = u.shape[-1]
    y = lax.conv_general_dilated(u, w[:, None, :].astype(u.dtype), window_strides=(1,),
                                 padding=((w.shape[0] - 1, 0),),
                                 dimension_numbers=('NWC', 'WIO', 'NWC'),
                                 feature_group_count=c)
    return y + b.astype(u.dtype)


def segsum(a):
    t = a.shape[-1]
    rep = jnp.broadcast_to(a[..., None], a.shape + (t,))
    rep = jnp.where(jnp.tril(jnp.ones((t, t), bool), -1), rep, 0.0)
    cs = jnp.cumsum(rep, axis=-2)
    return jnp.where(jnp.tril(jnp.ones((t, t), bool)), cs, -jnp.inf)


def ssd_chunked(xh, dt, A, Bm, Cm):
    b, s, g, e, p = xh.shape
    n = Bm.shape[-1]
    c = s // CHUNK
    X = (xh * dt[..., None]).reshape(b, c, CHUNK, g, e, p)
    Adt = (dt * A).reshape(b, c, CHUNK, g, e).transpose(0, 3, 4, 1, 2)
    Bc = Bm.reshape(b, c, CHUNK, g, n)
    Cc = Cm.reshape(b, c, CHUNK, g, n)
    A_cs = jnp.cumsum(Adt, axis=-1)
    L = jnp.exp(segsum(Adt))
    CB = jnp.einsum('bclgn,bcsgn->bcgls', Cc, Bc)
    y_diag = jnp.einsum('bcgls,bgecls,bcsgep->bclgep', CB, L, X)
    decay_states = jnp.exp(A_cs[..., -1:] - A_cs)
    states = jnp.einsum('bclgn,bgecl,bclgep->bcgepn', Bc, decay_states, X)
    chunk_decay = jnp.exp(A_cs[..., -1])

    def step(h, inp):
        st, dec = inp
        return h * dec[..., None, None] + st, h

    h0 = jnp.zeros((b, g, e, p, n), X.dtype)
    _, h_in = lax.scan(step, h0, (states.transpose(1, 0, 2, 3, 4, 5), chunk_decay.transpose(3, 0, 1, 2)))
    h_in = h_in.transpose(1, 0, 2, 3, 4, 5)
    y_off = jnp.einsum('bclgn,bcgepn,bgecl->bclgep', Cc, h_in, jnp.exp(A_cs))
    return (y_diag + y_off).reshape(b, s, g, e, p)


def ssd_mixer(z, xbc, dt_raw, conv_w, conv_b, dt_bias, A_log, D_skip, norm_g):
    b, s, _ = z.shape
    f32 = jnp.float32
    xbc = jax.nn.silu(causal_depthwise_conv(xbc, conv_w, conv_b)).astype(f32)
    xs = xbc[..., :D_SSD]
    Bm = xbc[..., D_SSD:D_SSD + SSD_GROUPS * SSD_STATE].reshape(b, s, SSD_GROUPS, SSD_STATE)
    Cm = xbc[..., D_SSD + SSD_GROUPS * SSD_STATE:].reshape(b, s, SSD_GROUPS, SSD_STATE)
    xh = xs.reshape(b, s, SSD_GROUPS, SSD_HPG, HEAD_DIM)
    dt = jax.nn.softplus(dt_raw.astype(f32) + dt_bias.astype(f32)).reshape(b, s, SSD_GROUPS, SSD_HPG)
    A = -jnp.exp(A_log.astype(f32)).reshape(SSD_GROUPS, SSD_HPG)
    y = ssd_chunked(xh, dt, A, Bm, Cm)
    y = y + D_skip.astype(f32).reshape(SSD_GROUPS, SSD_HPG)[..., None] * xh
    y = y.reshape(b, s, D_SSD) * jax.nn.silu(z.astype(f32))
    yg = y.reshape(b, s, SSD_GROUPS, D_SSD // SSD_GROUPS)
    yg = yg * lax.rsqrt(jnp.mean(yg * yg, axis=-1, keepdims=True) + GATED_NORM_EPS)
    return (yg.reshape(b, s, D_SSD) * norm_g.astype(f32)).astype(z.dtype)


def wkv7_scan(r, w, k, v, a, bb):
    bsz, _, h, n = r.shape

    def step(S, inp):
        r_t, w_t, k_t, v_t, a_t, b_t = inp
        sa = jnp.einsum('bhvk,bhk->bhv', S, a_t)
        S = S * w_t[:, :, None, :] + sa[..., None] * b_t[:, :, None, :] + v_t[..., None] * k_t[:, :, None, :]
        return S, jnp.einsum('bhvk,bhk->bhv', S, r_t)

    seq = (r.swapaxes(0, 1), w.swapaxes(0, 1), k.swapaxes(0, 1), v.swapaxes(0, 1),
           a.swapaxes(0, 1), bb.swapaxes(0, 1))
    S0 = jnp.zeros((bsz, h, n, n), jnp.float32)
    _, y = lax.scan(step, S0, seq)
    return y.swapaxes(0, 1)


def rwkv7_mixer(r, k, v, wd, ad, gd, w0, w2, a0, a2, g2, k_k, k_a, r_k, gn_w, gn_b):
    f32 = jnp.float32
    b, s, _ = r.shape
    H, N = RWKV_HEADS, HEAD_DIM
    r, k, v, wd, ad, gd = (t.astype(f32) for t in (r, k, v, wd, ad, gd))
    logw = -jax.nn.softplus(-(w0.astype(f32) + jnp.tanh(wd) @ w2.astype(f32))) - 0.5
    decay = jnp.exp(-jnp.exp(logw))
    a = jax.nn.sigmoid(a0.astype(f32) + ad @ a2.astype(f32))
    g = jax.nn.sigmoid(gd) @ g2.astype(f32)
    kk = (k * k_k.astype(f32)).reshape(b, s, H, N)
    kk = kk / jnp.maximum(jnp.sqrt(jnp.sum(kk * kk, axis=-1, keepdims=True)), 1e-12)
    k = k * (1.0 + (a - 1.0) * k_a.astype(f32))
    rh = r.reshape(b, s, H, N)
    kh = k.reshape(b, s, H, N)
    vh = v.reshape(b, s, H, N)
    ah = a.reshape(b, s, H, N)
    y = wkv7_scan(rh, decay.reshape(b, s, H, N), kh, vh, -kk, kk * ah)
    mu = jnp.mean(y, axis=-1, keepdims=True)
    var = jnp.mean(jnp.square(y - mu), axis=-1, keepdims=True)
    y = ((y - mu) * lax.rsqrt(var + GN_EPS)).reshape(b, s, D_RWKV) * gn_w.astype(f32) + gn_b.astype(f32)
    bonus = jnp.sum(rh * kh * r_k.astype(f32).reshape(H, N), axis=-1, keepdims=True) * vh
    y = y + bonus.reshape(b, s, D_RWKV)
    return y * g


def setup_inputs(seed: int = 0) -> dict:
    key = jax.random.key(seed)
    ks = jax.random.split(key, 32)
    f32 = jnp.float32
    L = DEPTH

    def nrm(k, shape, scale):
        return jax.random.normal(k, shape, f32) * scale

    dt0 = jnp.exp(jax.random.uniform(ks[5], (L, SSD_HEADS), f32, np.log(1e-3), np.log(1e-1)))
    w0_base = jnp.linspace(-6.0, -1.0, D_RWKV, dtype=f32) + 0.5
    return {
        "x": jax.random.normal(ks[0], (BATCH, SEQ, D_MODEL), f32),
        "norm1_g": 1.0 + nrm(ks[1], (L, D_MODEL), 0.02),
        "w_in": nrm(ks[2], (L, D_MODEL, D_IN), D_MODEL ** -0.5),
        "ssd_conv_w": nrm(ks[3], (L, CONV_WIDTH, D_XBC), CONV_WIDTH ** -0.5),
        "ssd_conv_b": nrm(ks[4], (L, D_XBC), 0.02),
        "ssd_dt_bias": dt0 + jnp.log(-jnp.expm1(-dt0)),
        "ssd_A_log": jnp.log(jax.random.uniform(ks[6], (L, SSD_HEADS), f32, 1.0, 16.0)),
        "ssd_D": 1.0 + nrm(ks[7], (L, SSD_HEADS), 0.1),
        "ssd_norm_g": 1.0 + nrm(ks[8], (L, D_SSD), 0.02),
        "rwkv_mu": jax.random.uniform(ks[9], (L, D_RWKV_IN), f32),
        "rwkv_w0": w0_base + nrm(ks[10], (L, D_RWKV), 0.1),
        "rwkv_w2": nrm(ks[11], (L, DECAY_LORA, D_RWKV), 0.1 * DECAY_LORA ** -0.5),
        "rwkv_a0": nrm(ks[12], (L, D_RWKV), 0.1),
        "rwkv_a2": nrm(ks[13], (L, AAA_LORA, D_RWKV), 0.1 * AAA_LORA ** -0.5),
        "rwkv_g2": nrm(ks[14], (L, GATE_LORA, D_RWKV), GATE_LORA ** -0.5),
        "rwkv_k_k": 0.85 + nrm(ks[15], (L, D_RWKV), 0.05),
        "rwkv_k_a": 1.0 + nrm(ks[16], (L, D_RWKV), 0.05),
        "rwkv_r_k": nrm(ks[17], (L, D_RWKV), 0.1),
        "rwkv_gn_w": 1.0 + nrm(ks[18], (L, D_RWKV), 0.02),
        "rwkv_gn_b": nrm(ks[19], (L, D_RWKV), 0.02),
        "w_out": nrm(ks[20], (L, D_MIX, D_MODEL), D_MIX ** -0.5),
        "norm2_g": 1.0 + nrm(ks[21], (L, D_MODEL), 0.02),
        "w_gate": nrm(ks[22], (L, D_MODEL, D_FF), D_MODEL ** -0.5),
        "w_up": nrm(ks[23], (L, D_MODEL, D_FF), D_MODEL ** -0.5),
        "w_down": nrm(ks[24], (L, D_FF, D_MODEL), D_FF ** -0.5),
        "norm_f_g": 1.0 + nrm(ks[25], (D_MODEL,), 0.02),
    }


def reference(x, norm1_g, w_in, ssd_conv_w, ssd_conv_b, ssd_dt_bias, ssd_A_log, ssd_D, ssd_norm_g,
              rwkv_mu, rwkv_w0, rwkv_w2, rwkv_a0, rwkv_a2, rwkv_g2, rwkv_k_k, rwkv_k_a, rwkv_r_k,
              rwkv_gn_w, rwkv_gn_b, w_out, norm2_g, w_gate, w_up, w_down, norm_f_g):
    h = x
    for l in range(DEPTH):
        u = rms_norm(h, norm1_g[l])
        P = u @ w_in[l]
        o1 = D_SSD
        o2 = o1 + D_XBC
        o3 = o2 + SSD_HEADS
        z = P[..., :o1]
        xbc = P[..., o1:o2]
        dt_raw = P[..., o2:o3]
        pr = P[..., o3:]
        prev = jnp.pad(pr, ((0, 0), (1, 0), (0, 0)))[:, :-1]
        pr = pr + (prev - pr) * rwkv_mu[l]
        c1 = D_RWKV
        c2 = 2 * D_RWKV
        c3 = 3 * D_RWKV
        c4 = c3 + DECAY_LORA
        c5 = c4 + AAA_LORA
        y_ssd = ssd_mixer(z, xbc, dt_raw, ssd_conv_w[l], ssd_conv_b[l], ssd_dt_bias[l],
                          ssd_A_log[l], ssd_D[l], ssd_norm_g[l])
        y_rwkv = rwkv7_mixer(pr[..., :c1], pr[..., c1:c2], pr[..., c2:c3], pr[..., c3:c4],
                             pr[..., c4:c5], pr[..., c5:], rwkv_w0[l], rwkv_w2[l], rwkv_a0[l],
                             rwkv_a2[l], rwkv_g2[l], rwkv_k_k[l], rwkv_k_a[l], rwkv_r_k[l],
                             rwkv_gn_w[l], rwkv_gn_b[l]).astype(h.dtype)
        h = h + jnp.concatenate([y_ssd, y_rwkv], axis=-1) @ w_out[l]
        v = rms_norm(h, norm2_g[l])
        h = h + (jax.nn.silu(v @ w_gate[l]) * (v @ w_up[l])) @ w_down[l]
    return rms_norm(h, norm_f_g)
```

```python
from contextlib import ExitStack
import numpy as np
import concourse.bass as bass
import concourse.mybir as mybir

EPOCH = 12000


class Buf:
    __slots__ = ("name", "w", "r", "excl")

    def __init__(self, name, excl=False):
        self.name = name
        self.excl = excl
        self.w = None
        self.r = []


class Sched:
    ENG = ("tensor", "vector", "scalar", "gpsimd", "sync")

    def __init__(self, nc, sem_ctx):
        self.nc = nc
        self.sem_ctx = sem_ctx
        self.count = {e: 0 for e in self.ENG}
        self.sems = {e: [] for e in self.ENG}
        self.waited = {e: {} for e in self.ENG}
        self.prog = {e: [] for e in self.ENG}
        self.dma_sems = {}
        self.cc_tags = []
        self.nwaits = 0

    def _sem(self, eng, k):
        idx = (k - 1) // EPOCH
        lst = self.sems[eng]
        while len(lst) <= idx:
            lst.append(self.sem_ctx.enter_context(self.nc.semaphore(f"s_{eng}_{len(lst)}")))
        return lst[idx], (k - 1) % EPOCH + 1, (eng, idx)

    def _wait(self, eng, dep):
        if dep is None:
            return
        if dep[0] == "dma":
            _, sem, val, key = dep
        else:
            sem, val, key = self._sem(dep[0], dep[1])
        w = self.waited[eng]
        if w.get(key, 0) >= val:
            return
        w[key] = val
        self.nwaits += 1
        self.prog[eng].append(lambda e, sem=sem, val=val: e.wait_ge(sem, val))

    def op(self, eng, fn, reads=(), writes=(), serial=False):
        deps = []
        if serial and self.count[eng] > 0:
            self._wait(eng, (eng, self.count[eng]))
        for b in reads:
            if b.w is not None:
                deps.append(b.w)
            if b.excl:
                deps.extend(r for r in b.r if r[0] != eng)
        for b in writes:
            if b.w is not None:
                deps.append(b.w)
            deps.extend(b.r)
        for d in deps:
            if d[0] == eng and d[0] != "dma" and eng == "tensor":
                continue
            self._wait(eng, d)
        self.count[eng] += 1
        k = self.count[eng]
        sem, val, _ = self._sem(eng, k)
        self.prog[eng].append(lambda e, fn=fn, sem=sem: fn(e).then_inc(sem, 1))
        tag = (eng, k)
        for b in reads:
            b.r.append(tag)
        for b in writes:
            b.w = tag
            b.r = []
        return tag

    def dma(self, eng, key, out, in_, reads=(), writes=(), **kw):
        if key not in self.dma_sems:
            self.dma_sems[key] = [self.sem_ctx.enter_context(self.nc.semaphore(f"d_{key}")), 0]
        ent = self.dma_sems[key]
        deps = []
        for b in reads:
            if b.w is not None:
                deps.append(b.w)
        for b in writes:
            if b.w is not None and not (b.w[0] == "dma" and b.w[3] == ("dma", key)):
                deps.append(b.w)
            deps.extend(b.r)
        for d in deps:
            self._wait(eng, d)
        ent[1] += 16
        sem, val = ent[0], ent[1]
        self.prog[eng].append(lambda e, sem=sem, out=out, in_=in_, kw=kw: e.dma_start(out=out, in_=in_, **kw).then_inc(sem, 16))
        tag = ("dma", sem, val, ("dma", key))
        for b in reads:
            b.r.append(tag)
        for b in writes:
            b.w = tag
            b.r = []
        return tag

    def dma_fn(self, eng, key, fn, reads=(), writes=()):
        if key not in self.dma_sems:
            self.dma_sems[key] = [self.sem_ctx.enter_context(self.nc.semaphore(f"d_{key}")), 0]
        ent = self.dma_sems[key]
        deps = []
        for b in reads:
            if b.w is not None:
                deps.append(b.w)
        for b in writes:
            if b.w is not None and not (b.w[0] == "dma" and b.w[3] == ("dma", key)):
                deps.append(b.w)
            deps.extend(b.r)
        for d in deps:
            self._wait(eng, d)
        ent[1] += 16
        sem, val = ent[0], ent[1]
        self.prog[eng].append(lambda e, sem=sem, fn=fn: fn(e).then_inc(sem, 16))
        tag = ("dma", sem, val, ("dma", key))
        for b in reads:
            b.r.append(tag)
        for b in writes:
            b.w = tag
            b.r = []
        return tag

    def collective(self, name, fn, reads=(), writes=()):
        sem = self.sem_ctx.enter_context(self.nc.semaphore(f"cc_{name}"))
        deps = []
        for b in reads:
            if b.w is not None:
                deps.append(b.w)
        for b in writes:
            if b.w is not None:
                deps.append(b.w)
            deps.extend(b.r)
        for d in deps:
            self._wait("gpsimd", d)
        self.prog["gpsimd"].append(lambda e, sem=sem, fn=fn: fn(e).then_inc(sem))
        tag = ("dma", sem, 1, ("cc", name))
        self.cc_tags.append(tag)
        for b in reads:
            b.r.append(tag)
        for b in writes:
            b.w = tag
            b.r = []
        return tag

    def barrier(self):
        for e in self.ENG:
            for o in self.ENG:
                if o != e and self.count[o] > 0:
                    self._wait(e, (o, self.count[o]))
            for key, ent in self.dma_sems.items():
                if ent[1] > 0:
                    self._wait(e, ("dma", ent[0], ent[1], ("dma", key)))
            for tag in self.cc_tags:
                self._wait(e, tag)

    def wait_all(self, eng, bufs):
        for b in bufs:
            if b.w is not None:
                self._wait(eng, b.w)
            for d in b.r:
                self._wait(eng, d)

    def emit(self, block):
        def mk(eng):
            def body(e):
                for f in self.prog[eng]:
                    f(e)
            return body
        block.tensor(mk("tensor"))
        block.vector(mk("vector"))
        block.scalar(mk("scalar"))
        block.gpsimd(mk("gpsimd"))
        block.sync(mk("sync"))


F32 = mybir.dt.float32
BF16 = mybir.dt.bfloat16
AF = mybir.ActivationFunctionType
ALU = mybir.AluOpType
AX = mybir.AxisListType

TT = 256
NCH = TT // 64
NST = TT // 128
C0 = float(np.exp(-0.5))
RMS_EPS = 1e-6
GATED_NORM_EPS = 1e-5
GN_EPS = 64e-5
NEG = -30000.0

N1 = 1288
N2 = 1984
PV_CW = 0
PV_CB = 24
PV_MU = 30
PV_W0 = 46
PV_A0 = 50
PV_KK = 54
PV_KA = 58
PV_RK = 62
PV_G1 = 66
NPV = 82
BC_NG = 0
BC_GNW = 512
BC_GNB = 1024
BC_DTB = 1536
BC_ALOG = 1544
BC_D = 1552
NBC = 1560
CS_ID = 0
CS_TRI2 = 128
CS_TRIL = 256
CS_BONES = 320
CS_NEGM = 448
CS_CH0 = 960
CS_CH1 = 1088
CS_MASKA = 1216
CS_MASKQ = 1344
CS_SCAN = 1408
CS_IDL = 1664
CS_HSEL = 1728
NCS = 1730


def make_consts():
    c = np.zeros((128, NCS), np.float32)
    p = np.arange(128)
    pl = p % 64
    c[:, CS_ID:CS_ID + 128] = np.eye(128)
    c[:, CS_TRI2:CS_TRI2 + 128] = ((p[:, None] // 64 == p[None, :] // 64) & (p[:, None] <= p[None, :]))
    l64 = np.arange(64)
    c[:, CS_TRIL:CS_TRIL + 64] = (pl[:, None] <= l64[None, :])
    c[:, CS_BONES:CS_BONES + 128] = (p[:, None] // 64 == p[None, :] // 64)
    nm = np.where(l64[None, :] < pl[:, None], NEG, 0.0)
    c[:, CS_NEGM:CS_NEGM + 512] = np.tile(nm, (1, 8))
    c[:, CS_CH0:CS_CH0 + 128] = (p[:, None] < 64)
    c[:, CS_CH1:CS_CH1 + 128] = (p[:, None] >= 64)
    c[:, CS_MASKA:CS_MASKA + 64] = (l64[None, :] > pl[:, None])
    c[:, CS_MASKA + 64:CS_MASKA + 128] = (l64[None, :] >= pl[:, None])
    c[:, CS_MASKQ:CS_MASKQ + 64] = (pl[:, None] > l64[None, :])
    sm = np.ones(256); sm[::64] = 0
    c[:, CS_SCAN:CS_SCAN + 256] = sm[None, :]
    c[:, CS_IDL:CS_IDL + 64] = (pl[:, None] == l64[None, :])
    c[:, CS_HSEL] = (p < 64)
    c[:, CS_HSEL + 1] = (p >= 64)
    return c


class T:
    def __init__(self, t, name, excl=False):
        self.t = t
        self.b = Buf(name, excl)


class _Stop(Exception):
    pass


def build_p1(nc, NT, dram, do_ssd=True, do_rwkv=True, dbg=99, S=None, es=None, fused=None):
    NTILES = NT // TT
    if S is None:
        es = ExitStack()
        S = Sched(nc, es)

    def sb(name, shape, dt):
        return T(es.enter_context(nc.sbuf_tensor("s_" + name, shape, dt)), name)

    def ps(name):
        return T(es.enter_context(nc.psum_tensor(name, [128, 512], F32)), name, True)

    def V(fn, r, w): return S.op("vector", fn, [a.b for a in r], [a.b for a in w])
    def A(fn, r, w): return S.op("scalar", fn, [a.b for a in r], [a.b for a in w])
    def G(fn, r, w): return S.op("gpsimd", fn, [a.b for a in r], [a.b for a in w])
    def PE(fn, r, w, serial=False): return S.op("tensor", fn, [a.b for a in r], [a.b for a in w], serial=serial)

    W = sb("W", [128, 16, N2], BF16)
    pv = sb("pv", [128, NPV], F32)
    bc = sb("bc", [128, NBC], F32)
    cst = sb("cst", [128, NCS], F32)
    cb = sb("cb", [128, 1216], BF16)
    omk = sb("omk", [128, 4], F32)
    aneg = sb("aneg", [128, 8], F32)
    DI = sb("DI", [128, 512], F32)
    lw2b = sb("lw2b", [96, 512], BF16)
    la2b = sb("la2b", [96, 512], BF16)
    lg2b = sb("lg2b", [128, 2, 512], BF16)
    xt = [sb(f"xt{i}", [128, 2048], F32) for i in range(2)]
    xn = sb("xn", [128, 2048], BF16)
    uT = sb("uT", [128, 16, TT], BF16)
    sm = sb("sm", [128, 64], F32)
    ost = [sb(f"ost{i}", [128, 512], BF16 if fused else F32) for i in range(2)]

    banks = [ps(f"bank{i}") for i in range(8)]
    tpB = banks[0]

    ident_f = cst.t[:, CS_ID:CS_ID + 128]
    ident_b = cb.t[:, 0:128]

    S.dma("sync", "ld_c0", pv.t[:], dram["pv"][:, :], writes=[pv.b])
    S.dma("sync", "ld_c1", bc.t[:], dram["bc"][:, :], writes=[bc.b])
    S.dma("sync", "ld_c2", cst.t[:], dram["cst"][:, :], writes=[cst.b])
    S.dma("gpsimd", "ld_l0", lw2b.t[:], dram["lw2"][:, :], writes=[lw2b.b])
    S.dma("gpsimd", "ld_l1", la2b.t[:], dram["la2"][:, :], writes=[la2b.b])
    S.dma("gpsimd", "ld_l2", lg2b.t[:], dram["lg2"].rearrange("(c p) n -> p c n", p=128), writes=[lg2b.b])
    V(lambda e: e.tensor_copy(out=cb.t[:, 0:128], in_=cst.t[:, CS_ID:CS_ID + 128]), [cst], [cb])
    V(lambda e: e.tensor_copy(out=cb.t[:, 128:130], in_=cst.t[:, CS_HSEL:CS_HSEL + 2]), [cst], [cb])
    hsel_b = cb.t[:, 128:130]
    V(lambda e: e.tensor_scalar(out=omk.t[:], in0=pv.t[:, PV_KA:PV_KA + 4], scalar1=-1.0, scalar2=1.0, op0=ALU.mult, op1=ALU.add), [pv], [omk])
    A(lambda e: e.activation(out=aneg.t[:], in_=bc.t[:, BC_ALOG:BC_ALOG + 8], func=AF.Exp), [bc], [aneg])
    V(lambda e: e.tensor_scalar(out=aneg.t[:], in0=aneg.t[:], scalar1=-1.0, scalar2=None, op0=ALU.mult), [aneg], [aneg])
    V(lambda e: e.tensor_tensor(out=DI.t[:].rearrange("p (e l) -> p e l", l=64),
                                in0=cst.t[:, CS_IDL:CS_IDL + 64].unsqueeze(1).to_broadcast([128, 8, 64]),
                                in1=bc.t[:, BC_D:BC_D + 8].unsqueeze(2).to_broadcast([128, 8, 64]), op=ALU.mult), [cst, bc], [DI])

    def load_weights(wdram, ncols):
        tmp = ExitStack()
        wstg = [T(tmp.enter_context(nc.sbuf_tensor(f"s_wstg{i}_{ncols}", [128, ncols], F32)), f"wstg{i}") for i in range(3)]
        wv = wdram.rearrange("(c p) n -> p c n", p=128)
        for kc in range(16):
            stg = wstg[kc % 3]
            S.dma("sync", f"ld_w{kc % 3}", stg.t[:, 0:ncols], wv[:, kc, :], writes=[stg.b])
            gcol = pv.t[:, PV_G1 + kc:PV_G1 + kc + 1]
            m = kc % 3
            if m == 0:
                V(lambda e, kc=kc, stg=stg, gcol=gcol: e.tensor_scalar(out=W.t[:, kc, 0:ncols], in0=stg.t[:, 0:ncols], scalar1=gcol, scalar2=None, op0=ALU.mult), [stg, pv], [W])
            elif m == 1:
                A(lambda e, kc=kc, stg=stg, gcol=gcol: e.activation(out=W.t[:, kc, 0:ncols], in_=stg.t[:, 0:ncols], func=AF.Copy, scale=gcol), [stg, pv], [W])
            else:
                G(lambda e, kc=kc, stg=stg, gcol=gcol: e.tensor_scalar(out=W.t[:, kc, 0:ncols], in0=stg.t[:, 0:ncols], scalar1=gcol, scalar2=None, op0=ALU.mult), [stg, pv], [W])
        S.barrier()
        tmp.close()

    xcount = [0]

    def load_norm_transpose(tile):
        for st in range(NST):
            slot = xcount[0] % 2
            xcount[0] += 1
            X = xt[slot]
            tok0 = tile * TT + st * 128
            S.dma("sync", f"ldx{slot}", X.t[:], dram["x"][tok0:tok0 + 128, :], writes=[X.b])
            A(lambda e, X=X: e.activation(out=xn.t[:], in_=X.t[:], func=AF.Square, accum_out=sm.t[:, 0:1]), [X], [xn, sm])
            V(lambda e: e.tensor_scalar(out=sm.t[:, 1:2], in0=sm.t[:, 0:1], scalar1=1.0 / 2048, scalar2=RMS_EPS, op0=ALU.mult, op1=ALU.add), [sm], [sm])
            A(lambda e: e.activation(out=sm.t[:, 2:3], in_=sm.t[:, 1:2], func=AF.Sqrt), [sm], [sm])
            V(lambda e: e.reciprocal(out=sm.t[:, 3:4], in_=sm.t[:, 2:3]), [sm], [sm])
            A(lambda e, X=X: e.activation(out=xn.t[:], in_=X.t[:], func=AF.Copy, scale=sm.t[:, 3:4]), [X, sm], [xn])
            tpv = tpB.t[:].bitcast(BF16).rearrange("p (a b) -> p a b", b=128)
            for half in range(2):
                for k8 in range(8):
                    kc = half * 8 + k8
                    PE(lambda e, kc=kc, k8=k8: e.transpose(tpv[:, k8, :], xn.t[:, kc * 128:(kc + 1) * 128], ident_b), [xn, cb], [tpB])
                eng = V if half == 0 else A
                if half == 0:
                    V(lambda e, half=half, st=st: e.tensor_copy(out=uT.t[:, half * 8:(half + 1) * 8, st * 128:(st + 1) * 128], in_=tpv), [tpB], [uT])
                else:
                    A(lambda e, half=half, st=st: e.activation(out=uT.t[:, half * 8:(half + 1) * 8, st * 128:(st + 1) * 128], in_=tpv, func=AF.Copy), [tpB], [uT])

    def proj_fm(col0, M, out_ps):
        for kc in range(16):
            PE(lambda e, kc=kc: e.matmul(out_ps.t[0:M, 0:TT], lhsT=W.t[:, kc, col0:col0 + M], rhs=uT.t[:, kc, :],
                                         start=(kc == 0), stop=(kc == 15)), [W, uT], [out_ps])

    scount = [0]

    slab_tags = []

    def store(stage, tok0, col0):
        if not fused:
            S.dma("sync", f"st{scount[0] % 2}", dram["ymix"][tok0:tok0 + 128, col0:col0 + 512], stage.t[:], reads=[stage.b], writes=[])
            return
        Y = fused["YS"] if col0 == 0 else fused["YR"]
        Gt = fused["GS"] if col0 == 0 else fused["GR"]
        gb = fused["gsB"] if col0 == 0 else fused["grB"]
        tag = S.dma("sync", f"st{scount[0] % 2}", Y[tok0:tok0 + 128, :], stage.t[:], reads=[stage.b], writes=[])
        slab_tags.append(tag)
        if (tok0 + 128) % 1024 == 0:
            k = tok0 // 1024
            for tg in slab_tags:
                S._wait("gpsimd", tg)
            del slab_tags[:]
            S.collective(f"{'s' if col0 == 0 else 'r'}{k}",
                         lambda e, k=k, Y=Y, Gt=Gt: e.collective_compute("AllGather", ALU.bypass, replica_groups=fused["groups"],
                                                                        ins=[Y[k * 1024:(k + 1) * 1024, :].opt()],
                                                                        outs=[Gt[k * 4096:(k + 1) * 4096, :].opt()]),
                         reads=[], writes=[gb[k]])

    if do_ssd:
        load_weights(dram["w1"], N1)
        es1 = ExitStack()

        def sb1(name, shape, dt):
            return T(es1.enter_context(nc.sbuf_tensor("s_" + name, shape, dt)), name)

        projB, ARb, ydB, yoB, hnB, miscB, dtB = banks[1], banks[2], banks[3], banks[4], banks[5], banks[6], banks[7]
        Pb = sb1("Pb", [128, TT + 3], F32)
        hist = sb1("hist", [128, 6, 3], F32)
        acc = sb1("acc", [128, TT], F32)
        xsT = sb1("xsT", [128, 4, TT], BF16)
        sz2 = [sb1(f"sz{i}", [128, NST, 512], F32) for i in range(2)]
        BT2 = [sb1(f"BT{i}", [128, TT], BF16) for i in range(2)]
        CT2 = [sb1(f"CT{i}", [128, TT], BF16) for i in range(2)]
        Xtm2 = [sb1(f"Xtm{i}", [128, NST, 512], BF16) for i in range(2)]
        Btm2 = [sb1(f"Btm{i}", [128, NST, 128], BF16) for i in range(2)]
        dtr2 = [sb1(f"dtr{i}", [128, NST, 8], F32) for i in range(2)]
        dts = sb1("dts", [128, 64], F32)
        Dm = sb1("Dm", [128, 512], F32)
        LT = sb1("LT", [128, 512], F32)
        MTb = sb1("MTb", [128, 512], BF16)
        Xd = sb1("Xd", [128, 512], BF16)
        t1 = sb1("t1", [128, 512], F32)
        ys = sb1("ys", [128, 512], F32)
        hf = sb1("hf", [128, 512], F32)
        hb = sb1("hb", [128, 512], BF16)
        cdb = sb1("cdb", [128, 2, 8], F32)
        sm1 = sb1("sm1", [128, 8], F32)

        G(lambda e: e.memset(hist.t[:], 0.0), [], [hist])
        G(lambda e: e.memset(hf.t[:], 0.0), [], [hf])
        G(lambda e: e.memset(hb.t[:], 0.0), [], [hb])

        def v8(ap):
            return ap.unsqueeze(2).to_broadcast([128, 8, 64])

        def r3(ap):
            return ap.rearrange("p (e l) -> p e l", l=64)

        DT, ADT, ACS, EA, SD, TMP, NACS = 0, 8, 16, 24, 32, 40, 48

        def prep1_gen(tile):
            bs = tile % 2
            sz, BT, CT, Xtm, Btm, dtr = sz2[bs], BT2[bs], CT2[bs], Xtm2[bs], Btm2[bs], dtr2[bs]
            load_norm_transpose(tile)
            yield
            for st in range(NST):
                for kc in range(16):
                    PE(lambda e, kc=kc, st=st: e.matmul(projB.t[:, 0:512], lhsT=uT.t[:, kc, st * 128:(st + 1) * 128], rhs=W.t[:, kc, 0:512],
                                                        start=(kc == 0), stop=(kc == 15)), [W, uT], [projB])
                A(lambda e, st=st: e.activation(out=sz.t[:, st, :], in_=projB.t[:, 0:512], func=AF.Silu), [projB], [sz])
                for kc in range(16):
                    PE(lambda e, kc=kc, st=st: e.matmul(dtB.t[:, st * 8:(st + 1) * 8], lhsT=uT.t[:, kc, st * 128:(st + 1) * 128], rhs=W.t[:, kc, 512:520],
                                                        start=(kc == 0), stop=(kc == 15)), [W, uT], [dtB])
                yield
            V(lambda e: e.tensor_copy(out=dtr.t[:].rearrange("p a b -> p (a b)"), in_=dtB.t[:, 0:NST * 8]), [dtB], [dtr])
            for blk in range(6):
                proj_fm(520 + blk * 128, 128, projB)
                G(lambda e, blk=blk: e.tensor_copy(out=Pb.t[:, 0:3], in_=hist.t[:, blk, :]), [hist], [Pb])
                A(lambda e: e.activation(out=Pb.t[:, 3:3 + TT], in_=projB.t[:, 0:TT], func=AF.Copy), [projB], [Pb])
                A(lambda e, blk=blk: e.activation(out=acc.t[:], in_=projB.t[:, 0:TT], func=AF.Identity,
                                                  scale=pv.t[:, PV_CW + blk * 4 + 3:PV_CW + blk * 4 + 4],
                                                  bias=pv.t[:, PV_CB + blk:PV_CB + blk + 1]), [projB, pv], [acc])
                for j in (2, 1, 0):
                    V(lambda e, blk=blk, j=j: e.scalar_tensor_tensor(out=acc.t[:], in0=Pb.t[:, j:j + TT],
                                                                     scalar=pv.t[:, PV_CW + blk * 4 + j:PV_CW + blk * 4 + j + 1],
                                                                     in1=acc.t[:], op0=ALU.mult, op1=ALU.add), [Pb, pv, acc], [acc])
                G(lambda e, blk=blk: e.tensor_copy(out=hist.t[:, blk, :], in_=Pb.t[:, TT:TT + 3]), [Pb], [hist])
                if blk < 4:
                    A(lambda e, blk=blk: e.activation(out=xsT.t[:, blk, :], in_=acc.t[:], func=AF.Silu), [acc], [xsT])
                elif blk == 4:
                    A(lambda e: e.activation(out=BT.t[:], in_=acc.t[:], func=AF.Silu), [acc], [BT])
                else:
                    A(lambda e: e.activation(out=CT.t[:], in_=acc.t[:], func=AF.Silu), [acc], [CT])
                yield
            tpv1 = tpB.t[:].bitcast(BF16)
            for st in range(NST):
                tsl = slice(st * 128, (st + 1) * 128)
                for blk in range(4):
                    PE(lambda e, blk=blk, tsl=tsl: e.transpose(tpv1[:, blk * 128:(blk + 1) * 128], xsT.t[:, blk, tsl], ident_b), [xsT, cb], [tpB])
                PE(lambda e, tsl=tsl: e.transpose(tpv1[:, 512:640], BT.t[:, tsl], ident_b), [BT, cb], [tpB])
                A(lambda e, st=st: e.activation(out=Xtm.t[:, st, :], in_=tpv1[:, 0:512], func=AF.Copy), [tpB], [Xtm])
                A(lambda e, st=st: e.activation(out=Btm.t[:, st, :], in_=tpv1[:, 512:640], func=AF.Copy), [tpB], [Btm])
                yield

        def core1_gen(tile):
            bs = tile % 2
            sz, BT, CT, Xtm, Btm, dtr = sz2[bs], BT2[bs], CT2[bs], Xtm2[bs], Btm2[bs], dtr2[bs]
            for st in range(NST):
                V(lambda e, st=st: e.tensor_tensor(out=dts.t[:, TMP:TMP + 8], in0=dtr.t[:, st, :], in1=bc.t[:, BC_DTB:BC_DTB + 8], op=ALU.add), [dtr, bc], [dts])
                A(lambda e: e.activation(out=dts.t[:, DT:DT + 8], in_=dts.t[:, TMP:TMP + 8], func=AF.Abs), [dts], [dts])
                A(lambda e: e.activation(out=dts.t[:, DT:DT + 8], in_=dts.t[:, DT:DT + 8], func=AF.Exp, scale=-1.0), [dts], [dts])
                A(lambda e: e.activation(out=dts.t[:, DT:DT + 8], in_=dts.t[:, DT:DT + 8], func=AF.Ln, bias=1.0), [dts], [dts])
                V(lambda e: e.scalar_tensor_tensor(out=dts.t[:, DT:DT + 8], in0=dts.t[:, TMP:TMP + 8], scalar=0.0, in1=dts.t[:, DT:DT + 8],
                                                   op0=ALU.max, op1=ALU.add), [dts], [dts])
                V(lambda e: e.tensor_tensor(out=dts.t[:, ADT:ADT + 8], in0=dts.t[:, DT:DT + 8], in1=aneg.t[:], op=ALU.mult), [dts, aneg], [dts])
                PE(lambda e: e.matmul(miscB.t[:, 8:16], lhsT=cst.t[:, CS_TRI2:CS_TRI2 + 128], rhs=dts.t[:, ADT:ADT + 8], start=True, stop=True), [cst, dts], [miscB])
                PE(lambda e: e.matmul(miscB.t[:, 16:24], lhsT=cst.t[:, CS_BONES:CS_BONES + 128], rhs=dts.t[:, ADT:ADT + 8], start=True, stop=True), [cst, dts], [miscB])
                PE(lambda e: e.matmul(miscB.t[:, 24:32], lhsT=cst.t[:, CS_CH0:CS_CH0 + 128], rhs=dts.t[:, ADT:ADT + 8], start=True, stop=True), [cst, dts], [miscB])
                PE(lambda e: e.matmul(miscB.t[:, 32:40], lhsT=cst.t[:, CS_CH1:CS_CH1 + 128], rhs=dts.t[:, ADT:ADT + 8], start=True, stop=True), [cst, dts], [miscB])
                for c in range(2):
                    csl = slice(st * 128 + c * 64, st * 128 + (c + 1) * 64)
                    PE(lambda e, c=c, csl=csl: e.matmul(miscB.t[c * 64:(c + 1) * 64, 64:128], lhsT=BT.t[:, csl], rhs=CT.t[:, csl], start=True, stop=True), [BT, CT], [miscB])
                A(lambda e: e.activation(out=dts.t[:, ACS:ACS + 8], in_=miscB.t[:, 8:16], func=AF.Copy), [miscB], [dts])
                A(lambda e: e.activation(out=dts.t[:, EA:EA + 8], in_=miscB.t[:, 8:16], func=AF.Exp), [miscB], [dts])
                A(lambda e: e.activation(out=cdb.t[:].rearrange("p a b -> p (a b)"), in_=miscB.t[:, 24:40], func=AF.Exp), [miscB], [cdb])
                V(lambda e: e.tensor_tensor(out=dts.t[:, SD:SD + 8], in0=miscB.t[:, 16:24], in1=dts.t[:, ACS:ACS + 8], op=ALU.subtract), [miscB, dts], [dts])
                A(lambda e: e.activation(out=dts.t[:, SD:SD + 8], in_=dts.t[:, SD:SD + 8], func=AF.Exp), [dts], [dts])
                V(lambda e: e.tensor_tensor(out=dts.t[:, SD:SD + 8], in0=dts.t[:, SD:SD + 8], in1=dts.t[:, DT:DT + 8], op=ALU.mult), [dts], [dts])
                V(lambda e: e.tensor_tensor(out=r3(Dm.t[:]), in0=v8(dts.t[:, ADT:ADT + 8]),
                                            in1=cst.t[:, CS_TRIL:CS_TRIL + 64].unsqueeze(1).to_broadcast([128, 8, 64]), op=ALU.mult), [dts, cst], [Dm])
                yield
                PE(lambda e: e.matmul(ARb.t[:, :], lhsT=cst.t[:, CS_BONES:CS_BONES + 128], rhs=Dm.t[:], start=True, stop=False), [cst, Dm], [ARb])
                PE(lambda e: e.matmul(ARb.t[:, :], lhsT=ident_f, rhs=cst.t[:, CS_NEGM:CS_NEGM + 512], start=False, stop=True), [cst], [ARb])
                V(lambda e: e.tensor_tensor(out=r3(LT.t[:]), in0=r3(ARb.t[:, :]), in1=v8(dts.t[:, ACS:ACS + 8]), op=ALU.subtract), [ARb, dts], [LT])
                A(lambda e: e.activation(out=LT.t[:], in_=LT.t[:], func=AF.Exp), [LT], [LT])
                yield
                V(lambda e: e.tensor_tensor(out=r3(LT.t[:]), in0=r3(LT.t[:]), in1=miscB.t[:, 64:128].unsqueeze(1).to_broadcast([128, 8, 64]), op=ALU.mult), [LT, miscB], [LT])
                V(lambda e: e.tensor_tensor(out=r3(LT.t[:]), in0=r3(LT.t[:]), in1=v8(dts.t[:, DT:DT + 8]), op=ALU.mult), [LT, dts], [LT])
                V(lambda e: e.tensor_tensor(out=MTb.t[:], in0=LT.t[:], in1=DI.t[:], op=ALU.add), [LT, DI], [MTb])
                G(lambda e, st=st: e.tensor_tensor(out=r3(Xd.t[:]), in0=r3(Xtm.t[:, st, :]), in1=v8(dts.t[:, SD:SD + 8]), op=ALU.mult), [Xtm, dts], [Xd])
                yield
                for c in range(2):
                    for h in range(8):
                        PE(lambda e, c=c, h=h, st=st: e.matmul(ydB.t[c * 64:(c + 1) * 64, h * 64:(h + 1) * 64],
                                                               lhsT=MTb.t[c * 64:(c + 1) * 64, h * 64:(h + 1) * 64],
                                                               rhs=Xtm.t[c * 64:(c + 1) * 64, st, h * 64:(h + 1) * 64], start=True, stop=True), [MTb, Xtm], [ydB], serial=(c == 1 and h == 0))
                for c in range(2):
                    csl = slice(st * 128 + c * 64, st * 128 + (c + 1) * 64)
                    PE(lambda e, c=c, csl=csl: e.matmul(yoB.t[c * 64:(c + 1) * 64, :], lhsT=CT.t[:, csl], rhs=hb.t[:], start=True, stop=True), [CT, hb], [yoB])
                    PE(lambda e, c=c, st=st: e.matmul(hnB.t[:, :], lhsT=Btm.t[c * 64:(c + 1) * 64, st, :], rhs=Xd.t[c * 64:(c + 1) * 64, :], start=True, stop=True), [Btm, Xd], [hnB])
                    V(lambda e, c=c: e.tensor_tensor(out=r3(hf.t[:]), in0=r3(hf.t[:]), in1=v8(cdb.t[:, c, :]), op=ALU.mult), [hf, cdb], [hf])
                    V(lambda e: e.tensor_tensor(out=hf.t[:], in0=hf.t[:], in1=hnB.t[:, :], op=ALU.add), [hf, hnB], [hf])
                    A(lambda e: e.activation(out=hb.t[:], in_=hf.t[:], func=AF.Copy), [hf], [hb])
                    yield
                V(lambda e: e.tensor_tensor(out=r3(t1.t[:]), in0=r3(yoB.t[:, :]), in1=v8(dts.t[:, EA:EA + 8]), op=ALU.mult), [yoB, dts], [t1])
                V(lambda e: e.tensor_tensor(out=ys.t[:], in0=ydB.t[:, :], in1=t1.t[:], op=ALU.add), [ydB, t1], [ys])
                G(lambda e, st=st: e.tensor_tensor(out=ys.t[:], in0=ys.t[:], in1=sz.t[:, st, :], op=ALU.mult), [ys, sz], [ys])
                A(lambda e: e.activation(out=t1.t[:], in_=ys.t[:], func=AF.Square, accum_out=sm1.t[:, 0:1]), [ys], [t1, sm1])
                V(lambda e: e.tensor_scalar(out=sm1.t[:, 1:2], in0=sm1.t[:, 0:1], scalar1=1.0 / 512, scalar2=GATED_NORM_EPS, op0=ALU.mult, op1=ALU.add), [sm1], [sm1])
                A(lambda e: e.activation(out=sm1.t[:, 2:3], in_=sm1.t[:, 1:2], func=AF.Sqrt), [sm1], [sm1])
                V(lambda e: e.reciprocal(out=sm1.t[:, 3:4], in_=sm1.t[:, 2:3]), [sm1], [sm1])
                stage = ost[scount[0] % 2]
                V(lambda e, stage=stage: e.scalar_tensor_tensor(out=stage.t[:], in0=ys.t[:], scalar=sm1.t[:, 3:4], in1=bc.t[:, BC_NG:BC_NG + 512],
                                                                op0=ALU.mult, op1=ALU.mult), [ys, sm1, bc], [stage])
                store(stage, tile * TT + st * 128, 0)
                scount[0] += 1
                yield

        for it in range(NTILES + 1):
            gp = prep1_gen(it) if it < NTILES else None
            gc = core1_gen(it - 1) if it > 0 else None
            while gp is not None or gc is not None:
                if gc is not None:
                    try:
                        next(gc)
                    except StopIteration:
                        gc = None
                if gp is not None:
                    try:
                        next(gp)
                    except StopIteration:
                        gp = None
        S.barrier()
        es1.close()

    if do_rwkv:
        projB, tp2B, AmB0, AmB1, paB, qgB, yB = banks[1], banks[2], banks[3], banks[4], banks[5], banks[6], banks[7]
        load_weights(dram["w2"], N2)
        Pr = sb("Pr", [128, TT + 1], F32)
        carry = sb("carry", [128, 16], F32)
        dd = sb("dd", [128, TT], F32)
        sh = sb("sh", [128, TT], F32)
        twd = sb("twd", [96, TT], BF16)
        tad = sb("tad", [96, TT], BF16)
        rr = sb("rr", [128, TT], F32)
        kr = sb("kr", [128, TT], F32)
        sg = sb("sg", [128, TT], F32)
        cum = sb("cum", [128, TT], F32)
        Wc = sb("Wc", [128, TT], F32)
        iW = sb("iW", [128, TT], F32)
        Wex = sb("Wex", [128, TT], F32)
        alpha = sb("alpha", [128, TT], F32)
        kkr = sb("kkr", [128, TT], F32)
        sq = sb("sq", [128, TT], F32)
        k2 = sb("k2", [128, TT], F32)
        tmpa = sb("tmpa", [128, TT], F32)
        rkk = sb("rkk", [128, TT], BF16)
        vT = sb("vT", [128, 4, TT], BF16)
        sgT2 = [sb(f"sgT{i}", [128, 2, TT], BF16) for i in range(2)]
        AR2 = [sb(f"AR{i}", [128, 4, NCH, 2, 64], BF16) for i in range(2)]
        BK2 = [sb(f"BK{i}", [128, 4, NCH, 2, 64], BF16) for i in range(2)]
        vTz2 = [sb(f"vTz{i}", [128, 4, NCH, 2, 64], BF16) for i in range(2)]
        Vtm2 = [sb(f"Vtm{i}", [128, NST, 512], BF16) for i in range(2)]
        Wl2 = [sb(f"Wl{i}", [128, 4, NCH], F32) for i in range(2)]
        rks2 = [sb(f"rks{i}", [128, NST, 8], F32) for i in range(2)]
        A_sb = sb("A_sb", [128, 8, 128], BF16)
        Pp = [sb(f"Pp{i}", [64, 8, 64], BF16) for i in range(2)]
        Qp = [sb(f"Qp{i}", [64, 8, 64], BF16) for i in range(2)]
        Gp = [sb(f"Gp{i}", [64, 8, 64], BF16) for i in range(2)]
        BKtok2 = [sb(f"BKtok{i}", [128, 512], BF16) for i in range(2)]
        UV2 = [sb(f"UV{i}", [128, 512], BF16) for i in range(2)]
        Xs = sb("Xs", [64, 512], BF16)
        Sf = sb("Sf", [128, 256], F32)
        Sb_e = sb("Sb_e", [128, 256], BF16)
        Sb_o = sb("Sb_o", [128, 256], BF16)
        ysq = sb("ysq", [128, 512], F32)
        yc = sb("yc", [128, 512], F32)
        bon = ysq
        gst = sb("gst", [128, 64], F32)

        G(lambda e: e.memset(carry.t[:], 0.0), [], [carry])
        for i_ in range(2):
            G(lambda e, i_=i_: e.memset(vTz2[i_].t[:], 0.0), [], [vTz2[i_]])
        G(lambda e: e.memset(Sf.t[:], 0.0), [], [Sf])
        G(lambda e: e.memset(Sb_e.t[:], 0.0), [], [Sb_e])
        G(lambda e: e.memset(Sb_o.t[:], 0.0), [], [Sb_o])

        maskA3 = cst.t[:, CS_MASKA:CS_MASKA + 128].unsqueeze(1).to_broadcast([128, 4, 128])
        maskQ3 = cst.t[0:64, CS_MASKQ:CS_MASKQ + 64].unsqueeze(1).to_broadcast([64, 8, 64])
        identl3 = cst.t[0:64, CS_IDL:CS_IDL + 64].unsqueeze(1).to_broadcast([64, 8, 64])

        def shift_block(bi, col0, M, dest_fn):
            proj_fm(col0, M, projB)
            G(lambda e: e.tensor_copy(out=Pr.t[0:M, 0:1], in_=carry.t[0:M, bi:bi + 1]), [carry], [Pr])
            A(lambda e: e.activation(out=Pr.t[0:M, 1:TT + 1], in_=projB.t[0:M, 0:TT], func=AF.Copy), [projB], [Pr])
            V(lambda e: e.tensor_tensor(out=dd.t[0:M, :], in0=Pr.t[0:M, 0:TT], in1=Pr.t[0:M, 1:TT + 1], op=ALU.subtract), [Pr], [dd])
            G(lambda e: e.tensor_copy(out=carry.t[0:M, bi:bi + 1], in_=Pr.t[0:M, TT:TT + 1]), [Pr], [carry])
            dest_fn()

        def c4(ap):
            return ap.rearrange("p (c l) -> p c l", l=64)

        def prep_gen(tile):
            bs = tile % 2
            sgT, AR, BK, vTz, Vtm, Wl, rks = sgT2[bs], AR2[bs], BK2[bs], vTz2[bs], Vtm2[bs], Wl2[bs], rks2[bs]
            load_norm_transpose(tile)
            yield

            def d_wd():
                V(lambda e: e.scalar_tensor_tensor(out=sh.t[0:96, :], in0=dd.t[0:96, :], scalar=pv.t[0:96, PV_MU + 0:PV_MU + 1], in1=Pr.t[0:96, 1:TT + 1],
                                                   op0=ALU.mult, op1=ALU.add), [dd, pv, Pr], [sh])
                A(lambda e: e.activation(out=twd.t[:], in_=sh.t[0:96, :], func=AF.Tanh), [sh], [twd])
            shift_block(0, 0, 96, d_wd)
            yield

            def d_ad():
                V(lambda e: e.scalar_tensor_tensor(out=tad.t[:], in0=dd.t[0:96, :], scalar=pv.t[0:96, PV_MU + 1:PV_MU + 2], in1=Pr.t[0:96, 1:TT + 1],
                                                   op0=ALU.mult, op1=ALU.add), [dd, pv, Pr], [tad])
            shift_block(1, 96, 96, d_ad)
            yield
            for gi in range(2):
                def d_gd(gi=gi):
                    V(lambda e: e.scalar_tensor_tensor(out=sh.t[:], in0=dd.t[:], scalar=pv.t[:, PV_MU + 2 + gi:PV_MU + 3 + gi], in1=Pr.t[:, 1:TT + 1],
                                                       op0=ALU.mult, op1=ALU.add), [dd, pv, Pr], [sh])
                    A(lambda e: e.activation(out=sgT.t[:, gi, :], in_=sh.t[:], func=AF.Sigmoid), [sh], [sgT])
                shift_block(2 + gi, 192 + gi * 128, 128, d_gd)
                yield
            tpv2 = tpB.t[:].bitcast(BF16).rearrange("p (s a b) -> p s a b", s=NST, a=4)
            for j in range(4):
                cbase = 448 + j * 384
                mu0 = PV_MU + 4 + j * 3

                def d_r(j=j, mu0=mu0):
                    V(lambda e: e.scalar_tensor_tensor(out=rr.t[:], in0=dd.t[:], scalar=pv.t[:, mu0:mu0 + 1], in1=Pr.t[:, 1:TT + 1],
                                                       op0=ALU.mult, op1=ALU.add), [dd, pv, Pr], [rr])
                shift_block(4 + j * 3, cbase, 128, d_r)
                yield

                def d_k(j=j, mu0=mu0):
                    V(lambda e: e.scalar_tensor_tensor(out=kr.t[:], in0=dd.t[:], scalar=pv.t[:, mu0 + 1:mu0 + 2], in1=Pr.t[:, 1:TT + 1],
                                                       op0=ALU.mult, op1=ALU.add), [dd, pv, Pr], [kr])
                shift_block(5 + j * 3, cbase + 128, 128, d_k)
                yield

                def d_v(j=j, mu0=mu0):
                    V(lambda e: e.scalar_tensor_tensor(out=vT.t[:, j, :], in0=dd.t[:], scalar=pv.t[:, mu0 + 2:mu0 + 3], in1=Pr.t[:, 1:TT + 1],
                                                       op0=ALU.mult, op1=ALU.add), [dd, pv, Pr], [vT])
                    G(lambda e: e.tensor_copy(out=vTz.t[:, j, :, 1, :], in_=vT.t[:, j, :].rearrange("p (c l) -> p c l", l=64)), [vT], [vTz])
                shift_block(6 + j * 3, cbase + 256, 128, d_v)
                yield
                PE(lambda e, j=j: e.matmul(projB.t[:, 0:TT], lhsT=lw2b.t[:, j * 128:(j + 1) * 128], rhs=twd.t[:], start=True, stop=True), [lw2b, twd], [projB])
                A(lambda e, j=j: e.activation(out=sg.t[:], in_=projB.t[:, 0:TT], func=AF.Sigmoid, bias=pv.t[:, PV_W0 + j:PV_W0 + j + 1]), [projB, pv], [sg])
                V(lambda e: e.tensor_tensor_scan(out=cum.t[:], data0=cst.t[:, CS_SCAN:CS_SCAN + TT], data1=sg.t[:], initial=0.0, op0=ALU.mult, op1=ALU.subtract), [cst, sg], [cum])
                A(lambda e: e.activation(out=Wc.t[:], in_=cum.t[:], func=AF.Exp, scale=C0), [cum], [Wc])
                A(lambda e: e.activation(out=iW.t[:], in_=cum.t[:], func=AF.Exp, scale=-C0), [cum], [iW])
                V(lambda e: e.tensor_tensor(out=tmpa.t[:], in0=cum.t[:], in1=sg.t[:], op=ALU.add), [cum, sg], [tmpa])
                A(lambda e: e.activation(out=Wex.t[:], in_=tmpa.t[:], func=AF.Exp, scale=C0), [tmpa], [Wex])
                G(lambda e, j=j: e.tensor_copy(out=Wl.t[:, j, :], in_=c4(Wc.t[:])[:, :, 63]), [Wc], [Wl])
                yield
                PE(lambda e, j=j: e.matmul(projB.t[:, 0:TT], lhsT=la2b.t[:, j * 128:(j + 1) * 128], rhs=tad.t[:], start=True, stop=True), [la2b, tad], [projB])
                A(lambda e, j=j: e.activation(out=alpha.t[:], in_=projB.t[:, 0:TT], func=AF.Sigmoid, bias=pv.t[:, PV_A0 + j:PV_A0 + j + 1]), [projB, pv], [alpha])
                V(lambda e, j=j: e.tensor_scalar(out=kkr.t[:], in0=kr.t[:], scalar1=pv.t[:, PV_KK + j:PV_KK + j + 1], scalar2=None, op0=ALU.mult), [kr, pv], [kkr])
                A(lambda e: e.activation(out=sq.t[:], in_=kkr.t[:], func=AF.Square), [kkr], [sq])
                PE(lambda e: e.matmul(projB.t[:, 0:TT], lhsT=cst.t[:, CS_BONES:CS_BONES + 128], rhs=sq.t[:], start=True, stop=True), [cst, sq], [projB])
                A(lambda e: e.activation(out=sq.t[:], in_=projB.t[:, 0:TT], func=AF.Sqrt), [projB], [sq])
                V(lambda e: e.tensor_scalar(out=sq.t[:], in0=sq.t[:], scalar1=1e-12, scalar2=None, op0=ALU.max), [sq], [sq])
                V(lambda e: e.reciprocal(out=sq.t[:], in_=sq.t[:]), [sq], [sq])
                V(lambda e: e.tensor_tensor(out=kkr.t[:], in0=kkr.t[:], in1=sq.t[:], op=ALU.mult), [kkr, sq], [kkr])
                yield
                V(lambda e, j=j: e.tensor_scalar(out=tmpa.t[:], in0=alpha.t[:], scalar1=pv.t[:, PV_KA + j:PV_KA + j + 1], scalar2=omk.t[:, j:j + 1],
                                                 op0=ALU.mult, op1=ALU.add), [alpha, pv, omk], [tmpa])
                V(lambda e: e.tensor_tensor(out=k2.t[:], in0=kr.t[:], in1=tmpa.t[:], op=ALU.mult), [kr, tmpa], [k2])
                V(lambda e, j=j: e.scalar_tensor_tensor(out=AR.t[:, j, :, 0, :], in0=c4(kkr.t[:]), scalar=-1.0, in1=c4(Wex.t[:]), op0=ALU.mult, op1=ALU.mult), [kkr, Wex], [AR])
                G(lambda e, j=j: e.tensor_tensor(out=AR.t[:, j, :, 1, :], in0=c4(rr.t[:]), in1=c4(Wc.t[:]), op=ALU.mult), [rr, Wc], [AR])
                V(lambda e: e.tensor_tensor(out=tmpa.t[:], in0=kkr.t[:], in1=alpha.t[:], op=ALU.mult), [kkr, alpha], [tmpa])
                V(lambda e, j=j: e.tensor_tensor(out=BK.t[:, j, :, 0, :], in0=c4(tmpa.t[:]), in1=c4(iW.t[:]), op=ALU.mult), [tmpa, iW], [BK])
                G(lambda e, j=j: e.tensor_tensor(out=BK.t[:, j, :, 1, :], in0=c4(k2.t[:]), in1=c4(iW.t[:]), op=ALU.mult), [k2, iW], [BK])
                V(lambda e, j=j: e.scalar_tensor_tensor(out=rkk.t[:], in0=rr.t[:], scalar=pv.t[:, PV_RK + j:PV_RK + j + 1], in1=k2.t[:], op0=ALU.mult, op1=ALU.mult), [rr, pv, k2], [rkk])
                for st in range(NST):
                    PE(lambda e, j=j, st=st: e.matmul(projB.t[:, 256 + st * 8 + j * 2:256 + st * 8 + j * 2 + 2], lhsT=rkk.t[:, st * 128:(st + 1) * 128], rhs=hsel_b,
                                                      start=True, stop=True), [rkk, cb], [projB])
                for st in range(NST):
                    PE(lambda e, j=j, st=st: e.transpose(tpv2[:, st, j, :], vT.t[:, j, st * 128:(st + 1) * 128], ident_b), [vT, cb], [tpB])
                yield
            A(lambda e: e.activation(out=rks.t[:].rearrange("p a b -> p (a b)"), in_=projB.t[:, 256:256 + NST * 8], func=AF.Copy), [projB], [rks])
            A(lambda e: e.activation(out=Vtm.t[:].rearrange("p a b -> p (a b)"), in_=tpB.t[:].bitcast(BF16), func=AF.Copy), [tpB], [Vtm])
            yield

        A_sb2 = [A_sb, sb("A_sb1", [128, 8, 128], BF16)]
        Gp2 = [Gp, [sb(f"Gq{i}", [64, 8, 64], BF16) for i in range(2)]]
        Gfin = {}

        def tchain_gen(tile, c):
            bs = tile % 2
            AR, BK = AR2[bs], BK2[bs]
            par = c % 2
            A_s = A_sb2[par]
            Gq = Gp2[par]
            q3 = qgB.t[0:64, :].rearrange("p (a b) -> p a b", b=64)
            p3 = AmB0.t[0:64, :].rearrange("p (a b) -> p a b", b=64)
            g3 = AmB1.t[0:64, :].rearrange("p (a b) -> p a b", b=64)
            vTz = vTz2[bs]
            BKtok, UV = BKtok2[par], UV2[par]
            tp2 = tp2B.t[:].bitcast(BF16).rearrange("p (a b) -> p a b", b=128)
            for h in range(8):
                j, i = h // 2, h % 2
                bank = AmB0 if i == 0 else AmB1
                PE(lambda e, j=j, i=i, h=h, c=c, bank=bank: e.matmul(bank.t[:, j * 128:(j + 1) * 128],
                                                                     lhsT=BK.t[i * 64:(i + 1) * 64, j, c, :, :].rearrange("p a b -> p (a b)"),
                                                                     rhs=AR.t[i * 64:(i + 1) * 64, j, c, :, :].rearrange("p a b -> p (a b)"),
                                                                     start=True, stop=True), [BK, AR], [bank])
            for hb_, bank in enumerate((AmB0, AmB1)):
                V(lambda e, hb_=hb_, bank=bank: e.tensor_tensor(out=A_s.t[:, hb_::2, :], in0=bank.t[:, :].rearrange("p (a b) -> p a b", b=128),
                                                                in1=maskA3, op=ALU.mult), [bank, cst], [A_s])
            for j in range(4):
                PE(lambda e, j=j, c=c: e.transpose(tp2[:, j, :], BK.t[:, j, c, :, :].rearrange("p a b -> p (a b)"), ident_b), [BK, cb], [tp2B])
                PE(lambda e, j=j, c=c: e.transpose(tp2[:, 4 + j, :], vTz.t[:, j, c, :, :].rearrange("p a b -> p (a b)"), ident_b), [vTz, cb], [tp2B])
            A(lambda e: e.activation(out=BKtok.t[:], in_=tp2B.t[:].bitcast(BF16)[:, 0:512], func=AF.Copy), [tp2B], [BKtok])
            A(lambda e: e.activation(out=UV.t[:, :], in_=tp2B.t[:].bitcast(BF16)[:, 512:1024], func=AF.Copy), [tp2B], [UV])
            yield
            for h in (0, 2, 4, 6, 1, 3, 5, 7):
                j, i = h // 2, h % 2
                PE(lambda e, j=j, i=i, h=h, c=c: e.matmul(q3[:, h, :], lhsT=AR.t[i * 64:(i + 1) * 64, j, c, 0, :], rhs=BK.t[i * 64:(i + 1) * 64, j, c, 0, :],
                                                          start=True, stop=True), [AR, BK], [qgB], serial=(h == 1))
            V(lambda e: e.tensor_tensor(out=Qp[0].t[:], in0=q3, in1=maskQ3, op=ALU.mult), [qgB, cst], [Qp[0]])
            A(lambda e: e.activation(out=Pp[0].t[:], in_=A_s.t[0:64, :, 0:64], func=AF.Copy), [A_s], [Pp[0]])
            G(lambda e: e.tensor_tensor(out=Gq[0].t[:], in0=A_s.t[0:64, :, 0:64], in1=identl3, op=ALU.add), [A_s, cst], [Gq[0]])
            yield
            for h in range(8):
                PE(lambda e, h=h: e.matmul(p3[:, h, :], lhsT=Qp[0].t[:, h, :], rhs=Pp[0].t[:, h, :], start=True, stop=True), [Qp[0], Pp[0]], [AmB0])
            for h in range(8):
                PE(lambda e, h=h: e.matmul(q3[:, h, :], lhsT=Pp[0].t[:, h, :], rhs=Qp[0].t[:, h, :], start=True, stop=True), [Qp[0], Pp[0]], [qgB])
            A(lambda e: e.activation(out=Pp[1].t[:], in_=p3, func=AF.Copy), [AmB0], [Pp[1]])
            V(lambda e: e.tensor_copy(out=Qp[1].t[:], in_=q3), [qgB], [Qp[1]])
            yield
            for l in range(1, 5):
                li, pi = l % 2, (l - 1) % 2
                for h in range(8):
                    PE(lambda e, h=h, li=li, pi=pi: e.matmul(g3[:, h, :], lhsT=Qp[li].t[:, h, :], rhs=Gq[pi].t[:, h, :], start=True, stop=True), [Qp[li], Gq[pi]], [AmB1])
                if l <= 3:
                    for h in range(8):
                        PE(lambda e, h=h, li=li: e.matmul(p3[:, h, :], lhsT=Qp[li].t[:, h, :], rhs=Pp[li].t[:, h, :], start=True, stop=True), [Qp[li], Pp[li]], [AmB0])
                for h in range(8):
                    PE(lambda e, h=h, li=li: e.matmul(q3[:, h, :], lhsT=Pp[li].t[:, h, :], rhs=Qp[li].t[:, h, :], start=True, stop=True), [Qp[li], Pp[li]], [qgB])
                V(lambda e, li=li, pi=pi: e.tensor_tensor(out=Gq[li].t[:], in0=g3, in1=Gq[pi].t[:], op=ALU.add), [AmB1, Gq[pi]], [Gq[li]])
                if l <= 3:
                    A(lambda e, pi=pi: e.activation(out=Pp[pi].t[:], in_=p3, func=AF.Copy), [AmB0], [Pp[pi]])
                V(lambda e, pi=pi: e.tensor_copy(out=Qp[pi].t[:], in_=q3), [qgB], [Qp[pi]])
                yield
            for h in range(8):
                PE(lambda e, h=h: e.matmul(g3[:, h, :], lhsT=Qp[1].t[:, h, :], rhs=Gq[0].t[:, h, :], start=True, stop=True), [Qp[1], Gq[0]], [AmB1])
            V(lambda e: e.tensor_tensor(out=Gq[1].t[:], in0=g3, in1=Gq[0].t[:], op=ALU.add), [AmB1, Gq[0]], [Gq[1]])
            yield
            Gfin[(tile, c)] = Gq[1]

        def state_gen(tile, c):
            bs = tile % 2
            sgT, AR, BK, vTz, Vtm, Wl, rks = sgT2[bs], AR2[bs], BK2[bs], vTz2[bs], Vtm2[bs], Wl2[bs], rks2[bs]
            tp2 = tp2B.t[:].bitcast(BF16).rearrange("p (a b) -> p a b", b=128)
            p3 = paB.t[0:64, :].rearrange("p (a b) -> p a b", b=64)
            A_s = A_sb2[c % 2]
            Gf = Gfin[(tile, c)]
            cp = c % 2
            st = c // 2
            BKtok, UV = BKtok2[c % 2], UV2[c % 2]
            for h in range(8):
                j, i = h // 2, h % 2
                Sm = Sb_e if i == 0 else Sb_o
                PE(lambda e, j=j, h=h, c=c, Sm=Sm: e.matmul(p3[:, h, :], lhsT=AR.t[:, j, c, 0, :], rhs=Sm.t[:, j * 64:(j + 1) * 64],
                                                            start=True, stop=False), [AR, Sm], [paB])
                PE(lambda e, h=h: e.matmul(p3[:, h, :], lhsT=A_s.t[:, h, 0:64], rhs=UV.t[:, h * 64:(h + 1) * 64], start=False, stop=True), [A_s, UV], [paB])
            A(lambda e: e.activation(out=Xs.t[:], in_=paB.t[0:64, :], func=AF.Copy), [paB], [Xs])
            yield
            for h in range(8):
                PE(lambda e, h=h, Gf=Gf: e.matmul(p3[:, h, :], lhsT=Gf.t[:, h, :], rhs=Xs.t[:, h * 64:(h + 1) * 64], start=True, stop=True), [Gf, Xs], [paB])
            V(lambda e: e.tensor_copy(out=UV.t[0:64, :], in_=paB.t[0:64, :]), [paB], [UV])
            yield
            for h in range(8):
                j, i = h // 2, h % 2
                Sm = Sb_e if i == 0 else Sb_o
                PE(lambda e, j=j, h=h, c=c, cp=cp, Sm=Sm: e.matmul(yB.t[cp * 64:(cp + 1) * 64, h * 64:(h + 1) * 64], lhsT=AR.t[:, j, c, 1, :],
                                                                   rhs=Sm.t[:, j * 64:(j + 1) * 64], start=True, stop=False), [AR, Sm], [yB])
                PE(lambda e, h=h, cp=cp: e.matmul(yB.t[cp * 64:(cp + 1) * 64, h * 64:(h + 1) * 64], lhsT=A_s.t[:, h, 64:128], rhs=UV.t[:, h * 64:(h + 1) * 64],
                                                  start=False, stop=True), [A_s, UV], [yB])
            for h in range(8):
                j, i = h // 2, h % 2
                PE(lambda e, j=j, i=i, h=h: e.matmul(paB.t[i * 64:(i + 1) * 64, 256 + j * 64:256 + (j + 1) * 64], lhsT=BKtok.t[:, j * 128 + i * 64:j * 128 + (i + 1) * 64],
                                                     rhs=UV.t[:, h * 64:(h + 1) * 64], start=True, stop=True), [BKtok, UV], [paB])
            V(lambda e: e.tensor_tensor(out=Sf.t[:], in0=Sf.t[:], in1=paB.t[:, 256:512], op=ALU.add), [Sf, paB], [Sf])
            V(lambda e, c=c: e.tensor_tensor(out=Sf.t[:].rearrange("p (a b) -> p a b", b=64), in0=Sf.t[:].rearrange("p (a b) -> p a b", b=64),
                                             in1=Wl.t[:, :, c].unsqueeze(2).to_broadcast([128, 4, 64]), op=ALU.mult), [Sf, Wl], [Sf])
            A(lambda e: e.activation(out=Sb_e.t[0:64, :], in_=Sf.t[0:64, :], func=AF.Copy), [Sf], [Sb_e])
            A(lambda e: e.activation(out=Sb_o.t[64:128, :], in_=Sf.t[64:128, :], func=AF.Copy), [Sf], [Sb_o])
            yield

            if cp == 1:
                y3 = yB.t[:, :].rearrange("p (h v) -> p h v", v=64)

                def g8(col):
                    return gst.t[:, col:col + 8]

                def b8(col):
                    return gst.t[:, col:col + 8].unsqueeze(2).to_broadcast([128, 8, 64])
                V(lambda e: e.tensor_reduce(out=g8(0), in_=y3, axis=AX.X, op=ALU.add), [yB], [gst])
                A(lambda e: e.activation(out=ysq.t[:], in_=yB.t[:, :], func=AF.Square), [yB], [ysq])
                V(lambda e: e.tensor_reduce(out=g8(8), in_=ysq.t[:].rearrange("p (h v) -> p h v", v=64), axis=AX.X, op=ALU.add), [ysq], [gst])
                V(lambda e: e.tensor_scalar(out=g8(16), in0=g8(0), scalar1=1.0 / 64, scalar2=None, op0=ALU.mult), [gst], [gst])
                V(lambda e: e.tensor_tensor(out=g8(24), in0=g8(16), in1=g8(16), op=ALU.mult), [gst], [gst])
                V(lambda e: e.scalar_tensor_tensor(out=g8(32), in0=g8(8), scalar=1.0 / 64, in1=g8(24), op0=ALU.mult, op1=ALU.subtract), [gst], [gst])
                V(lambda e: e.tensor_scalar(out=g8(32), in0=g8(32), scalar1=GN_EPS, scalar2=None, op0=ALU.add), [gst], [gst])
                A(lambda e: e.activation(out=g8(40), in_=g8(32), func=AF.Sqrt), [gst], [gst])
                V(lambda e: e.reciprocal(out=g8(48), in_=g8(40)), [gst], [gst])
                yc3 = yc.t[:].rearrange("p (h v) -> p h v", v=64)
                V(lambda e: e.tensor_tensor(out=yc3, in0=y3, in1=b8(16), op=ALU.subtract), [yB, gst], [yc])
                yield
                V(lambda e: e.tensor_tensor(out=yc3, in0=yc3, in1=b8(48), op=ALU.mult), [yc, gst], [yc])
                G(lambda e: e.tensor_tensor(out=yc.t[:], in0=yc.t[:], in1=bc.t[:, BC_GNW:BC_GNW + 512], op=ALU.mult), [yc, bc], [yc])
                G(lambda e: e.tensor_tensor(out=yc.t[:], in0=yc.t[:], in1=bc.t[:, BC_GNB:BC_GNB + 512], op=ALU.add), [yc, bc], [yc])
                V(lambda e, st=st: e.tensor_tensor(out=bon.t[:].rearrange("p (h v) -> p h v", v=64), in0=Vtm.t[:, st, :].rearrange("p (h v) -> p h v", v=64),
                                                   in1=rks.t[:, st, :].unsqueeze(2).to_broadcast([128, 8, 64]), op=ALU.mult), [Vtm, rks], [bon])
                V(lambda e: e.tensor_tensor(out=yc.t[:], in0=yc.t[:], in1=bon.t[:], op=ALU.add), [yc, bon], [yc])
                for kc2 in range(2):
                    PE(lambda e, kc2=kc2, st=st: e.matmul(paB.t[:, :], lhsT=sgT.t[:, kc2, st * 128:(st + 1) * 128], rhs=lg2b.t[:, kc2, :], start=(kc2 == 0), stop=(kc2 == 1)), [sgT, lg2b], [paB])
                stage = ost[scount[0] % 2]
                V(lambda e, stage=stage: e.tensor_tensor(out=stage.t[:], in0=yc.t[:], in1=paB.t[:, :], op=ALU.mult), [yc, paB], [stage])
                store(stage, tile * TT + st * 128, 512)
                scount[0] += 1
                yield

        def drain(g):
            if g is not None:
                for _ in g:
                    pass

        chunks = [(tile, c) for tile in range(NTILES) for c in range(NCH)]
        drain(prep_gen(0))
        drain(tchain_gen(0, 0))
        pg = None
        pg_tile = -1
        for idx, (tile, c) in enumerate(chunks):
            if c == 0 and tile + 1 < NTILES:
                pg = prep_gen(tile + 1)
                pg_tile = tile + 1
            gs = state_gen(tile, c)
            gt = None
            if idx + 1 < len(chunks):
                nt, ncn = chunks[idx + 1]
                if nt != tile:
                    drain(pg)
                    pg = None
                gt = tchain_gen(nt, ncn)
            def step(g, n=1):
                if g is None:
                    return None
                for _ in range(n):
                    try:
                        next(g)
                    except StopIteration:
                        return None
                return g
            while gs is not None or gt is not None:
                gt = step(gt)
                gs = step(gs)
                pg = step(pg, 3)
                gt = step(gt)

    if fused:
        return S, es
    for key, ent in S.dma_sems.items():
        if key.startswith("st"):
            S._wait("sync", ("dma", ent[0], ent[1], ("dma", key)))
    return S, es


TP = 512
NSTP = TP // 128
D = 2048
DMIX = 4096
DFF = 5632
NFF = DFF // 128
NWB = 8


def build_p2(nc, S, es, NT2, dram, fused=None):
    NTILES = NT2 // TP

    def sb(name, shape, dt):
        return T(es.enter_context(nc.sbuf_tensor("q_" + name, shape, dt)), name)

    def ps(name):
        return T(es.enter_context(nc.psum_tensor("q_" + name, [128, 512], F32)), name, True)

    def V(fn, r, w): return S.op("vector", fn, [a.b for a in r], [a.b for a in w])
    def A(fn, r, w): return S.op("scalar", fn, [a.b for a in r], [a.b for a in w])
    def G(fn, r, w): return S.op("gpsimd", fn, [a.b for a in r], [a.b for a in w])
    def PE(fn, r, w): return S.op("tensor", fn, [a.b for a in r], [a.b for a in w])

    gv = sb("gv", [128, 32], F32)
    idf = sb("idf", [128, 128], F32)
    idb = sb("idb", [128, 128], BF16)
    onesf = sb("onesf", [128, 128], F32)
    xin = sb("xin", [128, D], F32)
    if fused:
        ymg = [sb(f"ymg{i}", [128, 4096], BF16) for i in range(2)]
        idx = sb("idx", [128, NT2 // 128], mybir.dt.uint32)
        S.dma("sync", "q_c2", idx.t[:], dram["idx"][:, :], writes=[idx.b])
    else:
        ymin = [sb(f"ymin{i}", [128, 1024], F32) for i in range(2)]
        ymb = [sb(f"ymb{i}", [128, 1024], BF16) for i in range(2)]
    ymT = sb("ymT", [128, 32 * TP], BF16)
    hT = sb("hT", [128, 16, TP], F32)
    vT = sb("vT", [128, 16, TP], BF16)
    aT = sb("aT", [128, NFF, TP], BF16)
    sgt = sb("sgt", [128, 4, TP], F32)
    rstd = sb("rstd", [128, TP], F32)
    hsq = [sb(f"hsq{i}", [128, TP], F32) for i in range(2)]
    oT = [sb(f"oT{i}", [128, TP], F32) for i in range(2)]
    wst = [sb(f"wst{i}", [128, 512], F32) for i in range(NWB)]
    wbf = [sb(f"wbf{i}", [128, 512], BF16) for i in range(NWB)]
    acc = [ps(f"acc{i}") for i in range(4)]
    tpP = ps("tpP")
    nrmP = ps("nrmP")

    ymT3 = ymT.t[:].rearrange("p (k t) -> p k t", t=TP)
    ostg = ymT.t[:].bitcast(F32).rearrange("p (s f) -> p s f", f=D)

    S.dma("sync", "q_c0", gv.t[:], dram["gv"][:, :], writes=[gv.b])
    S.dma("sync", "q_c1", idf.t[:], dram["idf"][:, :], writes=[idf.b])
    V(lambda e: e.tensor_copy(out=idb.t[:], in_=idf.t[:]), [idf], [idb])
    G(lambda e: e.memset(onesf.t[:], 1.0), [], [onesf])

    wcount = [0]

    def wload(src_ap):
        i = wcount[0] % NWB
        wcount[0] += 1
        S.dma("sync", f"q_w{i}", wst[i].t[:], src_ap, writes=[wst[i].b])
        m = wcount[0] % 8
        if m in (0, 3, 6):
            A(lambda e, i=i: e.activation(out=wbf[i].t[:], in_=wst[i].t[:], func=AF.Copy), [wst[i]], [wbf[i]])
        elif m == 4:
            G(lambda e, i=i: e.tensor_copy(out=wbf[i].t[:], in_=wst[i].t[:]), [wst[i]], [wbf[i]])
        else:
            V(lambda e, i=i: e.tensor_copy(out=wbf[i].t[:], in_=wst[i].t[:]), [wst[i]], [wbf[i]])
        return wbf[i]

    def rmsnorm_scale(gcol0):
        for fb in range(16):
            hq = hsq[fb % 2]
            A(lambda e, fb=fb, hq=hq: e.activation(out=hq.t[:], in_=hT.t[:, fb, :], func=AF.Square), [hT], [hq])
            PE(lambda e, fb=fb, hq=hq: e.matmul(nrmP.t[:, :], lhsT=onesf.t[:], rhs=hq.t[:], start=(fb == 0), stop=(fb == 15)), [onesf, hq], [nrmP])
        V(lambda e: e.tensor_scalar(out=rstd.t[:], in0=nrmP.t[:, :], scalar1=1.0 / D, scalar2=RMS_EPS, op0=ALU.mult, op1=ALU.add), [nrmP], [rstd])
        A(lambda e: e.activation(out=rstd.t[:], in_=rstd.t[:], func=AF.Sqrt), [rstd], [rstd])
        V(lambda e: e.reciprocal(out=rstd.t[:], in_=rstd.t[:]), [rstd], [rstd])

    ycount = [0]
    for tile in range(NTILES):
        t0 = tile * TP
        for st in range(NSTP):
            tok = t0 + st * 128
            S.dma("sync", "q_x", xin.t[:], dram["xres"][tok:tok + 128, :], writes=[xin.b])
            for g4 in range(4):
                for k in range(4):
                    fb = g4 * 4 + k
                    PE(lambda e, fb=fb, k=k: e.transpose(tpP.t[:, k * 128:(k + 1) * 128], xin.t[:, fb * 128:(fb + 1) * 128], idf.t[:]), [xin, idf], [tpP])
                if g4 % 2 == 0:
                    A(lambda e, g4=g4, st=st: e.activation(out=hT.t[:, g4 * 4:(g4 + 1) * 4, st * 128:(st + 1) * 128],
                                                           in_=tpP.t[:, :].rearrange("p (a b) -> p a b", b=128), func=AF.Copy), [tpP], [hT])
                else:
                    V(lambda e, g4=g4, st=st: e.tensor_copy(out=hT.t[:, g4 * 4:(g4 + 1) * 4, st * 128:(st + 1) * 128],
                                                            in_=tpP.t[:, :].rearrange("p (a b) -> p a b", b=128)), [tpP], [hT])
            if fused:
                gi = ycount[0] % 2
                ycount[0] += 1
                sti = tile * NSTP + st
                slab = sti // 8
                for half, (Gt, gbl) in enumerate(((fused["GS"], fused["gsB"]), (fused["GR"], fused["grB"]))):
                    for r in range(4):
                        c0 = half * 2048 + r * 512
                        S.dma_fn("gpsimd", f"q_g{gi}",
                                 lambda e, gi=gi, c0=c0, Gt=Gt, sti=sti, r=r: e.indirect_dma_start(
                                     out=ymg[gi].t[:, c0:c0 + 512], out_offset=None, in_=Gt[:, :],
                                     in_offset=bass.IndirectOffsetOnAxis(ap=idx.t[:, sti:sti + 1], axis=0),
                                     element_offset=r * 1024 * 512),
                                 reads=[idx.b] + list(gbl), writes=[ymg[gi].b])
            for pc in range(4):
                if fused:
                    src = ymg[gi]
                    cb0 = pc * 1024
                else:
                    i = ycount[0] % 2
                    ycount[0] += 1
                    S.dma("sync", f"q_y{i}", ymin[i].t[:], dram["ymix"][tok:tok + 128, pc * 1024:(pc + 1) * 1024], writes=[ymin[i].b])
                    G(lambda e, i=i: e.tensor_copy(out=ymb[i].t[:], in_=ymin[i].t[:]), [ymin[i]], [ymb[i]])
                    src = ymb[i]
                    cb0 = 0
                tpb = tpP.t[:].bitcast(BF16).rearrange("p (a b) -> p a b", b=128)
                for k in range(8):
                    PE(lambda e, src=src, cb0=cb0, k=k, tpb=tpb: e.transpose(tpb[:, k, :], src.t[:, cb0 + k * 128:cb0 + (k + 1) * 128], idb.t[:]), [src, idb], [tpP])
                V(lambda e, pc=pc, st=st, tpb=tpb: e.tensor_copy(out=ymT3[:, pc * 8:(pc + 1) * 8, st * 128:(st + 1) * 128], in_=tpb), [tpP], [ymT])
        for fg in range(4):
            for kc in range(32):
                wb = wload(dram["w_out"][kc * 128:(kc + 1) * 128, fg * 512:(fg + 1) * 512])
                for fi in range(4):
                    PE(lambda e, wb=wb, fi=fi, kc=kc: e.matmul(acc[fi].t[:, :], lhsT=wb.t[:, fi * 128:(fi + 1) * 128], rhs=ymT3[:, kc, :],
                                                               start=(kc == 0), stop=(kc == 31)), [wb, ymT], [acc[fi]])
            for fi in range(4):
                V(lambda e, fi=fi, fg=fg: e.tensor_tensor(out=hT.t[:, fg * 4 + fi, :], in0=hT.t[:, fg * 4 + fi, :], in1=acc[fi].t[:, :], op=ALU.add), [hT, acc[fi]], [hT])
        rmsnorm_scale(0)
        for fb in range(16):
            V(lambda e, fb=fb: e.scalar_tensor_tensor(out=vT.t[:, fb, :], in0=hT.t[:, fb, :], scalar=gv.t[:, fb:fb + 1], in1=rstd.t[:],
                                                      op0=ALU.mult, op1=ALU.mult), [hT, gv, rstd], [vT])
        for gg in range(NFF // 4):
            for kc in range(16):
                wb = wload(dram["w_gate"][kc * 128:(kc + 1) * 128, gg * 512:(gg + 1) * 512])
                for fi in range(4):
                    PE(lambda e, wb=wb, fi=fi, kc=kc: e.matmul(acc[fi].t[:, :], lhsT=wb.t[:, fi * 128:(fi + 1) * 128], rhs=vT.t[:, kc, :],
                                                               start=(kc == 0), stop=(kc == 15)), [wb, vT], [acc[fi]])
            for fi in range(4):
                A(lambda e, fi=fi: e.activation(out=sgt.t[:, fi, :], in_=acc[fi].t[:, :], func=AF.Silu), [acc[fi]], [sgt])
            for kc in range(16):
                wb = wload(dram["w_up"][kc * 128:(kc + 1) * 128, gg * 512:(gg + 1) * 512])
                for fi in range(4):
                    PE(lambda e, wb=wb, fi=fi, kc=kc: e.matmul(acc[fi].t[:, :], lhsT=wb.t[:, fi * 128:(fi + 1) * 128], rhs=vT.t[:, kc, :],
                                                               start=(kc == 0), stop=(kc == 15)), [wb, vT], [acc[fi]])
            for fi in range(4):
                V(lambda e, fi=fi, gg=gg: e.tensor_tensor(out=aT.t[:, gg * 4 + fi, :], in0=sgt.t[:, fi, :], in1=acc[fi].t[:, :], op=ALU.mult), [sgt, acc[fi]], [aT])
        for fg in range(4):
            for kc in range(NFF):
                wb = wload(dram["w_down"][kc * 128:(kc + 1) * 128, fg * 512:(fg + 1) * 512])
                for fi in range(4):
                    PE(lambda e, wb=wb, fi=fi, kc=kc: e.matmul(acc[fi].t[:, :], lhsT=wb.t[:, fi * 128:(fi + 1) * 128], rhs=aT.t[:, kc, :],
                                                               start=(kc == 0), stop=(kc == NFF - 1)), [wb, aT], [acc[fi]])
            for fi in range(4):
                V(lambda e, fi=fi, fg=fg: e.tensor_tensor(out=hT.t[:, fg * 4 + fi, :], in0=hT.t[:, fg * 4 + fi, :], in1=acc[fi].t[:, :], op=ALU.add), [hT, acc[fi]], [hT])
        rmsnorm_scale(16)
        for fb in range(16):
            o = oT[fb % 2]
            V(lambda e, fb=fb, o=o: e.scalar_tensor_tensor(out=o.t[:], in0=hT.t[:, fb, :], scalar=gv.t[:, 16 + fb:17 + fb], in1=rstd.t[:],
                                                           op0=ALU.mult, op1=ALU.mult), [hT, gv, rstd], [o])
            for st in range(NSTP):
                PE(lambda e, st=st, o=o: e.transpose(tpP.t[:, st * 128:(st + 1) * 128], o.t[:, st * 128:(st + 1) * 128], idf.t[:]), [o, idf], [tpP])
            A(lambda e, fb=fb: e.activation(out=ostg[:, :, fb * 128:(fb + 1) * 128], in_=tpP.t[:, :].rearrange("p (a b) -> p a b", b=128), func=AF.Copy), [tpP], [ymT])
        for st in range(NSTP):
            tok = t0 + st * 128
            S.dma("sync", f"q_o{st}", dram["out"][tok:tok + 128, :], ostg[:, st, :], reads=[ymT.b], writes=[])
    for key, ent in S.dma_sems.items():
        if key.startswith("q_o"):
            S._wait("sync", ("dma", ent[0], ent[1], ("dma", key)))


def prep_core_p1(inp, b, q, NT):
    f = np.float32
    w_in = inp["w_in"][0]
    d = {}
    d["x"] = np.ascontiguousarray(inp["x"][b, :NT, :])
    zc = w_in[:, q * 512:(q + 1) * 512]
    dtc = w_in[:, 5120 + q * 8:5120 + (q + 1) * 8]
    xsc = w_in[:, 2048 + q * 512:2048 + (q + 1) * 512]
    Bc = w_in[:, 4096 + q * 128:4096 + (q + 1) * 128]
    Cc = w_in[:, 4608 + q * 128:4608 + (q + 1) * 128]
    d["w1"] = np.ascontiguousarray(np.concatenate([zc, dtc, xsc, Bc, Cc], axis=1))
    rw = w_in[:, 5152:]
    cols = [rw[:, 6144:6240], rw[:, 6240:6336], rw[:, 6336:6592]]
    for j in range(4):
        o = q * 512 + j * 128
        cols += [rw[:, o:o + 128], rw[:, 2048 + o:2048 + o + 128], rw[:, 4096 + o:4096 + o + 128]]
    d["w2"] = np.ascontiguousarray(np.concatenate(cols, axis=1))
    assert d["w1"].shape[1] == N1 and d["w2"].shape[1] == N2
    pv = np.zeros((128, NPV), f)
    cw = inp["ssd_conv_w"][0]; cbias = inp["ssd_conv_b"][0]
    chs = [q * 512 + blk * 128 for blk in range(4)] + [2048 + q * 128, 2560 + q * 128]
    for blk, ch in enumerate(chs):
        for j in range(4):
            pv[:, PV_CW + blk * 4 + j] = cw[j, ch:ch + 128]
        pv[:, PV_CB + blk] = cbias[ch:ch + 128]
    mu = inp["rwkv_mu"][0]
    pv[:96, PV_MU + 0] = mu[6144:6240]
    pv[:96, PV_MU + 1] = mu[6240:6336]
    pv[:, PV_MU + 2] = mu[6336:6464]
    pv[:, PV_MU + 3] = mu[6464:6592]
    for j in range(4):
        o = q * 512 + j * 128
        pv[:, PV_MU + 4 + j * 3 + 0] = mu[o:o + 128]
        pv[:, PV_MU + 4 + j * 3 + 1] = mu[2048 + o:2048 + o + 128]
        pv[:, PV_MU + 4 + j * 3 + 2] = mu[4096 + o:4096 + o + 128]
        pv[:, PV_W0 + j] = inp["rwkv_w0"][0][o:o + 128]
        pv[:, PV_A0 + j] = inp["rwkv_a0"][0][o:o + 128]
        pv[:, PV_KK + j] = inp["rwkv_k_k"][0][o:o + 128]
        pv[:, PV_KA + j] = inp["rwkv_k_a"][0][o:o + 128]
        pv[:, PV_RK + j] = inp["rwkv_r_k"][0][o:o + 128]
    pv[:, PV_G1:PV_G1 + 16] = inp["norm1_g"][0].reshape(16, 128).T
    d["pv"] = pv
    bc = np.zeros((128, NBC), f)
    bc[:, BC_NG:BC_NG + 512] = inp["ssd_norm_g"][0][q * 512:(q + 1) * 512][None]
    bc[:, BC_GNW:BC_GNW + 512] = inp["rwkv_gn_w"][0][q * 512:(q + 1) * 512][None]
    bc[:, BC_GNB:BC_GNB + 512] = inp["rwkv_gn_b"][0][q * 512:(q + 1) * 512][None]
    bc[:, BC_DTB:BC_DTB + 8] = inp["ssd_dt_bias"][0][q * 8:(q + 1) * 8][None]
    bc[:, BC_ALOG:BC_ALOG + 8] = inp["ssd_A_log"][0][q * 8:(q + 1) * 8][None]
    bc[:, BC_D:BC_D + 8] = inp["ssd_D"][0][q * 8:(q + 1) * 8][None]
    d["bc"] = bc
    d["cst"] = make_consts()
    d["lw2"] = np.ascontiguousarray(inp["rwkv_w2"][0][:, q * 512:(q + 1) * 512])
    d["la2"] = np.ascontiguousarray(inp["rwkv_a2"][0][:, q * 512:(q + 1) * 512])
    d["lg2"] = np.ascontiguousarray(inp["rwkv_g2"][0][:, q * 512:(q + 1) * 512])
    return d

from concourse.bass_utils import run_bass_kernel_spmd

SEQ = 8192
NCORES = 8


def _build_prog1(NT):
    nc = bass.Bass("TRN2", target_bir_lowering=False)
    dram = {}

    def din(name, shape):
        dram[name] = nc.dram_tensor(name, list(shape), F32, kind="ExternalInput").ap()
    din("x", [NT, 2048]); din("w1", [2048, N1]); din("w2", [2048, N2]); din("pv", [128, NPV]); din("bc", [128, NBC])
    din("cst", [128, NCS]); din("lw2", [96, 512]); din("la2", [96, 512]); din("lg2", [256, 512])
    dram["ymix"] = nc.dram_tensor("ymix", [NT, 1024], F32, kind="ExternalOutput").ap()
    S, es = build_p1(nc, NT, dram)
    with nc.Block() as block:
        S.emit(block)
    es.close()
    return nc


def _build_prog2(NT2):
    nc = bass.Bass("TRN2", target_bir_lowering=False)
    dram = {}

    def din(name, shape):
        dram[name] = nc.dram_tensor(name, list(shape), F32, kind="ExternalInput").ap()
    din("xres", [NT2, 2048]); din("ymix", [NT2, 4096]); din("w_out", [4096, 2048]); din("w_gate", [2048, 5632])
    din("w_up", [2048, 5632]); din("w_down", [5632, 2048]); din("gv", [128, 32]); din("idf", [128, 128])
    dram["out"] = nc.dram_tensor("out", [NT2, 2048], F32, kind="ExternalOutput").ap()
    es = ExitStack()
    S = Sched(nc, es)
    build_p2(nc, S, es, NT2, dram)
    with nc.Block() as block:
        S.emit(block)
    es.close()
    return nc


def _build_fused(NT, NT2):
    nc = bass.Bass("TRN2", target_bir_lowering=False)
    dram = {}

    def din(name, shape, dt=F32):
        dram[name] = nc.dram_tensor(name, list(shape), dt, kind="ExternalInput").ap()
    din("x", [NT, 2048]); din("w1", [2048, N1]); din("w2", [2048, N2]); din("pv", [128, NPV]); din("bc", [128, NBC])
    din("cst", [128, NCS]); din("lw2", [96, 512]); din("la2", [96, 512]); din("lg2", [256, 512])
    din("xres", [NT2, 2048]); din("w_out", [4096, 2048]); din("w_gate", [2048, 5632])
    din("w_up", [2048, 5632]); din("w_down", [5632, 2048]); din("gv", [128, 32]); din("idf", [128, 128])
    din("idx", [128, NT2 // 128], mybir.dt.uint32)
    dram["out"] = nc.dram_tensor("out", [NT2, 2048], F32, kind="ExternalOutput").ap()
    nslab = NT // 1024
    fused = {
        "YS": nc.dram_tensor("YS", [NT, 512], BF16), "YR": nc.dram_tensor("YR", [NT, 512], BF16),
        "GS": nc.dram_tensor("GS", [4 * NT, 512], BF16), "GR": nc.dram_tensor("GR", [4 * NT, 512], BF16),
        "gsB": [Buf(f"gs{k}") for k in range(nslab)], "grB": [Buf(f"gr{k}") for k in range(nslab)],
        "groups": [[0, 1, 2, 3], [4, 5, 6, 7]],
    }
    ses = ExitStack()
    S = Sched(nc, ses)
    es1 = ExitStack()
    build_p1(nc, NT, dram, S=S, es=es1, fused=fused)
    S.barrier()
    es1.close()
    es2 = ExitStack()
    build_p2(nc, S, es2, NT2, dram, fused=fused)
    with nc.Block() as block:
        S.emit(block)
    es2.close()
    ses.close()
    return nc


def kernel(**inp):
    inp = {k: np.asarray(v) for k, v in inp.items()}
    B = inp["x"].shape[0]
    NT = inp["x"].shape[1]
    NT2 = B * NT // NCORES
    nc = _build_fused(NT, NT2)
    gv = np.ascontiguousarray(np.concatenate([inp["norm2_g"][0].reshape(16, 128).T, inp["norm_f_g"].reshape(16, 128).T], axis=1).astype(np.float32))
    idf = np.eye(128, dtype=np.float32)
    shared = {"w_out": np.ascontiguousarray(inp["w_out"][0]), "w_gate": np.ascontiguousarray(inp["w_gate"][0]),
              "w_up": np.ascontiguousarray(inp["w_up"][0]), "w_down": np.ascontiguousarray(inp["w_down"][0]),
              "gv": gv, "idf": idf}
    maps = []
    for c in range(NCORES):
        b, j = c // 4, c % 4
        m = prep_core_p1(inp, b, j, NT)
        m.update(shared)
        m["xres"] = np.ascontiguousarray(inp["x"][b, j * NT2:(j + 1) * NT2, :])
        g = j * NT2 + np.arange(NT2 // 128)[None, :] * 128 + np.arange(128)[:, None]
        m["idx"] = ((g // 1024) * 4096 + (g % 1024)).astype(np.uint32)
        maps.append(m)
    res = run_bass_kernel_spmd(nc, maps, core_ids=list(range(NCORES)))
    out = np.stack([np.concatenate([res.results[b * 4 + j]["out"] for j in range(4)], axis=0) for b in range(B)], axis=0)
    return out.astype(np.float32)
```

```python
from contextlib import ExitStack
import numpy as np
import concourse.bass as bass
import concourse.mybir as mybir

EPOCH = 12000


class Buf:
    __slots__ = ("name", "w", "r", "excl")

    def __init__(self, name, excl=False):
        self.name = name
        self.excl = excl
        self.w = None
        self.r = []


class Sched:
    ENG = ("tensor", "vector", "scalar", "gpsimd", "sync")

    def __init__(self, nc, sem_ctx):
        self.nc = nc
        self.sem_ctx = sem_ctx
        self.count = {e: 0 for e in self.ENG}
        self.sems = {e: [] for e in self.ENG}
        self.waited = {e: {} for e in self.ENG}
        self.prog = {e: [] for e in self.ENG}
        self.dma_sems = {}
        self.cc_tags = []
        self.nwaits = 0

    def _sem(self, eng, k):
        idx = (k - 1) // EPOCH
        lst = self.sems[eng]
        while len(lst) <= idx:
            lst.append(self.sem_ctx.enter_context(self.nc.semaphore(f"s_{eng}_{len(lst)}")))
        return lst[idx], (k - 1) % EPOCH + 1, (eng, idx)

    def _wait(self, eng, dep):
        if dep is None:
            return
        if dep[0] == "dma":
            _, sem, val, key = dep
        else:
            sem, val, key = self._sem(dep[0], dep[1])
        w = self.waited[eng]
        if w.get(key, 0) >= val:
            return
        w[key] = val
        self.nwaits += 1
        self.prog[eng].append(lambda e, sem=sem, val=val: e.wait_ge(sem, val))

    def op(self, eng, fn, reads=(), writes=(), serial=False):
        deps = []
        if serial and self.count[eng] > 0:
            self._wait(eng, (eng, self.count[eng]))
        for b in reads:
            if b.w is not None:
                deps.append(b.w)
            if b.excl:
                deps.extend(r for r in b.r if r[0] != eng)
        for b in writes:
            if b.w is not None:
                deps.append(b.w)
            deps.extend(b.r)
        for d in deps:
            if d[0] == eng and d[0] != "dma" and eng == "tensor":
                continue
            self._wait(eng, d)
        self.count[eng] += 1
        k = self.count[eng]
        sem, val, _ = self._sem(eng, k)
        self.prog[eng].append(lambda e, fn=fn, sem=sem: fn(e).then_inc(sem, 1))
        tag = (eng, k)
        for b in reads:
            b.r.append(tag)
        for b in writes:
            b.w = tag
            b.r = []
        return tag

    def dma(self, eng, key, out, in_, reads=(), writes=(), **kw):
        if key not in self.dma_sems:
            self.dma_sems[key] = [self.sem_ctx.enter_context(self.nc.semaphore(f"d_{key}")), 0]
        ent = self.dma_sems[key]
        deps = []
        for b in reads:
            if b.w is not None:
                deps.append(b.w)
        for b in writes:
            if b.w is not None and not (b.w[0] == "dma" and b.w[3] == ("dma", key)):
                deps.append(b.w)
            deps.extend(b.r)
        for d in deps:
            self._wait(eng, d)
        ent[1] += 16
        sem, val = ent[0], ent[1]
        self.prog[eng].append(lambda e, sem=sem, out=out, in_=in_, kw=kw: e.dma_start(out=out, in_=in_, **kw).then_inc(sem, 16))
        tag = ("dma", sem, val, ("dma", key))
        for b in reads:
            b.r.append(tag)
        for b in writes:
            b.w = tag
            b.r = []
        return tag

    def dma_fn(self, eng, key, fn, reads=(), writes=()):
        if key not in self.dma_sems:
            self.dma_sems[key] = [self.sem_ctx.enter_context(self.nc.semaphore(f"d_{key}")), 0]
        ent = self.dma_sems[key]
        deps = []
        for b in reads:
            if b.w is not None:
                deps.append(b.w)
        for b in writes:
            if b.w is not None and not (b.w[0] == "dma" and b.w[3] == ("dma", key)):
                deps.append(b.w)
            deps.extend(b.r)
        for d in deps:
            self._wait(eng, d)
        ent[1] += 16
        sem, val = ent[0], ent[1]
        self.prog[eng].append(lambda e, sem=sem, fn=fn: fn(e).then_inc(sem, 16))
        tag = ("dma", sem, val, ("dma", key))
        for b in reads:
            b.r.append(tag)
        for b in writes:
            b.w = tag
            b.r = []
        return tag

    def collective(self, name, fn, reads=(), writes=()):
        sem = self.sem_ctx.enter_context(self.nc.semaphore(f"cc_{name}"))
        deps = []
        for b in reads:
            if b.w is not None:
                deps.append(b.w)
        for b in writes:
            if b.w is not None:
                deps.append(b.w)
            deps.extend(b.r)
        for d in deps:
            self._wait("gpsimd", d)
        self.prog["gpsimd"].append(lambda e, sem=sem, fn=fn: fn(e).then_inc(sem))
        tag = ("dma", sem, 1, ("cc", name))
        self.cc_tags.append(tag)
        for b in reads:
            b.r.append(tag)
        for b in writes:
            b.w = tag
            b.r = []
        return tag

    def barrier(self):
        for e in self.ENG:
            for o in self.ENG:
                if o != e and self.count[o] > 0:
                    self._wait(e, (o, self.count[o]))
            for key, ent in self.dma_sems.items():
                if ent[1] > 0:
                    self._wait(e, ("dma", ent[0], ent[1], ("dma", key)))
            for tag in self.cc_tags:
                self._wait(e, tag)

    def wait_all(self, eng, bufs):
        for b in bufs:
            if b.w is not None:
                self._wait(eng, b.w)
            for d in b.r:
                self._wait(eng, d)

    def emit(self, block):
        def mk(eng):
            def body(e):
                for f in self.prog[eng]:
                    f(e)
            return body
        block.tensor(mk("tensor"))
        block.vector(mk("vector"))
        block.scalar(mk("scalar"))
        block.gpsimd(mk("gpsimd"))
        block.sync(mk("sync"))


F32 = mybir.dt.float32
BF16 = mybir.dt.bfloat16
AF = mybir.ActivationFunctionType
ALU = mybir.AluOpType
AX = mybir.AxisListType

TT = 256
NCH = TT // 64
NST = TT // 128
C0 = float(np.exp(-0.5))
RMS_EPS = 1e-6
GATED_NORM_EPS = 1e-5
GN_EPS = 64e-5
NEG = -30000.0

N1 = 1288
N2 = 1984
PV_CW = 0
PV_CB = 24
PV_MU = 30
PV_W0 = 46
PV_A0 = 50
PV_KK = 54
PV_KA = 58
PV_RK = 62
PV_G1 = 66
NPV = 82
BC_NG = 0
BC_GNW = 512
BC_GNB = 1024
BC_DTB = 1536
BC_ALOG = 1544
BC_D = 1552
NBC = 1560
CS_ID = 0
CS_TRI2 = 128
CS_TRIL = 256
CS_BONES = 320
CS_NEGM = 448
CS_CH0 = 960
CS_CH1 = 1088
CS_MASKA = 1216
CS_MASKQ = 1344
CS_SCAN = 1408
CS_IDL = 1664
CS_HSEL = 1728
NCS = 1730


def make_consts():
    c = np.zeros((128, NCS), np.float32)
    p = np.arange(128)
    pl = p % 64
    c[:, CS_ID:CS_ID + 128] = np.eye(128)
    c[:, CS_TRI2:CS_TRI2 + 128] = ((p[:, None] // 64 == p[None, :] // 64) & (p[:, None] <= p[None, :]))
    l64 = np.arange(64)
    c[:, CS_TRIL:CS_TRIL + 64] = (pl[:, None] <= l64[None, :])
    c[:, CS_BONES:CS_BONES + 128] = (p[:, None] // 64 == p[None, :] // 64)
    nm = np.where(l64[None, :] < pl[:, None], NEG, 0.0)
    c[:, CS_NEGM:CS_NEGM + 512] = np.tile(nm, (1, 8))
    c[:, CS_CH0:CS_CH0 + 128] = (p[:, None] < 64)
    c[:, CS_CH1:CS_CH1 + 128] = (p[:, None] >= 64)
    c[:, CS_MASKA:CS_MASKA + 64] = (l64[None, :] > pl[:, None])
    c[:, CS_MASKA + 64:CS_MASKA + 128] = (l64[None, :] >= pl[:, None])
    c[:, CS_MASKQ:CS_MASKQ + 64] = (pl[:, None] > l64[None, :])
    sm = np.ones(256); sm[::64] = 0
    c[:, CS_SCAN:CS_SCAN + 256] = sm[None, :]
    c[:, CS_IDL:CS_IDL + 64] = (pl[:, None] == l64[None, :])
    c[:, CS_HSEL] = (p < 64)
    c[:, CS_HSEL + 1] = (p >= 64)
    return c


class T:
    def __init__(self, t, name, excl=False):
        self.t = t
        self.b = Buf(name, excl)


class _Stop(Exception):
    pass


def build_p1(nc, NT, dram, do_ssd=True, do_rwkv=True, dbg=99, S=None, es=None, fused=None):
    NTILES = NT // TT
    if S is None:
        es = ExitStack()
        S = Sched(nc, es)

    def sb(name, shape, dt):
        return T(es.enter_context(nc.sbuf_tensor("s_" + name, shape, dt)), name)

    def ps(name):
        return T(es.enter_context(nc.psum_tensor(name, [128, 512], F32)), name, True)

    def V(fn, r, w): return S.op("vector", fn, [a.b for a in r], [a.b for a in w])
    def A(fn, r, w): return S.op("scalar", fn, [a.b for a in r], [a.b for a in w])
    def G(fn, r, w): return S.op("gpsimd", fn, [a.b for a in r], [a.b for a in w])
    def PE(fn, r, w, serial=False): return S.op("tensor", fn, [a.b for a in r], [a.b for a in w], serial=serial)

    W = sb("W", [128, 16, N2], BF16)
    pv = sb("pv", [128, NPV], F32)
    bc = sb("bc", [128, NBC], F32)
    cst = sb("cst", [128, NCS], F32)
    cb = sb("cb", [128, 1216], BF16)
    omk = sb("omk", [128, 4], F32)
    aneg = sb("aneg", [128, 8], F32)
    DI = sb("DI", [128, 512], F32)
    lw2b = sb("lw2b", [96, 512], BF16)
    la2b = sb("la2b", [96, 512], BF16)
    lg2b = sb("lg2b", [128, 2, 512], BF16)
    xt = [sb(f"xt{i}", [128, 2048], F32) for i in range(2)]
    xn = sb("xn", [128, 2048], BF16)
    uT = sb("uT", [128, 16, TT], BF16)
    sm = sb("sm", [128, 64], F32)
    ost = [sb(f"ost{i}", [128, 512], BF16 if fused else F32) for i in range(2)]

    banks = [ps(f"bank{i}") for i in range(8)]
    tpB = banks[0]

    ident_f = cst.t[:, CS_ID:CS_ID + 128]
    ident_b = cb.t[:, 0:128]

    S.dma("sync", "ld_c0", pv.t[:], dram["pv"][:, :], writes=[pv.b])
    S.dma("sync", "ld_c1", bc.t[:], dram["bc"][:, :], writes=[bc.b])
    S.dma("sync", "ld_c2", cst.t[:], dram["cst"][:, :], writes=[cst.b])
    S.dma("gpsimd", "ld_l0", lw2b.t[:], dram["lw2"][:, :], writes=[lw2b.b])
    S.dma("gpsimd", "ld_l1", la2b.t[:], dram["la2"][:, :], writes=[la2b.b])
    S.dma("gpsimd", "ld_l2", lg2b.t[:], dram["lg2"].rearrange("(c p) n -> p c n", p=128), writes=[lg2b.b])
    V(lambda e: e.tensor_copy(out=cb.t[:, 0:128], in_=cst.t[:, CS_ID:CS_ID + 128]), [cst], [cb])
    V(lambda e: e.tensor_copy(out=cb.t[:, 128:130], in_=cst.t[:, CS_HSEL:CS_HSEL + 2]), [cst], [cb])
    hsel_b = cb.t[:, 128:130]
    V(lambda e: e.tensor_scalar(out=omk.t[:], in0=pv.t[:, PV_KA:PV_KA + 4], scalar1=-1.0, scalar2=1.0, op0=ALU.mult, op1=ALU.add), [pv], [omk])
    A(lambda e: e.activation(out=aneg.t[:], in_=bc.t[:, BC_ALOG:BC_ALOG + 8], func=AF.Exp), [bc], [aneg])
    V(lambda e: e.tensor_scalar(out=aneg.t[:], in0=aneg.t[:], scalar1=-1.0, scalar2=None, op0=ALU.mult), [aneg], [aneg])
    V(lambda e: e.tensor_tensor(out=DI.t[:].rearrange("p (e l) -> p e l", l=64),
                                in0=cst.t[:, CS_IDL:CS_IDL + 64].unsqueeze(1).to_broadcast([128, 8, 64]),
                                in1=bc.t[:, BC_D:BC_D + 8].unsqueeze(2).to_broadcast([128, 8, 64]), op=ALU.mult), [cst, bc], [DI])

    def load_weights(wdram, ncols):
        tmp = ExitStack()
        wstg = [T(tmp.enter_context(nc.sbuf_tensor(f"s_wstg{i}_{ncols}", [128, ncols], F32)), f"wstg{i}") for i in range(3)]
        wv = wdram.rearrange("(c p) n -> p c n", p=128)
        for kc in range(16):
            stg = wstg[kc % 3]
            S.dma("sync", f"ld_w{kc % 3}", stg.t[:, 0:ncols], wv[:, kc, :], writes=[stg.b])
            gcol = pv.t[:, PV_G1 + kc:PV_G1 + kc + 1]
            m = kc % 3
            if m == 0:
                V(lambda e, kc=kc, stg=stg, gcol=gcol: e.tensor_scalar(out=W.t[:, kc, 0:ncols], in0=stg.t[:, 0:ncols], scalar1=gcol, scalar2=None, op0=ALU.mult), [stg, pv], [W])
            elif m == 1:
                A(lambda e, kc=kc, stg=stg, gcol=gcol: e.activation(out=W.t[:, kc, 0:ncols], in_=stg.t[:, 0:ncols], func=AF.Copy, scale=gcol), [stg, pv], [W])
            else:
                G(lambda e, kc=kc, stg=stg, gcol=gcol: e.tensor_scalar(out=W.t[:, kc, 0:ncols], in0=stg.t[:, 0:ncols], scalar1=gcol, scalar2=None, op0=ALU.mult), [stg, pv], [W])
        S.barrier()
        tmp.close()

    xcount = [0]

    def load_norm_transpose(tile):
        for st in range(NST):
            slot = xcount[0] % 2
            xcount[0] += 1
            X = xt[slot]
            tok0 = tile * TT + st * 128
            S.dma("sync", f"ldx{slot}", X.t[:], dram["x"][tok0:tok0 + 128, :], writes=[X.b])
            A(lambda e, X=X: e.activation(out=xn.t[:], in_=X.t[:], func=AF.Square, accum_out=sm.t[:, 0:1]), [X], [xn, sm])
            V(lambda e: e.tensor_scalar(out=sm.t[:, 1:2], in0=sm.t[:, 0:1], scalar1=1.0 / 2048, scalar2=RMS_EPS, op0=ALU.mult, op1=ALU.add), [sm], [sm])
            A(lambda e: e.activation(out=sm.t[:, 2:3], in_=sm.t[:, 1:2], func=AF.Sqrt), [sm], [sm])
            V(lambda e: e.reciprocal(out=sm.t[:, 3:4], in_=sm.t[:, 2:3]), [sm], [sm])
            A(lambda e, X=X: e.activation(out=xn.t[:], in_=X.t[:], func=AF.Copy, scale=sm.t[:, 3:4]), [X, sm], [xn])
            tpv = tpB.t[:].bitcast(BF16).rearrange("p (a b) -> p a b", b=128)
            for half in range(2):
                for k8 in range(8):
                    kc = half * 8 + k8
                    PE(lambda e, kc=kc, k8=k8: e.transpose(tpv[:, k8, :], xn.t[:, kc * 128:(kc + 1) * 128], ident_b), [xn, cb], [tpB])
                eng = V if half == 0 else A
                if half == 0:
                    V(lambda e, half=half, st=st: e.tensor_copy(out=uT.t[:, half * 8:(half + 1) * 8, st * 128:(st + 1) * 128], in_=tpv), [tpB], [uT])
                else:
                    A(lambda e, half=half, st=st: e.activation(out=uT.t[:, half * 8:(half + 1) * 8, st * 128:(st + 1) * 128], in_=tpv, func=AF.Copy), [tpB], [uT])

    def proj_fm(col0, M, out_ps):
        for kc in range(16):
            PE(lambda e, kc=kc: e.matmul(out_ps.t[0:M, 0:TT], lhsT=W.t[:, kc, col0:col0 + M], rhs=uT.t[:, kc, :],
                                         start=(kc == 0), stop=(kc == 15)), [W, uT], [out_ps])

    scount = [0]
    UTd = fused["UT"] if fused else None

    slab_tags = []

    def store(stage, tok0, col0):
        if not fused:
            S.dma("sync", f"st{scount[0] % 2}", dram["ymix"][tok0:tok0 + 128, col0:col0 + 512], stage.t[:], reads=[stage.b], writes=[])
            return
        Y = fused["YS"] if col0 == 0 else fused["YR"]
        Gt = fused["GS"] if col0 == 0 else fused["GR"]
        gb = fused["gsB"] if col0 == 0 else fused["grB"]
        tag = S.dma("sync", f"st{scount[0] % 2}", Y[tok0:tok0 + 128, :], stage.t[:], reads=[stage.b], writes=[])
        slab_tags.append(tag)
        if (tok0 + 128) % 1024 == 0:
            k = tok0 // 1024
            for tg in slab_tags:
                S._wait("gpsimd", tg)
            del slab_tags[:]
            S.collective(f"{'s' if col0 == 0 else 'r'}{k}",
                         lambda e, k=k, Y=Y, Gt=Gt: e.collective_compute("AllGather", ALU.bypass, replica_groups=fused["groups"],
                                                                        ins=[Y[k * 1024:(k + 1) * 1024, :].opt()],
                                                                        outs=[Gt[k * 4096:(k + 1) * 4096, :].opt()]),
                         reads=[], writes=[gb[k]])

    if do_ssd:
        load_weights(dram["w1"], N1)
        es1 = ExitStack()

        def sb1(name, shape, dt):
            return T(es1.enter_context(nc.sbuf_tensor("s_" + name, shape, dt)), name)

        projB, ARb, ydB, yoB, hnB, miscB, dtB = banks[1], banks[2], banks[3], banks[4], banks[5], banks[6], banks[7]
        Pb = sb1("Pb", [128, TT + 3], F32)
        hist = sb1("hist", [128, 6, 3], F32)
        acc = sb1("acc", [128, TT], F32)
        xsT = sb1("xsT", [128, 4, TT], BF16)
        sz2 = [sb1(f"sz{i}", [128, NST, 512], F32) for i in range(2)]
        BT2 = [sb1(f"BT{i}", [128, TT], BF16) for i in range(2)]
        CT2 = [sb1(f"CT{i}", [128, TT], BF16) for i in range(2)]
        Xtm2 = [sb1(f"Xtm{i}", [128, NST, 512], BF16) for i in range(2)]
        Btm2 = [sb1(f"Btm{i}", [128, NST, 128], BF16) for i in range(2)]
        dtr2 = [sb1(f"dtr{i}", [128, NST, 8], F32) for i in range(2)]
        dts = sb1("dts", [128, 64], F32)
        Dm = sb1("Dm", [128, 512], F32)
        LT = sb1("LT", [128, 512], F32)
        MTb = sb1("MTb", [128, 512], BF16)
        Xd = sb1("Xd", [128, 512], BF16)
        t1 = sb1("t1", [128, 512], F32)
        ys = sb1("ys", [128, 512], F32)
        hf = sb1("hf", [128, 512], F32)
        hb = sb1("hb", [128, 512], BF16)
        cdb = sb1("cdb", [128, 2, 8], F32)
        sm1 = sb1("sm1", [128, 8], F32)

        G(lambda e: e.memset(hist.t[:], 0.0), [], [hist])
        G(lambda e: e.memset(hf.t[:], 0.0), [], [hf])
        G(lambda e: e.memset(hb.t[:], 0.0), [], [hb])

        def v8(ap):
            return ap.unsqueeze(2).to_broadcast([128, 8, 64])

        def r3(ap):
            return ap.rearrange("p (e l) -> p e l", l=64)

        DT, ADT, ACS, EA, SD, TMP, NACS = 0, 8, 16, 24, 32, 40, 48

        def prep1_gen(tile):
            bs = tile % 2
            sz, BT, CT, Xtm, Btm, dtr = sz2[bs], BT2[bs], CT2[bs], Xtm2[bs], Btm2[bs], dtr2[bs]
            load_norm_transpose(tile)
            if UTd is not None and do_rwkv:
                S.dma("sync", "st_u", UTd[tile], uT.t[:].rearrange("p a b -> p (a b)"), reads=[uT.b], writes=[])
            yield
            for st in range(NST):
                for kc in range(16):
                    PE(lambda e, kc=kc, st=st: e.matmul(projB.t[:, 0:512], lhsT=uT.t[:, kc, st * 128:(st + 1) * 128], rhs=W.t[:, kc, 0:512],
                                                        start=(kc == 0), stop=(kc == 15)), [W, uT], [projB])
                A(lambda e, st=st: e.activation(out=sz.t[:, st, :], in_=projB.t[:, 0:512], func=AF.Silu), [projB], [sz])
                for kc in range(16):
                    PE(lambda e, kc=kc, st=st: e.matmul(dtB.t[:, st * 8:(st + 1) * 8], lhsT=uT.t[:, kc, st * 128:(st + 1) * 128], rhs=W.t[:, kc, 512:520],
                                                        start=(kc == 0), stop=(kc == 15)), [W, uT], [dtB])
                yield
            V(lambda e: e.tensor_copy(out=dtr.t[:].rearrange("p a b -> p (a b)"), in_=dtB.t[:, 0:NST * 8]), [dtB], [dtr])
            for blk in range(6):
                proj_fm(520 + blk * 128, 128, projB)
                G(lambda e, blk=blk: e.tensor_copy(out=Pb.t[:, 0:3], in_=hist.t[:, blk, :]), [hist], [Pb])
                A(lambda e: e.activation(out=Pb.t[:, 3:3 + TT], in_=projB.t[:, 0:TT], func=AF.Copy), [projB], [Pb])
                A(lambda e, blk=blk: e.activation(out=acc.t[:], in_=projB.t[:, 0:TT], func=AF.Identity,
                                                  scale=pv.t[:, PV_CW + blk * 4 + 3:PV_CW + blk * 4 + 4],
                                                  bias=pv.t[:, PV_CB + blk:PV_CB + blk + 1]), [projB, pv], [acc])
                for j in (2, 1, 0):
                    V(lambda e, blk=blk, j=j: e.scalar_tensor_tensor(out=acc.t[:], in0=Pb.t[:, j:j + TT],
                                                                     scalar=pv.t[:, PV_CW + blk * 4 + j:PV_CW + blk * 4 + j + 1],
                                                                     in1=acc.t[:], op0=ALU.mult, op1=ALU.add), [Pb, pv, acc], [acc])
                G(lambda e, blk=blk: e.tensor_copy(out=hist.t[:, blk, :], in_=Pb.t[:, TT:TT + 3]), [Pb], [hist])
                if blk < 4:
                    A(lambda e, blk=blk: e.activation(out=xsT.t[:, blk, :], in_=acc.t[:], func=AF.Silu), [acc], [xsT])
                elif blk == 4:
                    A(lambda e: e.activation(out=BT.t[:], in_=acc.t[:], func=AF.Silu), [acc], [BT])
                else:
                    A(lambda e: e.activation(out=CT.t[:], in_=acc.t[:], func=AF.Silu), [acc], [CT])
                yield
            tpv1 = tpB.t[:].bitcast(BF16)
            for st in range(NST):
                tsl = slice(st * 128, (st + 1) * 128)
                for blk in range(4):
                    PE(lambda e, blk=blk, tsl=tsl: e.transpose(tpv1[:, blk * 128:(blk + 1) * 128], xsT.t[:, blk, tsl], ident_b), [xsT, cb], [tpB])
                PE(lambda e, tsl=tsl: e.transpose(tpv1[:, 512:640], BT.t[:, tsl], ident_b), [BT, cb], [tpB])
                A(lambda e, st=st: e.activation(out=Xtm.t[:, st, :], in_=tpv1[:, 0:512], func=AF.Copy), [tpB], [Xtm])
                A(lambda e, st=st: e.activation(out=Btm.t[:, st, :], in_=tpv1[:, 512:640], func=AF.Copy), [tpB], [Btm])
                yield

        def core1_gen(tile):
            bs = tile % 2
            sz, BT, CT, Xtm, Btm, dtr = sz2[bs], BT2[bs], CT2[bs], Xtm2[bs], Btm2[bs], dtr2[bs]
            for st in range(NST):
                V(lambda e, st=st: e.tensor_tensor(out=dts.t[:, TMP:TMP + 8], in0=dtr.t[:, st, :], in1=bc.t[:, BC_DTB:BC_DTB + 8], op=ALU.add), [dtr, bc], [dts])
                A(lambda e: e.activation(out=dts.t[:, DT:DT + 8], in_=dts.t[:, TMP:TMP + 8], func=AF.Abs), [dts], [dts])
                A(lambda e: e.activation(out=dts.t[:, DT:DT + 8], in_=dts.t[:, DT:DT + 8], func=AF.Exp, scale=-1.0), [dts], [dts])
                A(lambda e: e.activation(out=dts.t[:, DT:DT + 8], in_=dts.t[:, DT:DT + 8], func=AF.Ln, bias=1.0), [dts], [dts])
                V(lambda e: e.scalar_tensor_tensor(out=dts.t[:, DT:DT + 8], in0=dts.t[:, TMP:TMP + 8], scalar=0.0, in1=dts.t[:, DT:DT + 8],
                                                   op0=ALU.max, op1=ALU.add), [dts], [dts])
                V(lambda e: e.tensor_tensor(out=dts.t[:, ADT:ADT + 8], in0=dts.t[:, DT:DT + 8], in1=aneg.t[:], op=ALU.mult), [dts, aneg], [dts])
                PE(lambda e: e.matmul(miscB.t[:, 8:16], lhsT=cst.t[:, CS_TRI2:CS_TRI2 + 128], rhs=dts.t[:, ADT:ADT + 8], start=True, stop=True), [cst, dts], [miscB])
                PE(lambda e: e.matmul(miscB.t[:, 16:24], lhsT=cst.t[:, CS_BONES:CS_BONES + 128], rhs=dts.t[:, ADT:ADT + 8], start=True, stop=True), [cst, dts], [miscB])
                PE(lambda e: e.matmul(miscB.t[:, 24:32], lhsT=cst.t[:, CS_CH0:CS_CH0 + 128], rhs=dts.t[:, ADT:ADT + 8], start=True, stop=True), [cst, dts], [miscB])
                PE(lambda e: e.matmul(miscB.t[:, 32:40], lhsT=cst.t[:, CS_CH1:CS_CH1 + 128], rhs=dts.t[:, ADT:ADT + 8], start=True, stop=True), [cst, dts], [miscB])
                for c in range(2):
                    csl = slice(st * 128 + c * 64, st * 128 + (c + 1) * 64)
                    PE(lambda e, c=c, csl=csl: e.matmul(miscB.t[c * 64:(c + 1) * 64, 64:128], lhsT=BT.t[:, csl], rhs=CT.t[:, csl], start=True, stop=True), [BT, CT], [miscB])
                A(lambda e: e.activation(out=dts.t[:, ACS:ACS + 8], in_=miscB.t[:, 8:16], func=AF.Copy), [miscB], [dts])
                A(lambda e: e.activation(out=dts.t[:, EA:EA + 8], in_=miscB.t[:, 8:16], func=AF.Exp), [miscB], [dts])
                A(lambda e: e.activation(out=cdb.t[:].rearrange("p a b -> p (a b)"), in_=miscB.t[:, 24:40], func=AF.Exp), [miscB], [cdb])
                V(lambda e: e.tensor_tensor(out=dts.t[:, SD:SD + 8], in0=miscB.t[:, 16:24], in1=dts.t[:, ACS:ACS + 8], op=ALU.subtract), [miscB, dts], [dts])
                A(lambda e: e.activation(out=dts.t[:, SD:SD + 8], in_=dts.t[:, SD:SD + 8], func=AF.Exp), [dts], [dts])
                V(lambda e: e.tensor_tensor(out=dts.t[:, SD:SD + 8], in0=dts.t[:, SD:SD + 8], in1=dts.t[:, DT:DT + 8], op=ALU.mult), [dts], [dts])
                V(lambda e: e.tensor_tensor(out=r3(Dm.t[:]), in0=v8(dts.t[:, ADT:ADT + 8]),
                                            in1=cst.t[:, CS_TRIL:CS_TRIL + 64].unsqueeze(1).to_broadcast([128, 8, 64]), op=ALU.mult), [dts, cst], [Dm])
                yield
                PE(lambda e: e.matmul(ARb.t[:, :], lhsT=cst.t[:, CS_BONES:CS_BONES + 128], rhs=Dm.t[:], start=True, stop=False), [cst, Dm], [ARb])
                PE(lambda e: e.matmul(ARb.t[:, :], lhsT=ident_f, rhs=cst.t[:, CS_NEGM:CS_NEGM + 512], start=False, stop=True), [cst], [ARb])
                V(lambda e: e.tensor_tensor(out=r3(LT.t[:]), in0=r3(ARb.t[:, :]), in1=v8(dts.t[:, ACS:ACS + 8]), op=ALU.subtract), [ARb, dts], [LT])
                A(lambda e: e.activation(out=LT.t[:], in_=LT.t[:], func=AF.Exp), [LT], [LT])
                yield
                V(lambda e: e.tensor_tensor(out=r3(LT.t[:]), in0=r3(LT.t[:]), in1=miscB.t[:, 64:128].unsqueeze(1).to_broadcast([128, 8, 64]), op=ALU.mult), [LT, miscB], [LT])
                V(lambda e: e.tensor_tensor(out=r3(LT.t[:]), in0=r3(LT.t[:]), in1=v8(dts.t[:, DT:DT + 8]), op=ALU.mult), [LT, dts], [LT])
                V(lambda e: e.tensor_tensor(out=MTb.t[:], in0=LT.t[:], in1=DI.t[:], op=ALU.add), [LT, DI], [MTb])
                G(lambda e, st=st: e.tensor_tensor(out=r3(Xd.t[:]), in0=r3(Xtm.t[:, st, :]), in1=v8(dts.t[:, SD:SD + 8]), op=ALU.mult), [Xtm, dts], [Xd])
                yield
                for c in range(2):
                    for h in range(8):
                        PE(lambda e, c=c, h=h, st=st: e.matmul(ydB.t[c * 64:(c + 1) * 64, h * 64:(h + 1) * 64],
                                                               lhsT=MTb.t[c * 64:(c + 1) * 64, h * 64:(h + 1) * 64],
                                                               rhs=Xtm.t[c * 64:(c + 1) * 64, st, h * 64:(h + 1) * 64], start=True, stop=True), [MTb, Xtm], [ydB], serial=(c == 1 and h == 0))
                for c in range(2):
                    csl = slice(st * 128 + c * 64, st * 128 + (c + 1) * 64)
                    PE(lambda e, c=c, csl=csl: e.matmul(yoB.t[c * 64:(c + 1) * 64, :], lhsT=CT.t[:, csl], rhs=hb.t[:], start=True, stop=True), [CT, hb], [yoB])
                    PE(lambda e, c=c, st=st: e.matmul(hnB.t[:, :], lhsT=Btm.t[c * 64:(c + 1) * 64, st, :], rhs=Xd.t[c * 64:(c + 1) * 64, :], start=True, stop=True), [Btm, Xd], [hnB])
                    V(lambda e, c=c: e.tensor_tensor(out=r3(hf.t[:]), in0=r3(hf.t[:]), in1=v8(cdb.t[:, c, :]), op=ALU.mult), [hf, cdb], [hf])
                    V(lambda e: e.tensor_tensor(out=hf.t[:], in0=hf.t[:], in1=hnB.t[:, :], op=ALU.add), [hf, hnB], [hf])
                    A(lambda e: e.activation(out=hb.t[:], in_=hf.t[:], func=AF.Copy), [hf], [hb])
                    yield
                V(lambda e: e.tensor_tensor(out=r3(t1.t[:]), in0=r3(yoB.t[:, :]), in1=v8(dts.t[:, EA:EA + 8]), op=ALU.mult), [yoB, dts], [t1])
                V(lambda e: e.tensor_tensor(out=ys.t[:], in0=ydB.t[:, :], in1=t1.t[:], op=ALU.add), [ydB, t1], [ys])
                G(lambda e, st=st: e.tensor_tensor(out=ys.t[:], in0=ys.t[:], in1=sz.t[:, st, :], op=ALU.mult), [ys, sz], [ys])
                A(lambda e: e.activation(out=t1.t[:], in_=ys.t[:], func=AF.Square, accum_out=sm1.t[:, 0:1]), [ys], [t1, sm1])
                V(lambda e: e.tensor_scalar(out=sm1.t[:, 1:2], in0=sm1.t[:, 0:1], scalar1=1.0 / 512, scalar2=GATED_NORM_EPS, op0=ALU.mult, op1=ALU.add), [sm1], [sm1])
                A(lambda e: e.activation(out=sm1.t[:, 2:3], in_=sm1.t[:, 1:2], func=AF.Sqrt), [sm1], [sm1])
                V(lambda e: e.reciprocal(out=sm1.t[:, 3:4], in_=sm1.t[:, 2:3]), [sm1], [sm1])
                stage = ost[scount[0] % 2]
                V(lambda e, stage=stage: e.scalar_tensor_tensor(out=stage.t[:], in0=ys.t[:], scalar=sm1.t[:, 3:4], in1=bc.t[:, BC_NG:BC_NG + 512],
                                                                op0=ALU.mult, op1=ALU.mult), [ys, sm1, bc], [stage])
                store(stage, tile * TT + st * 128, 0)
                scount[0] += 1
                yield

        for it in range(NTILES + 1):
            gp = prep1_gen(it) if it < NTILES else None
            gc = core1_gen(it - 1) if it > 0 else None
            while gp is not None or gc is not None:
                if gc is not None:
                    try:
                        next(gc)
                    except StopIteration:
                        gc = None
                if gp is not None:
                    try:
                        next(gp)
                    except StopIteration:
                        gp = None
        S.barrier()
        es1.close()

    if do_rwkv:
        projB, tp2B, AmB0, AmB1, paB, qgB, yB = banks[1], banks[2], banks[3], banks[4], banks[5], banks[6], banks[7]
        load_weights(dram["w2"], N2)
        Pr = sb("Pr", [128, TT + 1], F32)
        carry = sb("carry", [128, 16], F32)
        dd = sb("dd", [128, TT], F32)
        sh = sb("sh", [128, TT], F32)
        twd = sb("twd", [96, TT], BF16)
        tad = sb("tad", [96, TT], BF16)
        rr = sb("rr", [128, TT], F32)
        kr = sb("kr", [128, TT], F32)
        sg = sb("sg", [128, TT], F32)
        cum = sb("cum", [128, TT], F32)
        Wc = sb("Wc", [128, TT], F32)
        iW = sb("iW", [128, TT], F32)
        Wex = sb("Wex", [128, TT], F32)
        alpha = sb("alpha", [128, TT], F32)
        kkr = sb("kkr", [128, TT], F32)
        sq = sb("sq", [128, TT], F32)
        k2 = sb("k2", [128, TT], F32)
        tmpa = sb("tmpa", [128, TT], F32)
        rkk = sb("rkk", [128, TT], BF16)
        vT = sb("vT", [128, 4, TT], BF16)
        sgT2 = [sb(f"sgT{i}", [128, 2, TT], BF16) for i in range(2)]
        AR2 = [sb(f"AR{i}", [128, 4, NCH, 2, 64], BF16) for i in range(2)]
        BK2 = [sb(f"BK{i}", [128, 4, NCH, 2, 64], BF16) for i in range(2)]
        vTz2 = [sb(f"vTz{i}", [128, 4, NCH, 2, 64], BF16) for i in range(2)]
        Vtm2 = [sb(f"Vtm{i}", [128, NST, 512], BF16) for i in range(2)]
        Wl2 = [sb(f"Wl{i}", [128, 4, NCH], F32) for i in range(2)]
        rks2 = [sb(f"rks{i}", [128, NST, 8], F32) for i in range(2)]
        A_sb = sb("A_sb", [128, 8, 128], BF16)
        Pp = [sb(f"Pp{i}", [64, 8, 64], BF16) for i in range(2)]
        Qp = [sb(f"Qp{i}", [64, 8, 64], BF16) for i in range(2)]
        Gp = [sb(f"Gp{i}", [64, 8, 64], BF16) for i in range(2)]
        BKtok2 = [sb(f"BKtok{i}", [128, 512], BF16) for i in range(2)]
        UV2 = [sb(f"UV{i}", [128, 512], BF16) for i in range(2)]
        Xs = sb("Xs", [64, 512], BF16)
        Sf = sb("Sf", [128, 256], F32)
        Sb_e = sb("Sb_e", [128, 256], BF16)
        Sb_o = sb("Sb_o", [128, 256], BF16)
        ysq = sb("ysq", [128, 512], F32)
        yc = sb("yc", [128, 512], F32)
        bon = ysq
        gst = sb("gst", [128, 64], F32)

        G(lambda e: e.memset(carry.t[:], 0.0), [], [carry])
        for i_ in range(2):
            G(lambda e, i_=i_: e.memset(vTz2[i_].t[:], 0.0), [], [vTz2[i_]])
        G(lambda e: e.memset(Sf.t[:], 0.0), [], [Sf])
        G(lambda e: e.memset(Sb_e.t[:], 0.0), [], [Sb_e])
        G(lambda e: e.memset(Sb_o.t[:], 0.0), [], [Sb_o])

        maskA3 = cst.t[:, CS_MASKA:CS_MASKA + 128].unsqueeze(1).to_broadcast([128, 4, 128])
        maskQ3 = cst.t[0:64, CS_MASKQ:CS_MASKQ + 64].unsqueeze(1).to_broadcast([64, 8, 64])
        identl3 = cst.t[0:64, CS_IDL:CS_IDL + 64].unsqueeze(1).to_broadcast([64, 8, 64])

        def shift_block(bi, col0, M, dest_fn):
            proj_fm(col0, M, projB)
            G(lambda e: e.tensor_copy(out=Pr.t[0:M, 0:1], in_=carry.t[0:M, bi:bi + 1]), [carry], [Pr])
            A(lambda e: e.activation(out=Pr.t[0:M, 1:TT + 1], in_=projB.t[0:M, 0:TT], func=AF.Copy), [projB], [Pr])
            V(lambda e: e.tensor_tensor(out=dd.t[0:M, :], in0=Pr.t[0:M, 0:TT], in1=Pr.t[0:M, 1:TT + 1], op=ALU.subtract), [Pr], [dd])
            G(lambda e: e.tensor_copy(out=carry.t[0:M, bi:bi + 1], in_=Pr.t[0:M, TT:TT + 1]), [Pr], [carry])
            dest_fn()

        def c4(ap):
            return ap.rearrange("p (c l) -> p c l", l=64)

        def prep_gen(tile):
            bs = tile % 2
            sgT, AR, BK, vTz, Vtm, Wl, rks = sgT2[bs], AR2[bs], BK2[bs], vTz2[bs], Vtm2[bs], Wl2[bs], rks2[bs]
            if UTd is not None and do_ssd:
                S.dma("sync", "ld_u", uT.t[:].rearrange("p a b -> p (a b)"), UTd[tile], writes=[uT.b])
            else:
                load_norm_transpose(tile)
            yield

            def d_wd():
                V(lambda e: e.scalar_tensor_tensor(out=sh.t[0:96, :], in0=dd.t[0:96, :], scalar=pv.t[0:96, PV_MU + 0:PV_MU + 1], in1=Pr.t[0:96, 1:TT + 1],
                                                   op0=ALU.mult, op1=ALU.add), [dd, pv, Pr], [sh])
                A(lambda e: e.activation(out=twd.t[:], in_=sh.t[0:96, :], func=AF.Tanh), [sh], [twd])
            shift_block(0, 0, 96, d_wd)
            yield

            def d_ad():
                V(lambda e: e.scalar_tensor_tensor(out=tad.t[:], in0=dd.t[0:96, :], scalar=pv.t[0:96, PV_MU + 1:PV_MU + 2], in1=Pr.t[0:96, 1:TT + 1],
                                                   op0=ALU.mult, op1=ALU.add), [dd, pv, Pr], [tad])
            shift_block(1, 96, 96, d_ad)
            yield
            for gi in range(2):
                def d_gd(gi=gi):
                    V(lambda e: e.scalar_tensor_tensor(out=sh.t[:], in0=dd.t[:], scalar=pv.t[:, PV_MU + 2 + gi:PV_MU + 3 + gi], in1=Pr.t[:, 1:TT + 1],
                                                       op0=ALU.mult, op1=ALU.add), [dd, pv, Pr], [sh])
                    A(lambda e: e.activation(out=sgT.t[:, gi, :], in_=sh.t[:], func=AF.Sigmoid), [sh], [sgT])
                shift_block(2 + gi, 192 + gi * 128, 128, d_gd)
                yield
            tpv2 = tpB.t[:].bitcast(BF16).rearrange("p (s a b) -> p s a b", s=NST, a=4)
            for j in range(4):
                cbase = 448 + j * 384
                mu0 = PV_MU + 4 + j * 3

                def d_r(j=j, mu0=mu0):
                    V(lambda e: e.scalar_tensor_tensor(out=rr.t[:], in0=dd.t[:], scalar=pv.t[:, mu0:mu0 + 1], in1=Pr.t[:, 1:TT + 1],
                                                       op0=ALU.mult, op1=ALU.add), [dd, pv, Pr], [rr])
                shift_block(4 + j * 3, cbase, 128, d_r)
                yield

                def d_k(j=j, mu0=mu0):
                    V(lambda e: e.scalar_tensor_tensor(out=kr.t[:], in0=dd.t[:], scalar=pv.t[:, mu0 + 1:mu0 + 2], in1=Pr.t[:, 1:TT + 1],
                                                       op0=ALU.mult, op1=ALU.add), [dd, pv, Pr], [kr])
                shift_block(5 + j * 3, cbase + 128, 128, d_k)
                yield

                def d_v(j=j, mu0=mu0):
                    V(lambda e: e.scalar_tensor_tensor(out=vT.t[:, j, :], in0=dd.t[:], scalar=pv.t[:, mu0 + 2:mu0 + 3], in1=Pr.t[:, 1:TT + 1],
                                                       op0=ALU.mult, op1=ALU.add), [dd, pv, Pr], [vT])
                    G(lambda e: e.tensor_copy(out=vTz.t[:, j, :, 1, :], in_=vT.t[:, j, :].rearrange("p (c l) -> p c l", l=64)), [vT], [vTz])
                shift_block(6 + j * 3, cbase + 256, 128, d_v)
                yield
                PE(lambda e, j=j: e.matmul(projB.t[:, 0:TT], lhsT=lw2b.t[:, j * 128:(j + 1) * 128], rhs=twd.t[:], start=True, stop=True), [lw2b, twd], [projB])
                A(lambda e, j=j: e.activation(out=sg.t[:], in_=projB.t[:, 0:TT], func=AF.Sigmoid, bias=pv.t[:, PV_W0 + j:PV_W0 + j + 1]), [projB, pv], [sg])
                V(lambda e: e.tensor_tensor_scan(out=cum.t[:], data0=cst.t[:, CS_SCAN:CS_SCAN + TT], data1=sg.t[:], initial=0.0, op0=ALU.mult, op1=ALU.subtract), [cst, sg], [cum])
                A(lambda e: e.activation(out=Wc.t[:], in_=cum.t[:], func=AF.Exp, scale=C0), [cum], [Wc])
                A(lambda e: e.activation(out=iW.t[:], in_=cum.t[:], func=AF.Exp, scale=-C0), [cum], [iW])
                V(lambda e: e.tensor_tensor(out=tmpa.t[:], in0=cum.t[:], in1=sg.t[:], op=ALU.add), [cum, sg], [tmpa])
                A(lambda e: e.activation(out=Wex.t[:], in_=tmpa.t[:], func=AF.Exp, scale=C0), [tmpa], [Wex])
                G(lambda e, j=j: e.tensor_copy(out=Wl.t[:, j, :], in_=c4(Wc.t[:])[:, :, 63]), [Wc], [Wl])
                yield
                PE(lambda e, j=j: e.matmul(projB.t[:, 0:TT], lhsT=la2b.t[:, j * 128:(j + 1) * 128], rhs=tad.t[:], start=True, stop=True), [la2b, tad], [projB])
                A(lambda e, j=j: e.activation(out=alpha.t[:], in_=projB.t[:, 0:TT], func=AF.Sigmoid, bias=pv.t[:, PV_A0 + j:PV_A0 + j + 1]), [projB, pv], [alpha])
                V(lambda e, j=j: e.tensor_scalar(out=kkr.t[:], in0=kr.t[:], scalar1=pv.t[:, PV_KK + j:PV_KK + j + 1], scalar2=None, op0=ALU.mult), [kr, pv], [kkr])
                A(lambda e: e.activation(out=sq.t[:], in_=kkr.t[:], func=AF.Square), [kkr], [sq])
                PE(lambda e: e.matmul(projB.t[:, 0:TT], lhsT=cst.t[:, CS_BONES:CS_BONES + 128], rhs=sq.t[:], start=True, stop=True), [cst, sq], [projB])
                A(lambda e: e.activation(out=sq.t[:], in_=projB.t[:, 0:TT], func=AF.Sqrt), [projB], [sq])
                V(lambda e: e.tensor_scalar(out=sq.t[:], in0=sq.t[:], scalar1=1e-12, scalar2=None, op0=ALU.max), [sq], [sq])
                V(lambda e: e.reciprocal(out=sq.t[:], in_=sq.t[:]), [sq], [sq])
                V(lambda e: e.tensor_tensor(out=kkr.t[:], in0=kkr.t[:], in1=sq.t[:], op=ALU.mult), [kkr, sq], [kkr])
                yield
                V(lambda e, j=j: e.tensor_scalar(out=tmpa.t[:], in0=alpha.t[:], scalar1=pv.t[:, PV_KA + j:PV_KA + j + 1], scalar2=omk.t[:, j:j + 1],
                                                 op0=ALU.mult, op1=ALU.add), [alpha, pv, omk], [tmpa])
                V(lambda e: e.tensor_tensor(out=k2.t[:], in0=kr.t[:], in1=tmpa.t[:], op=ALU.mult), [kr, tmpa], [k2])
                V(lambda e, j=j: e.scalar_tensor_tensor(out=AR.t[:, j, :, 0, :], in0=c4(kkr.t[:]), scalar=-1.0, in1=c4(Wex.t[:]), op0=ALU.mult, op1=ALU.mult), [kkr, Wex], [AR])
                G(lambda e, j=j: e.tensor_tensor(out=AR.t[:, j, :, 1, :], in0=c4(rr.t[:]), in1=c4(Wc.t[:]), op=ALU.mult), [rr, Wc], [AR])
                V(lambda e: e.tensor_tensor(out=tmpa.t[:], in0=kkr.t[:], in1=alpha.t[:], op=ALU.mult), [kkr, alpha], [tmpa])
                V(lambda e, j=j: e.tensor_tensor(out=BK.t[:, j, :, 0, :], in0=c4(tmpa.t[:]), in1=c4(iW.t[:]), op=ALU.mult), [tmpa, iW], [BK])
                G(lambda e, j=j: e.tensor_tensor(out=BK.t[:, j, :, 1, :], in0=c4(k2.t[:]), in1=c4(iW.t[:]), op=ALU.mult), [k2, iW], [BK])
                V(lambda e, j=j: e.scalar_tensor_tensor(out=rkk.t[:], in0=rr.t[:], scalar=pv.t[:, PV_RK + j:PV_RK + j + 1], in1=k2.t[:], op0=ALU.mult, op1=ALU.mult), [rr, pv, k2], [rkk])
                for st in range(NST):
                    PE(lambda e, j=j, st=st: e.matmul(projB.t[:, 256 + st * 8 + j * 2:256 + st * 8 + j * 2 + 2], lhsT=rkk.t[:, st * 128:(st + 1) * 128], rhs=hsel_b,
                                                      start=True, stop=True), [rkk, cb], [projB])
                for st in range(NST):
                    PE(lambda e, j=j, st=st: e.transpose(tpv2[:, st, j, :], vT.t[:, j, st * 128:(st + 1) * 128], ident_b), [vT, cb], [tpB])
                yield
            A(lambda e: e.activation(out=rks.t[:].rearrange("p a b -> p (a b)"), in_=projB.t[:, 256:256 + NST * 8], func=AF.Copy), [projB], [rks])
            A(lambda e: e.activation(out=Vtm.t[:].rearrange("p a b -> p (a b)"), in_=tpB.t[:].bitcast(BF16), func=AF.Copy), [tpB], [Vtm])
            yield

        A_sb2 = [A_sb, sb("A_sb1", [128, 8, 128], BF16)]
        Gp2 = [Gp, [sb(f"Gq{i}", [64, 8, 64], BF16) for i in range(2)]]
        Gfin = {}

        def tchain_gen(tile, c):
            bs = tile % 2
            AR, BK = AR2[bs], BK2[bs]
            par = c % 2
            A_s = A_sb2[par]
            Gq = Gp2[par]
            q3 = qgB.t[0:64, :].rearrange("p (a b) -> p a b", b=64)
            p3 = AmB0.t[0:64, :].rearrange("p (a b) -> p a b", b=64)
            g3 = AmB1.t[0:64, :].rearrange("p (a b) -> p a b", b=64)
            vTz = vTz2[bs]
            BKtok, UV = BKtok2[par], UV2[par]
            tp2 = tp2B.t[:].bitcast(BF16).rearrange("p (a b) -> p a b", b=128)
            for h in range(8):
                j, i = h // 2, h % 2
                bank = AmB0 if i == 0 else AmB1
                PE(lambda e, j=j, i=i, h=h, c=c, bank=bank: e.matmul(bank.t[:, j * 128:(j + 1) * 128],
                                                                     lhsT=BK.t[i * 64:(i + 1) * 64, j, c, :, :].rearrange("p a b -> p (a b)"),
                                                                     rhs=AR.t[i * 64:(i + 1) * 64, j, c, :, :].rearrange("p a b -> p (a b)"),
                                                                     start=True, stop=True), [BK, AR], [bank])
            for hb_, bank in enumerate((AmB0, AmB1)):
                V(lambda e, hb_=hb_, bank=bank: e.tensor_tensor(out=A_s.t[:, hb_::2, :], in0=bank.t[:, :].rearrange("p (a b) -> p a b", b=128),
                                                                in1=maskA3, op=ALU.mult), [bank, cst], [A_s])
            for j in range(4):
                PE(lambda e, j=j, c=c: e.transpose(tp2[:, j, :], BK.t[:, j, c, :, :].rearrange("p a b -> p (a b)"), ident_b), [BK, cb], [tp2B])
                PE(lambda e, j=j, c=c: e.transpose(tp2[:, 4 + j, :], vTz.t[:, j, c, :, :].rearrange("p a b -> p (a b)"), ident_b), [vTz, cb], [tp2B])
            A(lambda e: e.activation(out=BKtok.t[:], in_=tp2B.t[:].bitcast(BF16)[:, 0:512], func=AF.Copy), [tp2B], [BKtok])
            A(lambda e: e.activation(out=UV.t[:, :], in_=tp2B.t[:].bitcast(BF16)[:, 512:1024], func=AF.Copy), [tp2B], [UV])
            yield
            for h in (0, 2, 4, 6, 1, 3, 5, 7):
                j, i = h // 2, h % 2
                PE(lambda e, j=j, i=i, h=h, c=c: e.matmul(q3[:, h, :], lhsT=AR.t[i * 64:(i + 1) * 64, j, c, 0, :], rhs=BK.t[i * 64:(i + 1) * 64, j, c, 0, :],
                                                          start=True, stop=True), [AR, BK], [qgB], serial=(h == 1))
            V(lambda e: e.tensor_tensor(out=Qp[0].t[:], in0=q3, in1=maskQ3, op=ALU.mult), [qgB, cst], [Qp[0]])
            A(lambda e: e.activation(out=Pp[0].t[:], in_=A_s.t[0:64, :, 0:64], func=AF.Copy), [A_s], [Pp[0]])
            G(lambda e: e.tensor_tensor(out=Gq[0].t[:], in0=A_s.t[0:64, :, 0:64], in1=identl3, op=ALU.add), [A_s, cst], [Gq[0]])
            yield
            for h in range(8):
                PE(lambda e, h=h: e.matmul(p3[:, h, :], lhsT=Qp[0].t[:, h, :], rhs=Pp[0].t[:, h, :], start=True, stop=True), [Qp[0], Pp[0]], [AmB0])
            for h in range(8):
                PE(lambda e, h=h: e.matmul(q3[:, h, :], lhsT=Pp[0].t[:, h, :], rhs=Qp[0].t[:, h, :], start=True, stop=True), [Qp[0], Pp[0]], [qgB])
            A(lambda e: e.activation(out=Pp[1].t[:], in_=p3, func=AF.Copy), [AmB0], [Pp[1]])
            V(lambda e: e.tensor_copy(out=Qp[1].t[:], in_=q3), [qgB], [Qp[1]])
            yield
            for l in range(1, 5):
                li, pi = l % 2, (l - 1) % 2
                for h in range(8):
                    PE(lambda e, h=h, li=li, pi=pi: e.matmul(g3[:, h, :], lhsT=Qp[li].t[:, h, :], rhs=Gq[pi].t[:, h, :], start=True, stop=True), [Qp[li], Gq[pi]], [AmB1])
                if l <= 3:
                    for h in range(8):
                        PE(lambda e, h=h, li=li: e.matmul(p3[:, h, :], lhsT=Qp[li].t[:, h, :], rhs=Pp[li].t[:, h, :], start=True, stop=True), [Qp[li], Pp[li]], [AmB0])
                for h in range(8):
                    PE(lambda e, h=h, li=li: e.matmul(q3[:, h, :], lhsT=Pp[li].t[:, h, :], rhs=Qp[li].t[:, h, :], start=True, stop=True), [Qp[li], Pp[li]], [qgB])
                V(lambda e, li=li, pi=pi: e.tensor_tensor(out=Gq[li].t[:], in0=g3, in1=Gq[pi].t[:], op=ALU.add), [AmB1, Gq[pi]], [Gq[li]])
                if l <= 3:
                    A(lambda e, pi=pi: e.activation(out=Pp[pi].t[:], in_=p3, func=AF.Copy), [AmB0], [Pp[pi]])
                V(lambda e, pi=pi: e.tensor_copy(out=Qp[pi].t[:], in_=q3), [qgB], [Qp[pi]])
                yield
            for h in range(8):
                PE(lambda e, h=h: e.matmul(g3[:, h, :], lhsT=Qp[1].t[:, h, :], rhs=Gq[0].t[:, h, :], start=True, stop=True), [Qp[1], Gq[0]], [AmB1])
            V(lambda e: e.tensor_tensor(out=Gq[1].t[:], in0=g3, in1=Gq[0].t[:], op=ALU.add), [AmB1, Gq[0]], [Gq[1]])
            yield
            Gfin[(tile, c)] = Gq[1]

        def state_gen(tile, c):
            bs = tile % 2
            sgT, AR, BK, vTz, Vtm, Wl, rks = sgT2[bs], AR2[bs], BK2[bs], vTz2[bs], Vtm2[bs], Wl2[bs], rks2[bs]
            tp2 = tp2B.t[:].bitcast(BF16).rearrange("p (a b) -> p a b", b=128)
            p3 = paB.t[0:64, :].rearrange("p (a b) -> p a b", b=64)
            A_s = A_sb2[c % 2]
            Gf = Gfin[(tile, c)]
            cp = c % 2
            st = c // 2
            BKtok, UV = BKtok2[c % 2], UV2[c % 2]
            for h in range(8):
                j, i = h // 2, h % 2
                Sm = Sb_e if i == 0 else Sb_o
                PE(lambda e, j=j, h=h, c=c, Sm=Sm: e.matmul(p3[:, h, :], lhsT=AR.t[:, j, c, 0, :], rhs=Sm.t[:, j * 64:(j + 1) * 64],
                                                            start=True, stop=False), [AR, Sm], [paB])
                PE(lambda e, h=h: e.matmul(p3[:, h, :], lhsT=A_s.t[:, h, 0:64], rhs=UV.t[:, h * 64:(h + 1) * 64], start=False, stop=True), [A_s, UV], [paB])
            A(lambda e: e.activation(out=Xs.t[:], in_=paB.t[0:64, :], func=AF.Copy), [paB], [Xs])
            yield
            for h in range(8):
                PE(lambda e, h=h, Gf=Gf: e.matmul(p3[:, h, :], lhsT=Gf.t[:, h, :], rhs=Xs.t[:, h * 64:(h + 1) * 64], start=True, stop=True), [Gf, Xs], [paB])
            V(lambda e: e.tensor_copy(out=UV.t[0:64, :], in_=paB.t[0:64, :]), [paB], [UV])
            yield
            for h in range(8):
                j, i = h // 2, h % 2
                Sm = Sb_e if i == 0 else Sb_o
                PE(lambda e, j=j, h=h, c=c, cp=cp, Sm=Sm: e.matmul(yB.t[cp * 64:(cp + 1) * 64, h * 64:(h + 1) * 64], lhsT=AR.t[:, j, c, 1, :],
                                                                   rhs=Sm.t[:, j * 64:(j + 1) * 64], start=True, stop=False), [AR, Sm], [yB])
                PE(lambda e, h=h, cp=cp: e.matmul(yB.t[cp * 64:(cp + 1) * 64, h * 64:(h + 1) * 64], lhsT=A_s.t[:, h, 64:128], rhs=UV.t[:, h * 64:(h + 1) * 64],
                                                  start=False, stop=True), [A_s, UV], [yB])
            for h in range(8):
                j, i = h // 2, h % 2
                PE(lambda e, j=j, i=i, h=h: e.matmul(paB.t[i * 64:(i + 1) * 64, 256 + j * 64:256 + (j + 1) * 64], lhsT=BKtok.t[:, j * 128 + i * 64:j * 128 + (i + 1) * 64],
                                                     rhs=UV.t[:, h * 64:(h + 1) * 64], start=True, stop=True), [BKtok, UV], [paB])
            V(lambda e: e.tensor_tensor(out=Sf.t[:], in0=Sf.t[:], in1=paB.t[:, 256:512], op=ALU.add), [Sf, paB], [Sf])
            V(lambda e, c=c: e.tensor_tensor(out=Sf.t[:].rearrange("p (a b) -> p a b", b=64), in0=Sf.t[:].rearrange("p (a b) -> p a b", b=64),
                                             in1=Wl.t[:, :, c].unsqueeze(2).to_broadcast([128, 4, 64]), op=ALU.mult), [Sf, Wl], [Sf])
            A(lambda e: e.activation(out=Sb_e.t[0:64, :], in_=Sf.t[0:64, :], func=AF.Copy), [Sf], [Sb_e])
            A(lambda e: e.activation(out=Sb_o.t[64:128, :], in_=Sf.t[64:128, :], func=AF.Copy), [Sf], [Sb_o])
            yield

            if cp == 1:
                y3 = yB.t[:, :].rearrange("p (h v) -> p h v", v=64)

                def g8(col):
                    return gst.t[:, col:col + 8]

                def b8(col):
                    return gst.t[:, col:col + 8].unsqueeze(2).to_broadcast([128, 8, 64])
                V(lambda e: e.tensor_reduce(out=g8(0), in_=y3, axis=AX.X, op=ALU.add), [yB], [gst])
                A(lambda e: e.activation(out=ysq.t[:], in_=yB.t[:, :], func=AF.Square), [yB], [ysq])
                V(lambda e: e.tensor_reduce(out=g8(8), in_=ysq.t[:].rearrange("p (h v) -> p h v", v=64), axis=AX.X, op=ALU.add), [ysq], [gst])
                V(lambda e: e.tensor_scalar(out=g8(16), in0=g8(0), scalar1=1.0 / 64, scalar2=None, op0=ALU.mult), [gst], [gst])
                V(lambda e: e.tensor_tensor(out=g8(24), in0=g8(16), in1=g8(16), op=ALU.mult), [gst], [gst])
                V(lambda e: e.scalar_tensor_tensor(out=g8(32), in0=g8(8), scalar=1.0 / 64, in1=g8(24), op0=ALU.mult, op1=ALU.subtract), [gst], [gst])
                V(lambda e: e.tensor_scalar(out=g8(32), in0=g8(32), scalar1=GN_EPS, scalar2=None, op0=ALU.add), [gst], [gst])
                A(lambda e: e.activation(out=g8(40), in_=g8(32), func=AF.Sqrt), [gst], [gst])
                V(lambda e: e.reciprocal(out=g8(48), in_=g8(40)), [gst], [gst])
                yc3 = yc.t[:].rearrange("p (h v) -> p h v", v=64)
                V(lambda e: e.tensor_tensor(out=yc3, in0=y3, in1=b8(16), op=ALU.subtract), [yB, gst], [yc])
                yield
                V(lambda e: e.tensor_tensor(out=yc3, in0=yc3, in1=b8(48), op=ALU.mult), [yc, gst], [yc])
                G(lambda e: e.tensor_tensor(out=yc.t[:], in0=yc.t[:], in1=bc.t[:, BC_GNW:BC_GNW + 512], op=ALU.mult), [yc, bc], [yc])
                G(lambda e: e.tensor_tensor(out=yc.t[:], in0=yc.t[:], in1=bc.t[:, BC_GNB:BC_GNB + 512], op=ALU.add), [yc, bc], [yc])
                V(lambda e, st=st: e.tensor_tensor(out=bon.t[:].rearrange("p (h v) -> p h v", v=64), in0=Vtm.t[:, st, :].rearrange("p (h v) -> p h v", v=64),
                                                   in1=rks.t[:, st, :].unsqueeze(2).to_broadcast([128, 8, 64]), op=ALU.mult), [Vtm, rks], [bon])
                V(lambda e: e.tensor_tensor(out=yc.t[:], in0=yc.t[:], in1=bon.t[:], op=ALU.add), [yc, bon], [yc])
                for kc2 in range(2):
                    PE(lambda e, kc2=kc2, st=st: e.matmul(paB.t[:, :], lhsT=sgT.t[:, kc2, st * 128:(st + 1) * 128], rhs=lg2b.t[:, kc2, :], start=(kc2 == 0), stop=(kc2 == 1)), [sgT, lg2b], [paB])
                stage = ost[scount[0] % 2]
                V(lambda e, stage=stage: e.tensor_tensor(out=stage.t[:], in0=yc.t[:], in1=paB.t[:, :], op=ALU.mult), [yc, paB], [stage])
                store(stage, tile * TT + st * 128, 512)
                scount[0] += 1
                yield

        def drain(g):
            if g is not None:
                for _ in g:
                    pass

        chunks = [(tile, c) for tile in range(NTILES) for c in range(NCH)]
        drain(prep_gen(0))
        drain(tchain_gen(0, 0))
        pg = None
        pg_tile = -1
        for idx, (tile, c) in enumerate(chunks):
            if c == 0 and tile + 1 < NTILES:
                pg = prep_gen(tile + 1)
                pg_tile = tile + 1
            gs = state_gen(tile, c)
            gt = None
            if idx + 1 < len(chunks):
                nt, ncn = chunks[idx + 1]
                if nt != tile:
                    drain(pg)
                    pg = None
                gt = tchain_gen(nt, ncn)
            def step(g, n=1):
                if g is None:
                    return None
                for _ in range(n):
                    try:
                        next(g)
                    except StopIteration:
                        return None
                return g
            while gs is not None or gt is not None:
                gt = step(gt)
                gs = step(gs)
                pg = step(pg, 2)
                gt = step(gt)
                pg = step(pg, 1)

    if fused:
        return S, es
    for key, ent in S.dma_sems.items():
        if key.startswith("st"):
            S._wait("sync", ("dma", ent[0], ent[1], ("dma", key)))
    return S, es


TP = 512
NSTP = TP // 128
D = 2048
DMIX = 4096
DFF = 5632
NFF = DFF // 128
NWB = 8


def build_p2(nc, S, es, NT2, dram, fused=None):
    NTILES = NT2 // TP

    def sb(name, shape, dt):
        return T(es.enter_context(nc.sbuf_tensor("q_" + name, shape, dt)), name)

    def ps(name):
        return T(es.enter_context(nc.psum_tensor("q_" + name, [128, 512], F32)), name, True)

    def V(fn, r, w): return S.op("vector", fn, [a.b for a in r], [a.b for a in w])
    def A(fn, r, w): return S.op("scalar", fn, [a.b for a in r], [a.b for a in w])
    def G(fn, r, w): return S.op("gpsimd", fn, [a.b for a in r], [a.b for a in w])
    def PE(fn, r, w): return S.op("tensor", fn, [a.b for a in r], [a.b for a in w])

    gv = sb("gv", [128, 32], F32)
    idf = sb("idf", [128, 128], F32)
    idb = sb("idb", [128, 128], BF16)
    onesf = sb("onesf", [128, 128], F32)
    xin = sb("xin", [128, D], F32)
    if fused:
        ymg = [sb(f"ymg{i}", [128, 4096], BF16) for i in range(2)]
        idx = sb("idx", [128, NT2 // 128], mybir.dt.uint32)
        S.dma("sync", "q_c2", idx.t[:], dram["idx"][:, :], writes=[idx.b])
    else:
        ymin = [sb(f"ymin{i}", [128, 1024], F32) for i in range(2)]
        ymb = [sb(f"ymb{i}", [128, 1024], BF16) for i in range(2)]
    ymT = sb("ymT", [128, 32 * TP], BF16)
    hT = sb("hT", [128, 16, TP], F32)
    vT = sb("vT", [128, 16, TP], BF16)
    aT = sb("aT", [128, NFF, TP], BF16)
    sgt = sb("sgt", [128, 4, TP], F32)
    rstd = sb("rstd", [128, TP], F32)
    hsq = [sb(f"hsq{i}", [128, TP], F32) for i in range(2)]
    oT = [sb(f"oT{i}", [128, TP], F32) for i in range(2)]
    wst = [sb(f"wst{i}", [128, 512], F32) for i in range(NWB)]
    wbf = [sb(f"wbf{i}", [128, 512], BF16) for i in range(NWB)]
    acc = [ps(f"acc{i}") for i in range(4)]
    tpP = ps("tpP")
    nrmP = ps("nrmP")

    ymT3 = ymT.t[:].rearrange("p (k t) -> p k t", t=TP)
    ostg = ymT.t[:].bitcast(F32).rearrange("p (s f) -> p s f", f=D)

    S.dma("sync", "q_c0", gv.t[:], dram["gv"][:, :], writes=[gv.b])
    S.dma("sync", "q_c1", idf.t[:], dram["idf"][:, :], writes=[idf.b])
    V(lambda e: e.tensor_copy(out=idb.t[:], in_=idf.t[:]), [idf], [idb])
    G(lambda e: e.memset(onesf.t[:], 1.0), [], [onesf])

    wcount = [0]

    def wload(src_ap):
        i = wcount[0] % NWB
        wcount[0] += 1
        S.dma("sync", f"q_w{i}", wst[i].t[:], src_ap, writes=[wst[i].b])
        m = wcount[0] % 8
        if m in (0, 3, 6):
            A(lambda e, i=i: e.activation(out=wbf[i].t[:], in_=wst[i].t[:], func=AF.Copy), [wst[i]], [wbf[i]])
        elif m == 4:
            G(lambda e, i=i: e.tensor_copy(out=wbf[i].t[:], in_=wst[i].t[:]), [wst[i]], [wbf[i]])
        else:
            V(lambda e, i=i: e.tensor_copy(out=wbf[i].t[:], in_=wst[i].t[:]), [wst[i]], [wbf[i]])
        return wbf[i]

    def rmsnorm_scale(gcol0):
        for fb in range(16):
            hq = hsq[fb % 2]
            A(lambda e, fb=fb, hq=hq: e.activation(out=hq.t[:], in_=hT.t[:, fb, :], func=AF.Square), [hT], [hq])
            PE(lambda e, fb=fb, hq=hq: e.matmul(nrmP.t[:, :], lhsT=onesf.t[:], rhs=hq.t[:], start=(fb == 0), stop=(fb == 15)), [onesf, hq], [nrmP])
        V(lambda e: e.tensor_scalar(out=rstd.t[:], in0=nrmP.t[:, :], scalar1=1.0 / D, scalar2=RMS_EPS, op0=ALU.mult, op1=ALU.add), [nrmP], [rstd])
        A(lambda e: e.activation(out=rstd.t[:], in_=rstd.t[:], func=AF.Sqrt), [rstd], [rstd])
        V(lambda e: e.reciprocal(out=rstd.t[:], in_=rstd.t[:]), [rstd], [rstd])

    ycount = [0]
    for tile in range(NTILES):
        t0 = tile * TP
        for st in range(NSTP):
            tok = t0 + st * 128
            S.dma("sync", "q_x", xin.t[:], dram["xres"][tok:tok + 128, :], writes=[xin.b])
            for g4 in range(4):
                for k in range(4):
                    fb = g4 * 4 + k
                    PE(lambda e, fb=fb, k=k: e.transpose(tpP.t[:, k * 128:(k + 1) * 128], xin.t[:, fb * 128:(fb + 1) * 128], idf.t[:]), [xin, idf], [tpP])
                if g4 % 2 == 0:
                    A(lambda e, g4=g4, st=st: e.activation(out=hT.t[:, g4 * 4:(g4 + 1) * 4, st * 128:(st + 1) * 128],
                                                           in_=tpP.t[:, :].rearrange("p (a b) -> p a b", b=128), func=AF.Copy), [tpP], [hT])
                else:
                    V(lambda e, g4=g4, st=st: e.tensor_copy(out=hT.t[:, g4 * 4:(g4 + 1) * 4, st * 128:(st + 1) * 128],
                                                            in_=tpP.t[:, :].rearrange("p (a b) -> p a b", b=128)), [tpP], [hT])
            if fused:
                gi = ycount[0] % 2
                ycount[0] += 1
                sti = tile * NSTP + st
                slab = sti // 8
                for half, (Gt, gbl) in enumerate(((fused["GS"], fused["gsB"]), (fused["GR"], fused["grB"]))):
                    for r in range(4):
                        c0 = half * 2048 + r * 512
                        S.dma_fn("gpsimd", f"q_g{gi}",
                                 lambda e, gi=gi, c0=c0, Gt=Gt, sti=sti, r=r: e.indirect_dma_start(
                                     out=ymg[gi].t[:, c0:c0 + 512], out_offset=None, in_=Gt[:, :],
                                     in_offset=bass.IndirectOffsetOnAxis(ap=idx.t[:, sti:sti + 1], axis=0),
                                     element_offset=r * 1024 * 512),
                                 reads=[idx.b] + list(gbl), writes=[ymg[gi].b])
            for pc in range(4):
                if fused:
                    src = ymg[gi]
                    cb0 = pc * 1024
                else:
                    i = ycount[0] % 2
                    ycount[0] += 1
                    S.dma("sync", f"q_y{i}", ymin[i].t[:], dram["ymix"][tok:tok + 128, pc * 1024:(pc + 1) * 1024], writes=[ymin[i].b])
                    G(lambda e, i=i: e.tensor_copy(out=ymb[i].t[:], in_=ymin[i].t[:]), [ymin[i]], [ymb[i]])
                    src = ymb[i]
                    cb0 = 0
                tpb = tpP.t[:].bitcast(BF16).rearrange("p (a b) -> p a b", b=128)
                for k in range(8):
                    PE(lambda e, src=src, cb0=cb0, k=k, tpb=tpb: e.transpose(tpb[:, k, :], src.t[:, cb0 + k * 128:cb0 + (k + 1) * 128], idb.t[:]), [src, idb], [tpP])
                V(lambda e, pc=pc, st=st, tpb=tpb: e.tensor_copy(out=ymT3[:, pc * 8:(pc + 1) * 8, st * 128:(st + 1) * 128], in_=tpb), [tpP], [ymT])
        for fg in range(4):
            for kc in range(32):
                wb = wload(dram["w_out"][kc * 128:(kc + 1) * 128, fg * 512:(fg + 1) * 512])
                for fi in range(4):
                    PE(lambda e, wb=wb, fi=fi, kc=kc: e.matmul(acc[fi].t[:, :], lhsT=wb.t[:, fi * 128:(fi + 1) * 128], rhs=ymT3[:, kc, :],
                                                               start=(kc == 0), stop=(kc == 31)), [wb, ymT], [acc[fi]])
            for fi in range(4):
                V(lambda e, fi=fi, fg=fg: e.tensor_tensor(out=hT.t[:, fg * 4 + fi, :], in0=hT.t[:, fg * 4 + fi, :], in1=acc[fi].t[:, :], op=ALU.add), [hT, acc[fi]], [hT])
        rmsnorm_scale(0)
        for fb in range(16):
            V(lambda e, fb=fb: e.scalar_tensor_tensor(out=vT.t[:, fb, :], in0=hT.t[:, fb, :], scalar=gv.t[:, fb:fb + 1], in1=rstd.t[:],
                                                      op0=ALU.mult, op1=ALU.mult), [hT, gv, rstd], [vT])
        for gg in range(NFF // 4):
            for kc in range(16):
                wb = wload(dram["w_gate"][kc * 128:(kc + 1) * 128, gg * 512:(gg + 1) * 512])
                for fi in range(4):
                    PE(lambda e, wb=wb, fi=fi, kc=kc: e.matmul(acc[fi].t[:, :], lhsT=wb.t[:, fi * 128:(fi + 1) * 128], rhs=vT.t[:, kc, :],
                                                               start=(kc == 0), stop=(kc == 15)), [wb, vT], [acc[fi]])
            for fi in range(4):
                A(lambda e, fi=fi: e.activation(out=sgt.t[:, fi, :], in_=acc[fi].t[:, :], func=AF.Silu), [acc[fi]], [sgt])
            for kc in range(16):
                wb = wload(dram["w_up"][kc * 128:(kc + 1) * 128, gg * 512:(gg + 1) * 512])
                for fi in range(4):
                    PE(lambda e, wb=wb, fi=fi, kc=kc: e.matmul(acc[fi].t[:, :], lhsT=wb.t[:, fi * 128:(fi + 1) * 128], rhs=vT.t[:, kc, :],
                                                               start=(kc == 0), stop=(kc == 15)), [wb, vT], [acc[fi]])
            for fi in range(4):
                V(lambda e, fi=fi, gg=gg: e.tensor_tensor(out=aT.t[:, gg * 4 + fi, :], in0=sgt.t[:, fi, :], in1=acc[fi].t[:, :], op=ALU.mult), [sgt, acc[fi]], [aT])
        for fg in range(4):
            for kc in range(NFF):
                wb = wload(dram["w_down"][kc * 128:(kc + 1) * 128, fg * 512:(fg + 1) * 512])
                for fi in range(4):
                    PE(lambda e, wb=wb, fi=fi, kc=kc: e.matmul(acc[fi].t[:, :], lhsT=wb.t[:, fi * 128:(fi + 1) * 128], rhs=aT.t[:, kc, :],
                                                               start=(kc == 0), stop=(kc == NFF - 1)), [wb, aT], [acc[fi]])
            for fi in range(4):
                V(lambda e, fi=fi, fg=fg: e.tensor_tensor(out=hT.t[:, fg * 4 + fi, :], in0=hT.t[:, fg * 4 + fi, :], in1=acc[fi].t[:, :], op=ALU.add), [hT, acc[fi]], [hT])
        rmsnorm_scale(16)
        for fb in range(16):
            o = oT[fb % 2]
            V(lambda e, fb=fb, o=o: e.scalar_tensor_tensor(out=o.t[:], in0=hT.t[:, fb, :], scalar=gv.t[:, 16 + fb:17 + fb], in1=rstd.t[:],
                                                           op0=ALU.mult, op1=ALU.mult), [hT, gv, rstd], [o])
            for st in range(NSTP):
                PE(lambda e, st=st, o=o: e.transpose(tpP.t[:, st * 128:(st + 1) * 128], o.t[:, st * 128:(st + 1) * 128], idf.t[:]), [o, idf], [tpP])
            A(lambda e, fb=fb: e.activation(out=ostg[:, :, fb * 128:(fb + 1) * 128], in_=tpP.t[:, :].rearrange("p (a b) -> p a b", b=128), func=AF.Copy), [tpP], [ymT])
        for st in range(NSTP):
            tok = t0 + st * 128
            S.dma("sync", f"q_o{st}", dram["out"][tok:tok + 128, :], ostg[:, st, :], reads=[ymT.b], writes=[])
    for key, ent in S.dma_sems.items():
        if key.startswith("q_o"):
            S._wait("sync", ("dma", ent[0], ent[1], ("dma", key)))


def prep_core_p1(inp, b, q, NT):
    f = np.float32
    w_in = inp["w_in"][0]
    d = {}
    d["x"] = np.ascontiguousarray(inp["x"][b, :NT, :])
    zc = w_in[:, q * 512:(q + 1) * 512]
    dtc = w_in[:, 5120 + q * 8:5120 + (q + 1) * 8]
    xsc = w_in[:, 2048 + q * 512:2048 + (q + 1) * 512]
    Bc = w_in[:, 4096 + q * 128:4096 + (q + 1) * 128]
    Cc = w_in[:, 4608 + q * 128:4608 + (q + 1) * 128]
    d["w1"] = np.ascontiguousarray(np.concatenate([zc, dtc, xsc, Bc, Cc], axis=1))
    rw = w_in[:, 5152:]
    cols = [rw[:, 6144:6240], rw[:, 6240:6336], rw[:, 6336:6592]]
    for j in range(4):
        o = q * 512 + j * 128
        cols += [rw[:, o:o + 128], rw[:, 2048 + o:2048 + o + 128], rw[:, 4096 + o:4096 + o + 128]]
    d["w2"] = np.ascontiguousarray(np.concatenate(cols, axis=1))
    assert d["w1"].shape[1] == N1 and d["w2"].shape[1] == N2
    pv = np.zeros((128, NPV), f)
    cw = inp["ssd_conv_w"][0]; cbias = inp["ssd_conv_b"][0]
    chs = [q * 512 + blk * 128 for blk in range(4)] + [2048 + q * 128, 2560 + q * 128]
    for blk, ch in enumerate(chs):
        for j in range(4):
            pv[:, PV_CW + blk * 4 + j] = cw[j, ch:ch + 128]
        pv[:, PV_CB + blk] = cbias[ch:ch + 128]
    mu = inp["rwkv_mu"][0]
    pv[:96, PV_MU + 0] = mu[6144:6240]
    pv[:96, PV_MU + 1] = mu[6240:6336]
    pv[:, PV_MU + 2] = mu[6336:6464]
    pv[:, PV_MU + 3] = mu[6464:6592]
    for j in range(4):
        o = q * 512 + j * 128
        pv[:, PV_MU + 4 + j * 3 + 0] = mu[o:o + 128]
        pv[:, PV_MU + 4 + j * 3 + 1] = mu[2048 + o:2048 + o + 128]
        pv[:, PV_MU + 4 + j * 3 + 2] = mu[4096 + o:4096 + o + 128]
        pv[:, PV_W0 + j] = inp["rwkv_w0"][0][o:o + 128]
        pv[:, PV_A0 + j] = inp["rwkv_a0"][0][o:o + 128]
        pv[:, PV_KK + j] = inp["rwkv_k_k"][0][o:o + 128]
        pv[:, PV_KA + j] = inp["rwkv_k_a"][0][o:o + 128]
        pv[:, PV_RK + j] = inp["rwkv_r_k"][0][o:o + 128]
    pv[:, PV_G1:PV_G1 + 16] = inp["norm1_g"][0].reshape(16, 128).T
    d["pv"] = pv
    bc = np.zeros((128, NBC), f)
    bc[:, BC_NG:BC_NG + 512] = inp["ssd_norm_g"][0][q * 512:(q + 1) * 512][None]
    bc[:, BC_GNW:BC_GNW + 512] = inp["rwkv_gn_w"][0][q * 512:(q + 1) * 512][None]
    bc[:, BC_GNB:BC_GNB + 512] = inp["rwkv_gn_b"][0][q * 512:(q + 1) * 512][None]
    bc[:, BC_DTB:BC_DTB + 8] = inp["ssd_dt_bias"][0][q * 8:(q + 1) * 8][None]
    bc[:, BC_ALOG:BC_ALOG + 8] = inp["ssd_A_log"][0][q * 8:(q + 1) * 8][None]
    bc[:, BC_D:BC_D + 8] = inp["ssd_D"][0][q * 8:(q + 1) * 8][None]
    d["bc"] = bc
    d["cst"] = make_consts()
    d["lw2"] = np.ascontiguousarray(inp["rwkv_w2"][0][:, q * 512:(q + 1) * 512])
    d["la2"] = np.ascontiguousarray(inp["rwkv_a2"][0][:, q * 512:(q + 1) * 512])
    d["lg2"] = np.ascontiguousarray(inp["rwkv_g2"][0][:, q * 512:(q + 1) * 512])
    return d

from concourse.bass_utils import run_bass_kernel_spmd

SEQ = 8192
NCORES = 8


def _build_prog1(NT):
    nc = bass.Bass("TRN2", target_bir_lowering=False)
    dram = {}

    def din(name, shape):
        dram[name] = nc.dram_tensor(name, list(shape), F32, kind="ExternalInput").ap()
    din("x", [NT, 2048]); din("w1", [2048, N1]); din("w2", [2048, N2]); din("pv", [128, NPV]); din("bc", [128, NBC])
    din("cst", [128, NCS]); din("lw2", [96, 512]); din("la2", [96, 512]); din("lg2", [256, 512])
    dram["ymix"] = nc.dram_tensor("ymix", [NT, 1024], F32, kind="ExternalOutput").ap()
    S, es = build_p1(nc, NT, dram)
    with nc.Block() as block:
        S.emit(block)
    es.close()
    return nc


def _build_prog2(NT2):
    nc = bass.Bass("TRN2", target_bir_lowering=False)
    dram = {}

    def din(name, shape):
        dram[name] = nc.dram_tensor(name, list(shape), F32, kind="ExternalInput").ap()
    din("xres", [NT2, 2048]); din("ymix", [NT2, 4096]); din("w_out", [4096, 2048]); din("w_gate", [2048, 5632])
    din("w_up", [2048, 5632]); din("w_down", [5632, 2048]); din("gv", [128, 32]); din("idf", [128, 128])
    dram["out"] = nc.dram_tensor("out", [NT2, 2048], F32, kind="ExternalOutput").ap()
    es = ExitStack()
    S = Sched(nc, es)
    build_p2(nc, S, es, NT2, dram)
    with nc.Block() as block:
        S.emit(block)
    es.close()
    return nc


def _build_fused(NT, NT2):
    nc = bass.Bass("TRN2", target_bir_lowering=False)
    dram = {}

    def din(name, shape, dt=F32):
        dram[name] = nc.dram_tensor(name, list(shape), dt, kind="ExternalInput").ap()
    din("x", [NT, 2048]); din("w1", [2048, N1]); din("w2", [2048, N2]); din("pv", [128, NPV]); din("bc", [128, NBC])
    din("cst", [128, NCS]); din("lw2", [96, 512]); din("la2", [96, 512]); din("lg2", [256, 512])
    din("xres", [NT2, 2048]); din("w_out", [4096, 2048]); din("w_gate", [2048, 5632])
    din("w_up", [2048, 5632]); din("w_down", [5632, 2048]); din("gv", [128, 32]); din("idf", [128, 128])
    din("idx", [128, NT2 // 128], mybir.dt.uint32)
    dram["out"] = nc.dram_tensor("out", [NT2, 2048], F32, kind="ExternalOutput").ap()
    nslab = NT // 1024
    fused = {
        "YS": nc.dram_tensor("YS", [NT, 512], BF16), "YR": nc.dram_tensor("YR", [NT, 512], BF16),
        "GS": nc.dram_tensor("GS", [4 * NT, 512], BF16), "GR": nc.dram_tensor("GR", [4 * NT, 512], BF16),
        "UT": nc.dram_tensor("UT", [NT // TT, 128, 16 * TT], BF16),
        "gsB": [Buf(f"gs{k}") for k in range(nslab)], "grB": [Buf(f"gr{k}") for k in range(nslab)],
        "groups": [[0, 1, 2, 3], [4, 5, 6, 7]],
    }
    ses = ExitStack()
    S = Sched(nc, ses)
    es1 = ExitStack()
    build_p1(nc, NT, dram, S=S, es=es1, fused=fused)
    S.barrier()
    es1.close()
    es2 = ExitStack()
    build_p2(nc, S, es2, NT2, dram, fused=fused)
    with nc.Block() as block:
        S.emit(block)
    es2.close()
    ses.close()
    return nc


def kernel(**inp):
    inp = {k: np.asarray(v) for k, v in inp.items()}
    B = inp["x"].shape[0]
    NT = inp["x"].shape[1]
    NT2 = B * NT // NCORES
    nc = _build_fused(NT, NT2)
    gv = np.ascontiguousarray(np.concatenate([inp["norm2_g"][0].reshape(16, 128).T, inp["norm_f_g"].reshape(16, 128).T], axis=1).astype(np.float32))
    idf = np.eye(128, dtype=np.float32)
    shared = {"w_out": np.ascontiguousarray(inp["w_out"][0]), "w_gate": np.ascontiguousarray(inp["w_gate"][0]),
              "w_up": np.ascontiguousarray(inp["w_up"][0]), "w_down": np.ascontiguousarray(inp["w_down"][0]),
              "gv": gv, "idf": idf}
    maps = []
    for c in range(NCORES):
        b, j = c // 4, c % 4
        m = prep_core_p1(inp, b, j, NT)
        m.update(shared)
        m["xres"] = np.ascontiguousarray(inp["x"][b, j * NT2:(j + 1) * NT2, :])
        g = j * NT2 + np.arange(NT2 // 128)[None, :] * 128 + np.arange(128)[:, None]
        m["idx"] = ((g // 1024) * 4096 + (g % 1024)).astype(np.uint32)
        maps.append(m)
    res = run_bass_kernel_spmd(nc, maps, core_ids=list(range(NCORES)))
    out = np.stack([np.concatenate([res.results[b * 4 + j]["out"] for j in range(4)], axis=0) for b in range(B)], axis=0)
    return out.astype(np.float32)
```

```python
from contextlib import ExitStack
import numpy as np
import concourse.bass as bass
import concourse.mybir as mybir

EPOCH = 12000


class Buf:
    __slots__ = ("name", "w", "r", "excl")

    def __init__(self, name, excl=False):
        self.name = name
        self.excl = excl
        self.w = None
        self.r = []


class Sched:
    ENG = ("tensor", "vector", "scalar", "gpsimd", "sync")

    def __init__(self, nc, sem_ctx):
        self.nc = nc
        self.sem_ctx = sem_ctx
        self.count = {e: 0 for e in self.ENG}
        self.sems = {e: [] for e in self.ENG}
        self.waited = {e: {} for e in self.ENG}
        self.prog = {e: [] for e in self.ENG}
        self.dma_sems = {}
        self.cc_tags = []
        self.nwaits = 0

    def _sem(self, eng, k):
        idx = (k - 1) // EPOCH
        lst = self.sems[eng]
        while len(lst) <= idx:
            lst.append(self.sem_ctx.enter_context(self.nc.semaphore(f"s_{eng}_{len(lst)}")))
        return lst[idx], (k - 1) % EPOCH + 1, (eng, idx)

    def _wait(self, eng, dep):
        if dep is None:
            return
        if dep[0] == "dma":
            _, sem, val, key = dep
        else:
            sem, val, key = self._sem(dep[0], dep[1])
        w = self.waited[eng]
        if w.get(key, 0) >= val:
            return
        w[key] = val
        self.nwaits += 1
        self.prog[eng].append(lambda e, sem=sem, val=val: e.wait_ge(sem, val))

    def op(self, eng, fn, reads=(), writes=(), serial=False):
        deps = []
        if serial and self.count[eng] > 0:
            self._wait(eng, (eng, self.count[eng]))
        for b in reads:
            if b.w is not None:
                deps.append(b.w)
            if b.excl:
                deps.extend(r for r in b.r if r[0] != eng)
        for b in writes:
            if b.w is not None:
                deps.append(b.w)
            deps.extend(b.r)
        for d in deps:
            if d[0] == eng and d[0] != "dma" and eng == "tensor":
                continue
            self._wait(eng, d)
        self.count[eng] += 1
        k = self.count[eng]
        sem, val, _ = self._sem(eng, k)
        self.prog[eng].append(lambda e, fn=fn, sem=sem: fn(e).then_inc(sem, 1))
        tag = (eng, k)
        for b in reads:
            b.r.append(tag)
        for b in writes:
            b.w = tag
            b.r = []
        return tag

    def dma(self, eng, key, out, in_, reads=(), writes=(), **kw):
        if key not in self.dma_sems:
            self.dma_sems[key] = [self.sem_ctx.enter_context(self.nc.semaphore(f"d_{key}")), 0]
        ent = self.dma_sems[key]
        deps = []
        for b in reads:
            if b.w is not None:
                deps.append(b.w)
        for b in writes:
            if b.w is not None and not (b.w[0] == "dma" and b.w[3] == ("dma", key)):
                deps.append(b.w)
            deps.extend(b.r)
        for d in deps:
            self._wait(eng, d)
        ent[1] += 16
        sem, val = ent[0], ent[1]
        self.prog[eng].append(lambda e, sem=sem, out=out, in_=in_, kw=kw: e.dma_start(out=out, in_=in_, **kw).then_inc(sem, 16))
        tag = ("dma", sem, val, ("dma", key))
        for b in reads:
            b.r.append(tag)
        for b in writes:
            b.w = tag
            b.r = []
        return tag

    def dma_fn(self, eng, key, fn, reads=(), writes=()):
        if key not in self.dma_sems:
            self.dma_sems[key] = [self.sem_ctx.enter_context(self.nc.semaphore(f"d_{key}")), 0]
        ent = self.dma_sems[key]
        deps = []
        for b in reads:
            if b.w is not None:
                deps.append(b.w)
        for b in writes:
            if b.w is not None and not (b.w[0] == "dma" and b.w[3] == ("dma", key)):
                deps.append(b.w)
            deps.extend(b.r)
        for d in deps:
            self._wait(eng, d)
        ent[1] += 16
        sem, val = ent[0], ent[1]
        self.prog[eng].append(lambda e, sem=sem, fn=fn: fn(e).then_inc(sem, 16))
        tag = ("dma", sem, val, ("dma", key))
        for b in reads:
            b.r.append(tag)
        for b in writes:
            b.w = tag
            b.r = []
        return tag

    def collective(self, name, fn, reads=(), writes=()):
        sem = self.sem_ctx.enter_context(self.nc.semaphore(f"cc_{name}"))
        deps = []
        for b in reads:
            if b.w is not None:
                deps.append(b.w)
        for b in writes:
            if b.w is not None:
                deps.append(b.w)
            deps.extend(b.r)
        for d in deps:
            self._wait("gpsimd", d)
        self.prog["gpsimd"].append(lambda e, sem=sem, fn=fn: fn(e).then_inc(sem))
        tag = ("dma", sem, 1, ("cc", name))
        self.cc_tags.append(tag)
        for b in reads:
            b.r.append(tag)
        for b in writes:
            b.w = tag
            b.r = []
        return tag

    def barrier(self):
        for e in self.ENG:
            for o in self.ENG:
                if o != e and self.count[o] > 0:
                    self._wait(e, (o, self.count[o]))
            for key, ent in self.dma_sems.items():
                if ent[1] > 0:
                    self._wait(e, ("dma", ent[0], ent[1], ("dma", key)))
            for tag in self.cc_tags:
                self._wait(e, tag)

    def wait_all(self, eng, bufs):
        for b in bufs:
            if b.w is not None:
                self._wait(eng, b.w)
            for d in b.r:
                self._wait(eng, d)

    def emit(self, block):
        def mk(eng):
            def body(e):
                for f in self.prog[eng]:
                    f(e)
            return body
        block.tensor(mk("tensor"))
        block.vector(mk("vector"))
        block.scalar(mk("scalar"))
        block.gpsimd(mk("gpsimd"))
        block.sync(mk("sync"))


F32 = mybir.dt.float32
BF16 = mybir.dt.bfloat16
AF = mybir.ActivationFunctionType
ALU = mybir.AluOpType
AX = mybir.AxisListType

TT = 256
NCH = TT // 64
NST = TT // 128
C0 = float(np.exp(-0.5))
RMS_EPS = 1e-6
GATED_NORM_EPS = 1e-5
GN_EPS = 64e-5
NEG = -30000.0

N1 = 1288
N2 = 1984
PV_CW = 0
PV_CB = 24
PV_MU = 30
PV_W0 = 46
PV_A0 = 50
PV_KK = 54
PV_KA = 58
PV_RK = 62
PV_G1 = 66
NPV = 82
BC_NG = 0
BC_GNW = 512
BC_GNB = 1024
BC_DTB = 1536
BC_ALOG = 1544
BC_D = 1552
NBC = 1560
CS_ID = 0
CS_TRI2 = 128
CS_TRIL = 256
CS_BONES = 320
CS_NEGM = 448
CS_CH0 = 960
CS_CH1 = 1088
CS_MASKA = 1216
CS_MASKQ = 1344
CS_SCAN = 1408
CS_IDL = 1664
CS_HSEL = 1728
NCS = 1730


def make_consts():
    c = np.zeros((128, NCS), np.float32)
    p = np.arange(128)
    pl = p % 64
    c[:, CS_ID:CS_ID + 128] = np.eye(128)
    c[:, CS_TRI2:CS_TRI2 + 128] = ((p[:, None] // 64 == p[None, :] // 64) & (p[:, None] <= p[None, :]))
    l64 = np.arange(64)
    c[:, CS_TRIL:CS_TRIL + 64] = (pl[:, None] <= l64[None, :])
    c[:, CS_BONES:CS_BONES + 128] = (p[:, None] // 64 == p[None, :] // 64)
    nm = np.where(l64[None, :] < pl[:, None], NEG, 0.0)
    c[:, CS_NEGM:CS_NEGM + 512] = np.tile(nm, (1, 8))
    c[:, CS_CH0:CS_CH0 + 128] = (p[:, None] < 64)
    c[:, CS_CH1:CS_CH1 + 128] = (p[:, None] >= 64)
    c[:, CS_MASKA:CS_MASKA + 64] = (l64[None, :] > pl[:, None])
    c[:, CS_MASKA + 64:CS_MASKA + 128] = (l64[None, :] >= pl[:, None])
    c[:, CS_MASKQ:CS_MASKQ + 64] = (pl[:, None] > l64[None, :])
    sm = np.ones(256); sm[::64] = 0
    c[:, CS_SCAN:CS_SCAN + 256] = sm[None, :]
    c[:, CS_IDL:CS_IDL + 64] = (pl[:, None] == l64[None, :])
    c[:, CS_HSEL] = (p < 64)
    c[:, CS_HSEL + 1] = (p >= 64)
    return c


class T:
    def __init__(self, t, name, excl=False):
        self.t = t
        self.b = Buf(name, excl)


class _Stop(Exception):
    pass


def build_p1(nc, NT, dram, do_ssd=True, do_rwkv=True, dbg=99, S=None, es=None, fused=None):
    NTILES = NT // TT
    if S is None:
        es = ExitStack()
        S = Sched(nc, es)

    def sb(name, shape, dt):
        return T(es.enter_context(nc.sbuf_tensor("s_" + name, shape, dt)), name)

    def ps(name):
        return T(es.enter_context(nc.psum_tensor(name, [128, 512], F32)), name, True)

    def V(fn, r, w): return S.op("vector", fn, [a.b for a in r], [a.b for a in w])
    def A(fn, r, w): return S.op("scalar", fn, [a.b for a in r], [a.b for a in w])
    def G(fn, r, w): return S.op("gpsimd", fn, [a.b for a in r], [a.b for a in w])
    def PE(fn, r, w, serial=False): return S.op("tensor", fn, [a.b for a in r], [a.b for a in w], serial=serial)

    W = sb("W", [128, 16, N2], BF16)
    pv = sb("pv", [128, NPV], F32)
    bc = sb("bc", [128, NBC], F32)
    cst = sb("cst", [128, NCS], F32)
    cb = sb("cb", [128, 1216], BF16)
    omk = sb("omk", [128, 4], F32)
    aneg = sb("aneg", [128, 8], F32)
    DI = sb("DI", [128, 512], F32)
    lw2b = sb("lw2b", [96, 512], BF16)
    la2b = sb("la2b", [96, 512], BF16)
    lg2b = sb("lg2b", [128, 2, 512], BF16)
    xt = [sb(f"xt{i}", [128, 2048], F32) for i in range(2)]
    xn = sb("xn", [128, 2048], BF16)
    uT = sb("uT", [128, 16, TT], BF16)
    sm = sb("sm", [128, 64], F32)
    ost = [sb(f"ost{i}", [128, 512], BF16 if fused else F32) for i in range(2)]

    banks = [ps(f"bank{i}") for i in range(8)]
    tpB = banks[0]

    ident_f = cst.t[:, CS_ID:CS_ID + 128]
    ident_b = cb.t[:, 0:128]

    S.dma("sync", "ld_c0", pv.t[:], dram["pv"][:, :], writes=[pv.b])
    S.dma("sync", "ld_c1", bc.t[:], dram["bc"][:, :], writes=[bc.b])
    S.dma("sync", "ld_c2", cst.t[:], dram["cst"][:, :], writes=[cst.b])
    S.dma("gpsimd", "ld_l0", lw2b.t[:], dram["lw2"][:, :], writes=[lw2b.b])
    S.dma("gpsimd", "ld_l1", la2b.t[:], dram["la2"][:, :], writes=[la2b.b])
    S.dma("gpsimd", "ld_l2", lg2b.t[:], dram["lg2"].rearrange("(c p) n -> p c n", p=128), writes=[lg2b.b])
    V(lambda e: e.tensor_copy(out=cb.t[:, 0:128], in_=cst.t[:, CS_ID:CS_ID + 128]), [cst], [cb])
    V(lambda e: e.tensor_copy(out=cb.t[:, 128:130], in_=cst.t[:, CS_HSEL:CS_HSEL + 2]), [cst], [cb])
    hsel_b = cb.t[:, 128:130]
    V(lambda e: e.tensor_scalar(out=omk.t[:], in0=pv.t[:, PV_KA:PV_KA + 4], scalar1=-1.0, scalar2=1.0, op0=ALU.mult, op1=ALU.add), [pv], [omk])
    A(lambda e: e.activation(out=aneg.t[:], in_=bc.t[:, BC_ALOG:BC_ALOG + 8], func=AF.Exp), [bc], [aneg])
    V(lambda e: e.tensor_scalar(out=aneg.t[:], in0=aneg.t[:], scalar1=-1.0, scalar2=None, op0=ALU.mult), [aneg], [aneg])
    V(lambda e: e.tensor_tensor(out=DI.t[:].rearrange("p (e l) -> p e l", l=64),
                                in0=cst.t[:, CS_IDL:CS_IDL + 64].unsqueeze(1).to_broadcast([128, 8, 64]),
                                in1=bc.t[:, BC_D:BC_D + 8].unsqueeze(2).to_broadcast([128, 8, 64]), op=ALU.mult), [cst, bc], [DI])

    def load_weights(wdram, ncols):
        tmp = ExitStack()
        wstg = [T(tmp.enter_context(nc.sbuf_tensor(f"s_wstg{i}_{ncols}", [128, ncols], F32)), f"wstg{i}") for i in range(3)]
        wv = wdram.rearrange("(c p) n -> p c n", p=128)
        for kc in range(16):
            stg = wstg[kc % 3]
            S.dma("sync", f"ld_w{kc % 3}", stg.t[:, 0:ncols], wv[:, kc, :], writes=[stg.b])
            gcol = pv.t[:, PV_G1 + kc:PV_G1 + kc + 1]
            m = kc % 3
            if m == 0:
                V(lambda e, kc=kc, stg=stg, gcol=gcol: e.tensor_scalar(out=W.t[:, kc, 0:ncols], in0=stg.t[:, 0:ncols], scalar1=gcol, scalar2=None, op0=ALU.mult), [stg, pv], [W])
            elif m == 1:
                A(lambda e, kc=kc, stg=stg, gcol=gcol: e.activation(out=W.t[:, kc, 0:ncols], in_=stg.t[:, 0:ncols], func=AF.Copy, scale=gcol), [stg, pv], [W])
            else:
                G(lambda e, kc=kc, stg=stg, gcol=gcol: e.tensor_scalar(out=W.t[:, kc, 0:ncols], in0=stg.t[:, 0:ncols], scalar1=gcol, scalar2=None, op0=ALU.mult), [stg, pv], [W])
        S.barrier()
        tmp.close()

    xcount = [0]

    def load_norm_transpose(tile):
        for st in range(NST):
            slot = xcount[0] % 2
            xcount[0] += 1
            X = xt[slot]
            tok0 = tile * TT + st * 128
            S.dma("sync", f"ldx{slot}", X.t[:], dram["x"][tok0:tok0 + 128, :], writes=[X.b])
            A(lambda e, X=X: e.activation(out=xn.t[:], in_=X.t[:], func=AF.Square, accum_out=sm.t[:, 0:1]), [X], [xn, sm])
            V(lambda e: e.tensor_scalar(out=sm.t[:, 1:2], in0=sm.t[:, 0:1], scalar1=1.0 / 2048, scalar2=RMS_EPS, op0=ALU.mult, op1=ALU.add), [sm], [sm])
            A(lambda e: e.activation(out=sm.t[:, 2:3], in_=sm.t[:, 1:2], func=AF.Sqrt), [sm], [sm])
            V(lambda e: e.reciprocal(out=sm.t[:, 3:4], in_=sm.t[:, 2:3]), [sm], [sm])
            A(lambda e, X=X: e.activation(out=xn.t[:], in_=X.t[:], func=AF.Copy, scale=sm.t[:, 3:4]), [X, sm], [xn])
            tpv = tpB.t[:].bitcast(BF16).rearrange("p (a b) -> p a b", b=128)
            for half in range(2):
                for k8 in range(8):
                    kc = half * 8 + k8
                    PE(lambda e, kc=kc, k8=k8: e.transpose(tpv[:, k8, :], xn.t[:, kc * 128:(kc + 1) * 128], ident_b), [xn, cb], [tpB])
                eng = V if half == 0 else A
                if half == 0:
                    V(lambda e, half=half, st=st: e.tensor_copy(out=uT.t[:, half * 8:(half + 1) * 8, st * 128:(st + 1) * 128], in_=tpv), [tpB], [uT])
                else:
                    A(lambda e, half=half, st=st: e.activation(out=uT.t[:, half * 8:(half + 1) * 8, st * 128:(st + 1) * 128], in_=tpv, func=AF.Copy), [tpB], [uT])

    def proj_fm(col0, M, out_ps):
        for kc in range(16):
            PE(lambda e, kc=kc: e.matmul(out_ps.t[0:M, 0:TT], lhsT=W.t[:, kc, col0:col0 + M], rhs=uT.t[:, kc, :],
                                         start=(kc == 0), stop=(kc == 15)), [W, uT], [out_ps])

    scount = [0]
    UTd = fused["UT"] if fused else None

    slab_tags = []

    def store(stage, tok0, col0):
        if not fused:
            S.dma("sync", f"st{scount[0] % 2}", dram["ymix"][tok0:tok0 + 128, col0:col0 + 512], stage.t[:], reads=[stage.b], writes=[])
            return
        Y = fused["YS"] if col0 == 0 else fused["YR"]
        Gt = fused["GS"] if col0 == 0 else fused["GR"]
        gb = fused["gsB"] if col0 == 0 else fused["grB"]
        tag = S.dma("sync", f"st{scount[0] % 2}", Y[tok0:tok0 + 128, :], stage.t[:], reads=[stage.b], writes=[])
        slab_tags.append(tag)
        if (tok0 + 128) % 1024 == 0:
            k = tok0 // 1024
            for tg in slab_tags:
                S._wait("gpsimd", tg)
            del slab_tags[:]
            S.collective(f"{'s' if col0 == 0 else 'r'}{k}",
                         lambda e, k=k, Y=Y, Gt=Gt: e.collective_compute("AllGather", ALU.bypass, replica_groups=fused["groups"],
                                                                        ins=[Y[k * 1024:(k + 1) * 1024, :].opt()],
                                                                        outs=[Gt[k * 4096:(k + 1) * 4096, :].opt()]),
                         reads=[], writes=[gb[k]])

    if do_ssd:
        load_weights(dram["w1"], N1)
        es1 = ExitStack()

        def sb1(name, shape, dt):
            return T(es1.enter_context(nc.sbuf_tensor("s_" + name, shape, dt)), name)

        projB, ARb, ydB, yoB, hnB, miscB, dtB = banks[1], banks[2], banks[3], banks[4], banks[5], banks[6], banks[7]
        Pb = sb1("Pb", [128, TT + 3], F32)
        hist = sb1("hist", [128, 6, 3], F32)
        acc = sb1("acc", [128, TT], F32)
        xsT = sb1("xsT", [128, 4, TT], BF16)
        sz2 = [sb1(f"sz{i}", [128, NST, 512], F32) for i in range(2)]
        BT2 = [sb1(f"BT{i}", [128, TT], BF16) for i in range(2)]
        CT2 = [sb1(f"CT{i}", [128, TT], BF16) for i in range(2)]
        Xtm2 = [sb1(f"Xtm{i}", [128, NST, 512], BF16) for i in range(2)]
        Btm2 = [sb1(f"Btm{i}", [128, NST, 128], BF16) for i in range(2)]
        dtr2 = [sb1(f"dtr{i}", [128, NST, 8], F32) for i in range(2)]
        dts = sb1("dts", [128, 64], F32)
        Dm = sb1("Dm", [128, 512], F32)
        LT = sb1("LT", [128, 512], F32)
        MTb = sb1("MTb", [128, 512], BF16)
        Xd = sb1("Xd", [128, 512], BF16)
        t1 = sb1("t1", [128, 512], F32)
        ys = sb1("ys", [128, 512], F32)
        hf = sb1("hf", [128, 512], F32)
        hb = sb1("hb", [128, 512], BF16)
        cdb = sb1("cdb", [128, 2, 8], F32)
        sm1 = sb1("sm1", [128, 8], F32)

        G(lambda e: e.memset(hist.t[:], 0.0), [], [hist])
        G(lambda e: e.memset(hf.t[:], 0.0), [], [hf])
        G(lambda e: e.memset(hb.t[:], 0.0), [], [hb])

        def v8(ap):
            return ap.unsqueeze(2).to_broadcast([128, 8, 64])

        def r3(ap):
            return ap.rearrange("p (e l) -> p e l", l=64)

        DT, ADT, ACS, EA, SD, TMP, NACS = 0, 8, 16, 24, 32, 40, 48

        def prep1_gen(tile):
            bs = tile % 2
            sz, BT, CT, Xtm, Btm, dtr = sz2[bs], BT2[bs], CT2[bs], Xtm2[bs], Btm2[bs], dtr2[bs]
            load_norm_transpose(tile)
            if UTd is not None and do_rwkv:
                S.dma("sync", "st_u", UTd[tile], uT.t[:].rearrange("p a b -> p (a b)"), reads=[uT.b], writes=[])
            yield
            for st in range(NST):
                for kc in range(16):
                    PE(lambda e, kc=kc, st=st: e.matmul(projB.t[:, 0:512], lhsT=uT.t[:, kc, st * 128:(st + 1) * 128], rhs=W.t[:, kc, 0:512],
                                                        start=(kc == 0), stop=(kc == 15)), [W, uT], [projB])
                A(lambda e, st=st: e.activation(out=sz.t[:, st, :], in_=projB.t[:, 0:512], func=AF.Silu), [projB], [sz])
                for kc in range(16):
                    PE(lambda e, kc=kc, st=st: e.matmul(dtB.t[:, st * 8:(st + 1) * 8], lhsT=uT.t[:, kc, st * 128:(st + 1) * 128], rhs=W.t[:, kc, 512:520],
                                                        start=(kc == 0), stop=(kc == 15)), [W, uT], [dtB])
                yield
            V(lambda e: e.tensor_copy(out=dtr.t[:].rearrange("p a b -> p (a b)"), in_=dtB.t[:, 0:NST * 8]), [dtB], [dtr])
            for blk in range(6):
                proj_fm(520 + blk * 128, 128, projB)
                G(lambda e, blk=blk: e.tensor_copy(out=Pb.t[:, 0:3], in_=hist.t[:, blk, :]), [hist], [Pb])
                A(lambda e: e.activation(out=Pb.t[:, 3:3 + TT], in_=projB.t[:, 0:TT], func=AF.Copy), [projB], [Pb])
                A(lambda e, blk=blk: e.activation(out=acc.t[:], in_=projB.t[:, 0:TT], func=AF.Identity,
                                                  scale=pv.t[:, PV_CW + blk * 4 + 3:PV_CW + blk * 4 + 4],
                                                  bias=pv.t[:, PV_CB + blk:PV_CB + blk + 1]), [projB, pv], [acc])
                for j in (2, 1, 0):
                    V(lambda e, blk=blk, j=j: e.scalar_tensor_tensor(out=acc.t[:], in0=Pb.t[:, j:j + TT],
                                                                     scalar=pv.t[:, PV_CW + blk * 4 + j:PV_CW + blk * 4 + j + 1],
                                                                     in1=acc.t[:], op0=ALU.mult, op1=ALU.add), [Pb, pv, acc], [acc])
                G(lambda e, blk=blk: e.tensor_copy(out=hist.t[:, blk, :], in_=Pb.t[:, TT:TT + 3]), [Pb], [hist])
                if blk < 4:
                    A(lambda e, blk=blk: e.activation(out=xsT.t[:, blk, :], in_=acc.t[:], func=AF.Silu), [acc], [xsT])
                elif blk == 4:
                    A(lambda e: e.activation(out=BT.t[:], in_=acc.t[:], func=AF.Silu), [acc], [BT])
                else:
                    A(lambda e: e.activation(out=CT.t[:], in_=acc.t[:], func=AF.Silu), [acc], [CT])
                yield
            tpv1 = tpB.t[:].bitcast(BF16)
            for st in range(NST):
                tsl = slice(st * 128, (st + 1) * 128)
                for blk in range(4):
                    PE(lambda e, blk=blk, tsl=tsl: e.transpose(tpv1[:, blk * 128:(blk + 1) * 128], xsT.t[:, blk, tsl], ident_b), [xsT, cb], [tpB])
                PE(lambda e, tsl=tsl: e.transpose(tpv1[:, 512:640], BT.t[:, tsl], ident_b), [BT, cb], [tpB])
                A(lambda e, st=st: e.activation(out=Xtm.t[:, st, :], in_=tpv1[:, 0:512], func=AF.Copy), [tpB], [Xtm])
                A(lambda e, st=st: e.activation(out=Btm.t[:, st, :], in_=tpv1[:, 512:640], func=AF.Copy), [tpB], [Btm])
                yield

        def core1_gen(tile):
            bs = tile % 2
            sz, BT, CT, Xtm, Btm, dtr = sz2[bs], BT2[bs], CT2[bs], Xtm2[bs], Btm2[bs], dtr2[bs]
            for st in range(NST):
                V(lambda e, st=st: e.tensor_tensor(out=dts.t[:, TMP:TMP + 8], in0=dtr.t[:, st, :], in1=bc.t[:, BC_DTB:BC_DTB + 8], op=ALU.add), [dtr, bc], [dts])
                A(lambda e: e.activation(out=dts.t[:, DT:DT + 8], in_=dts.t[:, TMP:TMP + 8], func=AF.Abs), [dts], [dts])
                A(lambda e: e.activation(out=dts.t[:, DT:DT + 8], in_=dts.t[:, DT:DT + 8], func=AF.Exp, scale=-1.0), [dts], [dts])
                A(lambda e: e.activation(out=dts.t[:, DT:DT + 8], in_=dts.t[:, DT:DT + 8], func=AF.Ln, bias=1.0), [dts], [dts])
                V(lambda e: e.scalar_tensor_tensor(out=dts.t[:, DT:DT + 8], in0=dts.t[:, TMP:TMP + 8], scalar=0.0, in1=dts.t[:, DT:DT + 8],
                                                   op0=ALU.max, op1=ALU.add), [dts], [dts])
                V(lambda e: e.tensor_tensor(out=dts.t[:, ADT:ADT + 8], in0=dts.t[:, DT:DT + 8], in1=aneg.t[:], op=ALU.mult), [dts, aneg], [dts])
                PE(lambda e: e.matmul(miscB.t[:, 8:16], lhsT=cst.t[:, CS_TRI2:CS_TRI2 + 128], rhs=dts.t[:, ADT:ADT + 8], start=True, stop=True), [cst, dts], [miscB])
                PE(lambda e: e.matmul(miscB.t[:, 16:24], lhsT=cst.t[:, CS_BONES:CS_BONES + 128], rhs=dts.t[:, ADT:ADT + 8], start=True, stop=True), [cst, dts], [miscB])
                PE(lambda e: e.matmul(miscB.t[:, 24:32], lhsT=cst.t[:, CS_CH0:CS_CH0 + 128], rhs=dts.t[:, ADT:ADT + 8], start=True, stop=True), [cst, dts], [miscB])
                PE(lambda e: e.matmul(miscB.t[:, 32:40], lhsT=cst.t[:, CS_CH1:CS_CH1 + 128], rhs=dts.t[:, ADT:ADT + 8], start=True, stop=True), [cst, dts], [miscB])
                for c in range(2):
                    csl = slice(st * 128 + c * 64, st * 128 + (c + 1) * 64)
                    PE(lambda e, c=c, csl=csl: e.matmul(miscB.t[c * 64:(c + 1) * 64, 64:128], lhsT=BT.t[:, csl], rhs=CT.t[:, csl], start=True, stop=True), [BT, CT], [miscB])
                A(lambda e: e.activation(out=dts.t[:, ACS:ACS + 8], in_=miscB.t[:, 8:16], func=AF.Copy), [miscB], [dts])
                A(lambda e: e.activation(out=dts.t[:, EA:EA + 8], in_=miscB.t[:, 8:16], func=AF.Exp), [miscB], [dts])
                A(lambda e: e.activation(out=cdb.t[:].rearrange("p a b -> p (a b)"), in_=miscB.t[:, 24:40], func=AF.Exp), [miscB], [cdb])
                V(lambda e: e.tensor_tensor(out=dts.t[:, SD:SD + 8], in0=miscB.t[:, 16:24], in1=dts.t[:, ACS:ACS + 8], op=ALU.subtract), [miscB, dts], [dts])
                A(lambda e: e.activation(out=dts.t[:, SD:SD + 8], in_=dts.t[:, SD:SD + 8], func=AF.Exp), [dts], [dts])
                V(lambda e: e.tensor_tensor(out=dts.t[:, SD:SD + 8], in0=dts.t[:, SD:SD + 8], in1=dts.t[:, DT:DT + 8], op=ALU.mult), [dts], [dts])
                V(lambda e: e.tensor_tensor(out=r3(Dm.t[:]), in0=v8(dts.t[:, ADT:ADT + 8]),
                                            in1=cst.t[:, CS_TRIL:CS_TRIL + 64].unsqueeze(1).to_broadcast([128, 8, 64]), op=ALU.mult), [dts, cst], [Dm])
                yield
                PE(lambda e: e.matmul(ARb.t[:, :], lhsT=cst.t[:, CS_BONES:CS_BONES + 128], rhs=Dm.t[:], start=True, stop=False), [cst, Dm], [ARb])
                PE(lambda e: e.matmul(ARb.t[:, :], lhsT=ident_f, rhs=cst.t[:, CS_NEGM:CS_NEGM + 512], start=False, stop=True), [cst], [ARb])
                V(lambda e: e.tensor_tensor(out=r3(LT.t[:]), in0=r3(ARb.t[:, :]), in1=v8(dts.t[:, ACS:ACS + 8]), op=ALU.subtract), [ARb, dts], [LT])
                A(lambda e: e.activation(out=LT.t[:], in_=LT.t[:], func=AF.Exp), [LT], [LT])
                yield
                V(lambda e: e.tensor_tensor(out=r3(LT.t[:]), in0=r3(LT.t[:]), in1=miscB.t[:, 64:128].unsqueeze(1).to_broadcast([128, 8, 64]), op=ALU.mult), [LT, miscB], [LT])
                V(lambda e: e.tensor_tensor(out=r3(LT.t[:]), in0=r3(LT.t[:]), in1=v8(dts.t[:, DT:DT + 8]), op=ALU.mult), [LT, dts], [LT])
                V(lambda e: e.tensor_tensor(out=MTb.t[:], in0=LT.t[:], in1=DI.t[:], op=ALU.add), [LT, DI], [MTb])
                G(lambda e, st=st: e.tensor_tensor(out=r3(Xd.t[:]), in0=r3(Xtm.t[:, st, :]), in1=v8(dts.t[:, SD:SD + 8]), op=ALU.mult), [Xtm, dts], [Xd])
                yield
                for c in range(2):
                    for h in range(8):
                        PE(lambda e, c=c, h=h, st=st: e.matmul(ydB.t[c * 64:(c + 1) * 64, h * 64:(h + 1) * 64],
                                                               lhsT=MTb.t[c * 64:(c + 1) * 64, h * 64:(h + 1) * 64],
                                                               rhs=Xtm.t[c * 64:(c + 1) * 64, st, h * 64:(h + 1) * 64], start=True, stop=True), [MTb, Xtm], [ydB], serial=(c == 1 and h == 0))
                for c in range(2):
                    csl = slice(st * 128 + c * 64, st * 128 + (c + 1) * 64)
                    PE(lambda e, c=c, csl=csl: e.matmul(yoB.t[c * 64:(c + 1) * 64, :], lhsT=CT.t[:, csl], rhs=hb.t[:], start=True, stop=True), [CT, hb], [yoB])
                    PE(lambda e, c=c, st=st: e.matmul(hnB.t[:, :], lhsT=Btm.t[c * 64:(c + 1) * 64, st, :], rhs=Xd.t[c * 64:(c + 1) * 64, :], start=True, stop=True), [Btm, Xd], [hnB])
                    V(lambda e, c=c: e.tensor_tensor(out=r3(hf.t[:]), in0=r3(hf.t[:]), in1=v8(cdb.t[:, c, :]), op=ALU.mult), [hf, cdb], [hf])
                    V(lambda e: e.tensor_tensor(out=hf.t[:], in0=hf.t[:], in1=hnB.t[:, :], op=ALU.add), [hf, hnB], [hf])
                    A(lambda e: e.activation(out=hb.t[:], in_=hf.t[:], func=AF.Copy), [hf], [hb])
                    yield
                V(lambda e: e.tensor_tensor(out=r3(t1.t[:]), in0=r3(yoB.t[:, :]), in1=v8(dts.t[:, EA:EA + 8]), op=ALU.mult), [yoB, dts], [t1])
                V(lambda e: e.tensor_tensor(out=ys.t[:], in0=ydB.t[:, :], in1=t1.t[:], op=ALU.add), [ydB, t1], [ys])
                G(lambda e, st=st: e.tensor_tensor(out=ys.t[:], in0=ys.t[:], in1=sz.t[:, st, :], op=ALU.mult), [ys, sz], [ys])
                A(lambda e: e.activation(out=t1.t[:], in_=ys.t[:], func=AF.Square, accum_out=sm1.t[:, 0:1]), [ys], [t1, sm1])
                V(lambda e: e.tensor_scalar(out=sm1.t[:, 1:2], in0=sm1.t[:, 0:1], scalar1=1.0 / 512, scalar2=GATED_NORM_EPS, op0=ALU.mult, op1=ALU.add), [sm1], [sm1])
                A(lambda e: e.activation(out=sm1.t[:, 2:3], in_=sm1.t[:, 1:2], func=AF.Sqrt), [sm1], [sm1])
                V(lambda e: e.reciprocal(out=sm1.t[:, 3:4], in_=sm1.t[:, 2:3]), [sm1], [sm1])
                stage = ost[scount[0] % 2]
                V(lambda e, stage=stage: e.scalar_tensor_tensor(out=stage.t[:], in0=ys.t[:], scalar=sm1.t[:, 3:4], in1=bc.t[:, BC_NG:BC_NG + 512],
                                                                op0=ALU.mult, op1=ALU.mult), [ys, sm1, bc], [stage])
                store(stage, tile * TT + st * 128, 0)
                scount[0] += 1
                yield

        for it in range(NTILES + 1):
            gp = prep1_gen(it) if it < NTILES else None
            gc = core1_gen(it - 1) if it > 0 else None
            while gp is not None or gc is not None:
                if gc is not None:
                    try:
                        next(gc)
                    except StopIteration:
                        gc = None
                if gp is not None:
                    try:
                        next(gp)
                    except StopIteration:
                        gp = None
        S.barrier()
        es1.close()

    if do_rwkv:
        projB, tp2B, AmB0, AmB1, paB, qgB, yB = banks[1], banks[2], banks[3], banks[4], banks[5], banks[6], banks[7]
        load_weights(dram["w2"], N2)
        Pr = sb("Pr", [128, TT + 1], F32)
        carry = sb("carry", [128, 16], F32)
        dd = sb("dd", [128, TT], F32)
        sh = sb("sh", [128, TT], F32)
        twd = sb("twd", [96, TT], BF16)
        tad = sb("tad", [96, TT], BF16)
        rr = sb("rr", [128, TT], F32)
        kr = sb("kr", [128, TT], F32)
        sg = sb("sg", [128, TT], F32)
        cum = sb("cum", [128, TT], F32)
        Wc = sb("Wc", [128, TT], F32)
        iW = sb("iW", [128, TT], F32)
        Wex = sb("Wex", [128, TT], F32)
        alpha = sb("alpha", [128, TT], F32)
        kkr = sb("kkr", [128, TT], F32)
        sq = sb("sq", [128, TT], F32)
        k2 = sb("k2", [128, TT], F32)
        tmpa = sb("tmpa", [128, TT], F32)
        rkk = sb("rkk", [128, TT], BF16)
        vT = sb("vT", [128, 4, TT], BF16)
        sgT2 = [sb(f"sgT{i}", [128, 2, TT], BF16) for i in range(2)]
        AR2 = [sb(f"AR{i}", [128, 4, NCH, 2, 64], BF16) for i in range(2)]
        BK2 = [sb(f"BK{i}", [128, 4, NCH, 2, 64], BF16) for i in range(2)]
        vTz2 = [sb(f"vTz{i}", [128, 4, NCH, 2, 64], BF16) for i in range(2)]
        Vtm2 = [sb(f"Vtm{i}", [128, NST, 512], BF16) for i in range(2)]
        Wl2 = [sb(f"Wl{i}", [128, 4, NCH], F32) for i in range(2)]
        rks2 = [sb(f"rks{i}", [128, NST, 8], F32) for i in range(2)]
        A_sb = sb("A_sb", [128, 8, 128], BF16)
        Pp = [sb(f"Pp{i}", [64, 8, 64], BF16) for i in range(2)]
        Qp = [sb(f"Qp{i}", [64, 8, 64], BF16) for i in range(2)]
        Gp = [sb(f"Gp{i}", [64, 8, 64], BF16) for i in range(2)]
        BKtok2 = [sb(f"BKtok{i}", [128, 512], BF16) for i in range(2)]
        UV2 = [sb(f"UV{i}", [128, 512], BF16) for i in range(2)]
        Xs = sb("Xs", [64, 512], BF16)
        Sf = sb("Sf", [128, 256], F32)
        Sb_e = sb("Sb_e", [128, 256], BF16)
        Sb_o = sb("Sb_o", [128, 256], BF16)
        ysq = sb("ysq", [128, 512], F32)
        yc = sb("yc", [128, 512], F32)
        bon = ysq
        gst = sb("gst", [128, 64], F32)

        G(lambda e: e.memset(carry.t[:], 0.0), [], [carry])
        for i_ in range(2):
            G(lambda e, i_=i_: e.memset(vTz2[i_].t[:], 0.0), [], [vTz2[i_]])
        G(lambda e: e.memset(Sf.t[:], 0.0), [], [Sf])
        G(lambda e: e.memset(Sb_e.t[:], 0.0), [], [Sb_e])
        G(lambda e: e.memset(Sb_o.t[:], 0.0), [], [Sb_o])

        maskA3 = cst.t[:, CS_MASKA:CS_MASKA + 128].unsqueeze(1).to_broadcast([128, 4, 128])
        maskQ3 = cst.t[0:64, CS_MASKQ:CS_MASKQ + 64].unsqueeze(1).to_broadcast([64, 8, 64])
        identl3 = cst.t[0:64, CS_IDL:CS_IDL + 64].unsqueeze(1).to_broadcast([64, 8, 64])

        def shift_block(bi, col0, M, dest_fn):
            proj_fm(col0, M, projB)
            G(lambda e: e.tensor_copy(out=Pr.t[0:M, 0:1], in_=carry.t[0:M, bi:bi + 1]), [carry], [Pr])
            A(lambda e: e.activation(out=Pr.t[0:M, 1:TT + 1], in_=projB.t[0:M, 0:TT], func=AF.Copy), [projB], [Pr])
            V(lambda e: e.tensor_tensor(out=dd.t[0:M, :], in0=Pr.t[0:M, 0:TT], in1=Pr.t[0:M, 1:TT + 1], op=ALU.subtract), [Pr], [dd])
            G(lambda e: e.tensor_copy(out=carry.t[0:M, bi:bi + 1], in_=Pr.t[0:M, TT:TT + 1]), [Pr], [carry])
            dest_fn()

        def c4(ap):
            return ap.rearrange("p (c l) -> p c l", l=64)

        def prep_gen(tile):
            bs = tile % 2
            sgT, AR, BK, vTz, Vtm, Wl, rks = sgT2[bs], AR2[bs], BK2[bs], vTz2[bs], Vtm2[bs], Wl2[bs], rks2[bs]
            if UTd is not None and do_ssd:
                S.dma("sync", "ld_u", uT.t[:].rearrange("p a b -> p (a b)"), UTd[tile], writes=[uT.b])
            else:
                load_norm_transpose(tile)
            yield

            def d_wd():
                V(lambda e: e.scalar_tensor_tensor(out=sh.t[0:96, :], in0=dd.t[0:96, :], scalar=pv.t[0:96, PV_MU + 0:PV_MU + 1], in1=Pr.t[0:96, 1:TT + 1],
                                                   op0=ALU.mult, op1=ALU.add), [dd, pv, Pr], [sh])
                A(lambda e: e.activation(out=twd.t[:], in_=sh.t[0:96, :], func=AF.Tanh), [sh], [twd])
            shift_block(0, 0, 96, d_wd)
            yield

            def d_ad():
                V(lambda e: e.scalar_tensor_tensor(out=tad.t[:], in0=dd.t[0:96, :], scalar=pv.t[0:96, PV_MU + 1:PV_MU + 2], in1=Pr.t[0:96, 1:TT + 1],
                                                   op0=ALU.mult, op1=ALU.add), [dd, pv, Pr], [tad])
            shift_block(1, 96, 96, d_ad)
            yield
            for gi in range(2):
                def d_gd(gi=gi):
                    V(lambda e: e.scalar_tensor_tensor(out=sh.t[:], in0=dd.t[:], scalar=pv.t[:, PV_MU + 2 + gi:PV_MU + 3 + gi], in1=Pr.t[:, 1:TT + 1],
                                                       op0=ALU.mult, op1=ALU.add), [dd, pv, Pr], [sh])
                    A(lambda e: e.activation(out=sgT.t[:, gi, :], in_=sh.t[:], func=AF.Sigmoid), [sh], [sgT])
                shift_block(2 + gi, 192 + gi * 128, 128, d_gd)
                yield
            tpv2 = tpB.t[:].bitcast(BF16).rearrange("p (s a b) -> p s a b", s=NST, a=4)
            for j in range(4):
                cbase = 448 + j * 384
                mu0 = PV_MU + 4 + j * 3

                def d_r(j=j, mu0=mu0):
                    V(lambda e: e.scalar_tensor_tensor(out=rr.t[:], in0=dd.t[:], scalar=pv.t[:, mu0:mu0 + 1], in1=Pr.t[:, 1:TT + 1],
                                                       op0=ALU.mult, op1=ALU.add), [dd, pv, Pr], [rr])
                shift_block(4 + j * 3, cbase, 128, d_r)
                yield

                def d_k(j=j, mu0=mu0):
                    V(lambda e: e.scalar_tensor_tensor(out=kr.t[:], in0=dd.t[:], scalar=pv.t[:, mu0 + 1:mu0 + 2], in1=Pr.t[:, 1:TT + 1],
                                                       op0=ALU.mult, op1=ALU.add), [dd, pv, Pr], [kr])
                shift_block(5 + j * 3, cbase + 128, 128, d_k)
                yield

                def d_v(j=j, mu0=mu0):
                    V(lambda e: e.scalar_tensor_tensor(out=vT.t[:, j, :], in0=dd.t[:], scalar=pv.t[:, mu0 + 2:mu0 + 3], in1=Pr.t[:, 1:TT + 1],
                                                       op0=ALU.mult, op1=ALU.add), [dd, pv, Pr], [vT])
                    G(lambda e: e.tensor_copy(out=vTz.t[:, j, :, 1, :], in_=vT.t[:, j, :].rearrange("p (c l) -> p c l", l=64)), [vT], [vTz])
                shift_block(6 + j * 3, cbase + 256, 128, d_v)
                yield
                PE(lambda e, j=j: e.matmul(projB.t[:, 0:TT], lhsT=lw2b.t[:, j * 128:(j + 1) * 128], rhs=twd.t[:], start=True, stop=True), [lw2b, twd], [projB])
                A(lambda e, j=j: e.activation(out=sg.t[:], in_=projB.t[:, 0:TT], func=AF.Sigmoid, bias=pv.t[:, PV_W0 + j:PV_W0 + j + 1]), [projB, pv], [sg])
                V(lambda e: e.tensor_tensor_scan(out=cum.t[:], data0=cst.t[:, CS_SCAN:CS_SCAN + TT], data1=sg.t[:], initial=0.0, op0=ALU.mult, op1=ALU.subtract), [cst, sg], [cum])
                A(lambda e: e.activation(out=Wc.t[:], in_=cum.t[:], func=AF.Exp, scale=C0), [cum], [Wc])
                A(lambda e: e.activation(out=iW.t[:], in_=cum.t[:], func=AF.Exp, scale=-C0), [cum], [iW])
                V(lambda e: e.tensor_tensor(out=tmpa.t[:], in0=cum.t[:], in1=sg.t[:], op=ALU.add), [cum, sg], [tmpa])
                A(lambda e: e.activation(out=Wex.t[:], in_=tmpa.t[:], func=AF.Exp, scale=C0), [tmpa], [Wex])
                G(lambda e, j=j: e.tensor_copy(out=Wl.t[:, j, :], in_=c4(Wc.t[:])[:, :, 63]), [Wc], [Wl])
                yield
                PE(lambda e, j=j: e.matmul(projB.t[:, 0:TT], lhsT=la2b.t[:, j * 128:(j + 1) * 128], rhs=tad.t[:], start=True, stop=True), [la2b, tad], [projB])
                A(lambda e, j=j: e.activation(out=alpha.t[:], in_=projB.t[:, 0:TT], func=AF.Sigmoid, bias=pv.t[:, PV_A0 + j:PV_A0 + j + 1]), [projB, pv], [alpha])
                V(lambda e, j=j: e.tensor_scalar(out=kkr.t[:], in0=kr.t[:], scalar1=pv.t[:, PV_KK + j:PV_KK + j + 1], scalar2=None, op0=ALU.mult), [kr, pv], [kkr])
                A(lambda e: e.activation(out=sq.t[:], in_=kkr.t[:], func=AF.Square), [kkr], [sq])
                PE(lambda e: e.matmul(projB.t[:, 0:TT], lhsT=cst.t[:, CS_BONES:CS_BONES + 128], rhs=sq.t[:], start=True, stop=True), [cst, sq], [projB])
                A(lambda e: e.activation(out=sq.t[:], in_=projB.t[:, 0:TT], func=AF.Sqrt), [projB], [sq])
                V(lambda e: e.tensor_scalar(out=sq.t[:], in0=sq.t[:], scalar1=1e-12, scalar2=None, op0=ALU.max), [sq], [sq])
                V(lambda e: e.reciprocal(out=sq.t[:], in_=sq.t[:]), [sq], [sq])
                V(lambda e: e.tensor_tensor(out=kkr.t[:], in0=kkr.t[:], in1=sq.t[:], op=ALU.mult), [kkr, sq], [kkr])
                yield
                V(lambda e, j=j: e.tensor_scalar(out=tmpa.t[:], in0=alpha.t[:], scalar1=pv.t[:, PV_KA + j:PV_KA + j + 1], scalar2=omk.t[:, j:j + 1],
                                                 op0=ALU.mult, op1=ALU.add), [alpha, pv, omk], [tmpa])
                V(lambda e: e.tensor_tensor(out=k2.t[:], in0=kr.t[:], in1=tmpa.t[:], op=ALU.mult), [kr, tmpa], [k2])
                V(lambda e, j=j: e.scalar_tensor_tensor(out=AR.t[:, j, :, 0, :], in0=c4(kkr.t[:]), scalar=-1.0, in1=c4(Wex.t[:]), op0=ALU.mult, op1=ALU.mult), [kkr, Wex], [AR])
                G(lambda e, j=j: e.tensor_tensor(out=AR.t[:, j, :, 1, :], in0=c4(rr.t[:]), in1=c4(Wc.t[:]), op=ALU.mult), [rr, Wc], [AR])
                V(lambda e: e.tensor_tensor(out=tmpa.t[:], in0=kkr.t[:], in1=alpha.t[:], op=ALU.mult), [kkr, alpha], [tmpa])
                V(lambda e, j=j: e.tensor_tensor(out=BK.t[:, j, :, 0, :], in0=c4(tmpa.t[:]), in1=c4(iW.t[:]), op=ALU.mult), [tmpa, iW], [BK])
                G(lambda e, j=j: e.tensor_tensor(out=BK.t[:, j, :, 1, :], in0=c4(k2.t[:]), in1=c4(iW.t[:]), op=ALU.mult), [k2, iW], [BK])
                V(lambda e, j=j: e.scalar_tensor_tensor(out=rkk.t[:], in0=rr.t[:], scalar=pv.t[:, PV_RK + j:PV_RK + j + 1], in1=k2.t[:], op0=ALU.mult, op1=ALU.mult), [rr, pv, k2], [rkk])
                for st in range(NST):
                    PE(lambda e, j=j, st=st: e.matmul(projB.t[:, 256 + st * 8 + j * 2:256 + st * 8 + j * 2 + 2], lhsT=rkk.t[:, st * 128:(st + 1) * 128], rhs=hsel_b,
                                                      start=True, stop=True), [rkk, cb], [projB])
                for st in range(NST):
                    PE(lambda e, j=j, st=st: e.transpose(tpv2[:, st, j, :], vT.t[:, j, st * 128:(st + 1) * 128], ident_b), [vT, cb], [tpB])
                yield
            A(lambda e: e.activation(out=rks.t[:].rearrange("p a b -> p (a b)"), in_=projB.t[:, 256:256 + NST * 8], func=AF.Copy), [projB], [rks])
            A(lambda e: e.activation(out=Vtm.t[:].rearrange("p a b -> p (a b)"), in_=tpB.t[:].bitcast(BF16), func=AF.Copy), [tpB], [Vtm])
            yield

        A_sb2 = [A_sb, sb("A_sb1", [128, 8, 128], BF16)]
        Gp2 = [Gp, [sb(f"Gq{i}", [64, 8, 64], BF16) for i in range(2)]]
        Gfin = {}

        def tchain_gen(tile, c):
            bs = tile % 2
            AR, BK = AR2[bs], BK2[bs]
            par = c % 2
            A_s = A_sb2[par]
            Gq = Gp2[par]
            q3 = qgB.t[0:64, :].rearrange("p (a b) -> p a b", b=64)
            p3 = AmB0.t[0:64, :].rearrange("p (a b) -> p a b", b=64)
            g3 = AmB1.t[0:64, :].rearrange("p (a b) -> p a b", b=64)
            vTz = vTz2[bs]
            BKtok, UV = BKtok2[par], UV2[par]
            tp2 = tp2B.t[:].bitcast(BF16).rearrange("p (a b) -> p a b", b=128)
            for h in range(8):
                j, i = h // 2, h % 2
                bank = AmB0 if i == 0 else AmB1
                PE(lambda e, j=j, i=i, h=h, c=c, bank=bank: e.matmul(bank.t[:, j * 128:(j + 1) * 128],
                                                                     lhsT=BK.t[i * 64:(i + 1) * 64, j, c, :, :].rearrange("p a b -> p (a b)"),
                                                                     rhs=AR.t[i * 64:(i + 1) * 64, j, c, :, :].rearrange("p a b -> p (a b)"),
                                                                     start=True, stop=True), [BK, AR], [bank])
            for hb_, bank in enumerate((AmB0, AmB1)):
                V(lambda e, hb_=hb_, bank=bank: e.tensor_tensor(out=A_s.t[:, hb_::2, :], in0=bank.t[:, :].rearrange("p (a b) -> p a b", b=128),
                                                                in1=maskA3, op=ALU.mult), [bank, cst], [A_s])
            for j in range(4):
                PE(lambda e, j=j, c=c: e.transpose(tp2[:, j, :], BK.t[:, j, c, :, :].rearrange("p a b -> p (a b)"), ident_b), [BK, cb], [tp2B])
                PE(lambda e, j=j, c=c: e.transpose(tp2[:, 4 + j, :], vTz.t[:, j, c, :, :].rearrange("p a b -> p (a b)"), ident_b), [vTz, cb], [tp2B])
            A(lambda e: e.activation(out=BKtok.t[:], in_=tp2B.t[:].bitcast(BF16)[:, 0:512], func=AF.Copy), [tp2B], [BKtok])
            A(lambda e: e.activation(out=UV.t[:, :], in_=tp2B.t[:].bitcast(BF16)[:, 512:1024], func=AF.Copy), [tp2B], [UV])
            yield
            for h in (0, 2, 4, 6, 1, 3, 5, 7):
                j, i = h // 2, h % 2
                PE(lambda e, j=j, i=i, h=h, c=c: e.matmul(q3[:, h, :], lhsT=AR.t[i * 64:(i + 1) * 64, j, c, 0, :], rhs=BK.t[i * 64:(i + 1) * 64, j, c, 0, :],
                                                          start=True, stop=True), [AR, BK], [qgB], serial=(h == 1))
            V(lambda e: e.tensor_tensor(out=Qp[0].t[:], in0=q3, in1=maskQ3, op=ALU.mult), [qgB, cst], [Qp[0]])
            A(lambda e: e.activation(out=Pp[0].t[:], in_=A_s.t[0:64, :, 0:64], func=AF.Copy), [A_s], [Pp[0]])
            G(lambda e: e.tensor_tensor(out=Gq[0].t[:], in0=A_s.t[0:64, :, 0:64], in1=identl3, op=ALU.add), [A_s, cst], [Gq[0]])
            yield
            for h in range(8):
                PE(lambda e, h=h: e.matmul(p3[:, h, :], lhsT=Qp[0].t[:, h, :], rhs=Pp[0].t[:, h, :], start=True, stop=True), [Qp[0], Pp[0]], [AmB0])
            for h in range(8):
                PE(lambda e, h=h: e.matmul(q3[:, h, :], lhsT=Pp[0].t[:, h, :], rhs=Qp[0].t[:, h, :], start=True, stop=True), [Qp[0], Pp[0]], [qgB])
            A(lambda e: e.activation(out=Pp[1].t[:], in_=p3, func=AF.Copy), [AmB0], [Pp[1]])
            V(lambda e: e.tensor_copy(out=Qp[1].t[:], in_=q3), [qgB], [Qp[1]])
            yield
            for l in range(1, 5):
                li, pi = l % 2, (l - 1) % 2
                for h in range(8):
                    PE(lambda e, h=h, li=li, pi=pi: e.matmul(g3[:, h, :], lhsT=Qp[li].t[:, h, :], rhs=Gq[pi].t[:, h, :], start=True, stop=True), [Qp[li], Gq[pi]], [AmB1])
                if l <= 3:
                    for h in range(8):
                        PE(lambda e, h=h, li=li: e.matmul(p3[:, h, :], lhsT=Qp[li].t[:, h, :], rhs=Pp[li].t[:, h, :], start=True, stop=True), [Qp[li], Pp[li]], [AmB0])
                for h in range(8):
                    PE(lambda e, h=h, li=li: e.matmul(q3[:, h, :], lhsT=Pp[li].t[:, h, :], rhs=Qp[li].t[:, h, :], start=True, stop=True), [Qp[li], Pp[li]], [qgB])
                V(lambda e, li=li, pi=pi: e.tensor_tensor(out=Gq[li].t[:], in0=g3, in1=Gq[pi].t[:], op=ALU.add), [AmB1, Gq[pi]], [Gq[li]])
                if l <= 3:
                    A(lambda e, pi=pi: e.activation(out=Pp[pi].t[:], in_=p3, func=AF.Copy), [AmB0], [Pp[pi]])
                V(lambda e, pi=pi: e.tensor_copy(out=Qp[pi].t[:], in_=q3), [qgB], [Qp[pi]])
                yield
            for h in range(8):
                PE(lambda e, h=h: e.matmul(g3[:, h, :], lhsT=Qp[1].t[:, h, :], rhs=Gq[0].t[:, h, :], start=True, stop=True), [Qp[1], Gq[0]], [AmB1])
            V(lambda e: e.tensor_tensor(out=Gq[1].t[:], in0=g3, in1=Gq[0].t[:], op=ALU.add), [AmB1, Gq[0]], [Gq[1]])
            yield
            Gfin[(tile, c)] = Gq[1]

        def state_gen(tile, c):
            bs = tile % 2
            sgT, AR, BK, vTz, Vtm, Wl, rks = sgT2[bs], AR2[bs], BK2[bs], vTz2[bs], Vtm2[bs], Wl2[bs], rks2[bs]
            tp2 = tp2B.t[:].bitcast(BF16).rearrange("p (a b) -> p a b", b=128)
            p3 = paB.t[0:64, :].rearrange("p (a b) -> p a b", b=64)
            A_s = A_sb2[c % 2]
            Gf = Gfin[(tile, c)]
            cp = c % 2
            st = c // 2
            BKtok, UV = BKtok2[c % 2], UV2[c % 2]
            for h in range(8):
                j, i = h // 2, h % 2
                Sm = Sb_e if i == 0 else Sb_o
                PE(lambda e, j=j, h=h, c=c, Sm=Sm: e.matmul(p3[:, h, :], lhsT=AR.t[:, j, c, 0, :], rhs=Sm.t[:, j * 64:(j + 1) * 64],
                                                            start=True, stop=False), [AR, Sm], [paB])
                PE(lambda e, h=h: e.matmul(p3[:, h, :], lhsT=A_s.t[:, h, 0:64], rhs=UV.t[:, h * 64:(h + 1) * 64], start=False, stop=True), [A_s, UV], [paB])
            A(lambda e: e.activation(out=Xs.t[:], in_=paB.t[0:64, :], func=AF.Copy), [paB], [Xs])
            yield
            for h in range(8):
                PE(lambda e, h=h, Gf=Gf: e.matmul(p3[:, h, :], lhsT=Gf.t[:, h, :], rhs=Xs.t[:, h * 64:(h + 1) * 64], start=True, stop=True), [Gf, Xs], [paB])
            V(lambda e: e.tensor_copy(out=UV.t[0:64, :], in_=paB.t[0:64, :]), [paB], [UV])
            yield
            for h in range(8):
                j, i = h // 2, h % 2
                Sm = Sb_e if i == 0 else Sb_o
                PE(lambda e, j=j, h=h, c=c, cp=cp, Sm=Sm: e.matmul(yB.t[cp * 64:(cp + 1) * 64, h * 64:(h + 1) * 64], lhsT=AR.t[:, j, c, 1, :],
                                                                   rhs=Sm.t[:, j * 64:(j + 1) * 64], start=True, stop=False), [AR, Sm], [yB])
                PE(lambda e, h=h, cp=cp: e.matmul(yB.t[cp * 64:(cp + 1) * 64, h * 64:(h + 1) * 64], lhsT=A_s.t[:, h, 64:128], rhs=UV.t[:, h * 64:(h + 1) * 64],
                                                  start=False, stop=True), [A_s, UV], [yB])
            for h in range(8):
                j, i = h // 2, h % 2
                PE(lambda e, j=j, i=i, h=h: e.matmul(paB.t[i * 64:(i + 1) * 64, 256 + j * 64:256 + (j + 1) * 64], lhsT=BKtok.t[:, j * 128 + i * 64:j * 128 + (i + 1) * 64],
                                                     rhs=UV.t[:, h * 64:(h + 1) * 64], start=True, stop=True), [BKtok, UV], [paB])
            V(lambda e: e.tensor_tensor(out=Sf.t[:], in0=Sf.t[:], in1=paB.t[:, 256:512], op=ALU.add), [Sf, paB], [Sf])
            V(lambda e, c=c: e.tensor_tensor(out=Sf.t[:].rearrange("p (a b) -> p a b", b=64), in0=Sf.t[:].rearrange("p (a b) -> p a b", b=64),
                                             in1=Wl.t[:, :, c].unsqueeze(2).to_broadcast([128, 4, 64]), op=ALU.mult), [Sf, Wl], [Sf])
            A(lambda e: e.activation(out=Sb_e.t[0:64, :], in_=Sf.t[0:64, :], func=AF.Copy), [Sf], [Sb_e])
            A(lambda e: e.activation(out=Sb_o.t[64:128, :], in_=Sf.t[64:128, :], func=AF.Copy), [Sf], [Sb_o])
            yield

            if cp == 1:
                y3 = yB.t[:, :].rearrange("p (h v) -> p h v", v=64)

                def g8(col):
                    return gst.t[:, col:col + 8]

                def b8(col):
                    return gst.t[:, col:col + 8].unsqueeze(2).to_broadcast([128, 8, 64])
                V(lambda e: e.tensor_reduce(out=g8(0), in_=y3, axis=AX.X, op=ALU.add), [yB], [gst])
                A(lambda e: e.activation(out=ysq.t[:], in_=yB.t[:, :], func=AF.Square), [yB], [ysq])
                V(lambda e: e.tensor_reduce(out=g8(8), in_=ysq.t[:].rearrange("p (h v) -> p h v", v=64), axis=AX.X, op=ALU.add), [ysq], [gst])
                V(lambda e: e.tensor_scalar(out=g8(16), in0=g8(0), scalar1=1.0 / 64, scalar2=None, op0=ALU.mult), [gst], [gst])
                V(lambda e: e.tensor_tensor(out=g8(24), in0=g8(16), in1=g8(16), op=ALU.mult), [gst], [gst])
                V(lambda e: e.scalar_tensor_tensor(out=g8(32), in0=g8(8), scalar=1.0 / 64, in1=g8(24), op0=ALU.mult, op1=ALU.subtract), [gst], [gst])
                V(lambda e: e.tensor_scalar(out=g8(32), in0=g8(32), scalar1=GN_EPS, scalar2=None, op0=ALU.add), [gst], [gst])
                A(lambda e: e.activation(out=g8(40), in_=g8(32), func=AF.Sqrt), [gst], [gst])
                V(lambda e: e.reciprocal(out=g8(48), in_=g8(40)), [gst], [gst])
                yc3 = yc.t[:].rearrange("p (h v) -> p h v", v=64)
                V(lambda e: e.tensor_tensor(out=yc3, in0=y3, in1=b8(16), op=ALU.subtract), [yB, gst], [yc])
                yield
                V(lambda e: e.tensor_tensor(out=yc3, in0=yc3, in1=b8(48), op=ALU.mult), [yc, gst], [yc])
                G(lambda e: e.tensor_tensor(out=yc.t[:], in0=yc.t[:], in1=bc.t[:, BC_GNW:BC_GNW + 512], op=ALU.mult), [yc, bc], [yc])
                G(lambda e: e.tensor_tensor(out=yc.t[:], in0=yc.t[:], in1=bc.t[:, BC_GNB:BC_GNB + 512], op=ALU.add), [yc, bc], [yc])
                V(lambda e, st=st: e.tensor_tensor(out=bon.t[:].rearrange("p (h v) -> p h v", v=64), in0=Vtm.t[:, st, :].rearrange("p (h v) -> p h v", v=64),
                                                   in1=rks.t[:, st, :].unsqueeze(2).to_broadcast([128, 8, 64]), op=ALU.mult), [Vtm, rks], [bon])
                V(lambda e: e.tensor_tensor(out=yc.t[:], in0=yc.t[:], in1=bon.t[:], op=ALU.add), [yc, bon], [yc])
                for kc2 in range(2):
                    PE(lambda e, kc2=kc2, st=st: e.matmul(paB.t[:, :], lhsT=sgT.t[:, kc2, st * 128:(st + 1) * 128], rhs=lg2b.t[:, kc2, :], start=(kc2 == 0), stop=(kc2 == 1)), [sgT, lg2b], [paB])
                stage = ost[scount[0] % 2]
                V(lambda e, stage=stage: e.tensor_tensor(out=stage.t[:], in0=yc.t[:], in1=paB.t[:, :], op=ALU.mult), [yc, paB], [stage])
                store(stage, tile * TT + st * 128, 512)
                scount[0] += 1
                yield

        def drain(g):
            if g is not None:
                for _ in g:
                    pass

        chunks = [(tile, c) for tile in range(NTILES) for c in range(NCH)]
        drain(prep_gen(0))
        drain(tchain_gen(0, 0))
        pg = None
        pg_tile = -1
        for idx, (tile, c) in enumerate(chunks):
            if c == 0 and tile + 1 < NTILES:
                pg = prep_gen(tile + 1)
                pg_tile = tile + 1
            gs = state_gen(tile, c)
            gt = None
            if idx + 1 < len(chunks):
                nt, ncn = chunks[idx + 1]
                if nt != tile:
                    drain(pg)
                    pg = None
                gt = tchain_gen(nt, ncn)
            def step(g, n=1):
                if g is None:
                    return None
                for _ in range(n):
                    try:
                        next(g)
                    except StopIteration:
                        return None
                return g
            while gs is not None or gt is not None:
                gt = step(gt)
                pg = step(pg, 1)
                gs = step(gs)
                pg = step(pg, 1)
                gt = step(gt)
                pg = step(pg, 1)

    if fused:
        return S, es
    for key, ent in S.dma_sems.items():
        if key.startswith("st"):
            S._wait("sync", ("dma", ent[0], ent[1], ("dma", key)))
    return S, es


TP = 512
NSTP = TP // 128
D = 2048
DMIX = 4096
DFF = 5632
NFF = DFF // 128
NWB = 8


def build_p2(nc, S, es, NT2, dram, fused=None):
    NTILES = NT2 // TP

    def sb(name, shape, dt):
        return T(es.enter_context(nc.sbuf_tensor("q_" + name, shape, dt)), name)

    def ps(name):
        return T(es.enter_context(nc.psum_tensor("q_" + name, [128, 512], F32)), name, True)

    def V(fn, r, w): return S.op("vector", fn, [a.b for a in r], [a.b for a in w])
    def A(fn, r, w): return S.op("scalar", fn, [a.b for a in r], [a.b for a in w])
    def G(fn, r, w): return S.op("gpsimd", fn, [a.b for a in r], [a.b for a in w])
    def PE(fn, r, w): return S.op("tensor", fn, [a.b for a in r], [a.b for a in w])

    gv = sb("gv", [128, 32], F32)
    idf = sb("idf", [128, 128], F32)
    idb = sb("idb", [128, 128], BF16)
    onesf = sb("onesf", [128, 128], F32)
    xin = sb("xin", [128, D], F32)
    if fused:
        ymg = [sb(f"ymg{i}", [128, 4096], BF16) for i in range(2)]
        idx = sb("idx", [128, NT2 // 128], mybir.dt.uint32)
        S.dma("sync", "q_c2", idx.t[:], dram["idx"][:, :], writes=[idx.b])
    else:
        ymin = [sb(f"ymin{i}", [128, 1024], F32) for i in range(2)]
        ymb = [sb(f"ymb{i}", [128, 1024], BF16) for i in range(2)]
    ymT = sb("ymT", [128, 32 * TP], BF16)
    hT = sb("hT", [128, 16, TP], F32)
    vT = sb("vT", [128, 16, TP], BF16)
    aT = sb("aT", [128, NFF, TP], BF16)
    sgt = sb("sgt", [128, 4, TP], F32)
    rstd = sb("rstd", [128, TP], F32)
    hsq = [sb(f"hsq{i}", [128, TP], F32) for i in range(2)]
    oT = [sb(f"oT{i}", [128, TP], F32) for i in range(2)]
    wst = [sb(f"wst{i}", [128, 512], F32) for i in range(NWB)]
    wbf = [sb(f"wbf{i}", [128, 512], BF16) for i in range(NWB)]
    acc = [ps(f"acc{i}") for i in range(4)]
    tpP = ps("tpP")
    nrmP = ps("nrmP")

    ymT3 = ymT.t[:].rearrange("p (k t) -> p k t", t=TP)
    ostg = ymT.t[:].bitcast(F32).rearrange("p (s f) -> p s f", f=D)

    S.dma("sync", "q_c0", gv.t[:], dram["gv"][:, :], writes=[gv.b])
    S.dma("sync", "q_c1", idf.t[:], dram["idf"][:, :], writes=[idf.b])
    V(lambda e: e.tensor_copy(out=idb.t[:], in_=idf.t[:]), [idf], [idb])
    G(lambda e: e.memset(onesf.t[:], 1.0), [], [onesf])

    wcount = [0]

    def wload(src_ap):
        i = wcount[0] % NWB
        wcount[0] += 1
        S.dma("sync", f"q_w{i}", wst[i].t[:], src_ap, writes=[wst[i].b])
        m = wcount[0] % 8
        if m in (0, 3, 6):
            A(lambda e, i=i: e.activation(out=wbf[i].t[:], in_=wst[i].t[:], func=AF.Copy), [wst[i]], [wbf[i]])
        elif m == 4:
            G(lambda e, i=i: e.tensor_copy(out=wbf[i].t[:], in_=wst[i].t[:]), [wst[i]], [wbf[i]])
        else:
            V(lambda e, i=i: e.tensor_copy(out=wbf[i].t[:], in_=wst[i].t[:]), [wst[i]], [wbf[i]])
        return wbf[i]

    def rmsnorm_scale(gcol0):
        for fb in range(16):
            hq = hsq[fb % 2]
            A(lambda e, fb=fb, hq=hq: e.activation(out=hq.t[:], in_=hT.t[:, fb, :], func=AF.Square), [hT], [hq])
            PE(lambda e, fb=fb, hq=hq: e.matmul(nrmP.t[:, :], lhsT=onesf.t[:], rhs=hq.t[:], start=(fb == 0), stop=(fb == 15)), [onesf, hq], [nrmP])
        V(lambda e: e.tensor_scalar(out=rstd.t[:], in0=nrmP.t[:, :], scalar1=1.0 / D, scalar2=RMS_EPS, op0=ALU.mult, op1=ALU.add), [nrmP], [rstd])
        A(lambda e: e.activation(out=rstd.t[:], in_=rstd.t[:], func=AF.Sqrt), [rstd], [rstd])
        V(lambda e: e.reciprocal(out=rstd.t[:], in_=rstd.t[:]), [rstd], [rstd])

    ycount = [0]
    for tile in range(NTILES):
        t0 = tile * TP
        for st in range(NSTP):
            tok = t0 + st * 128
            S.dma("sync", "q_x", xin.t[:], dram["xres"][tok:tok + 128, :], writes=[xin.b])
            for g4 in range(4):
                for k in range(4):
                    fb = g4 * 4 + k
                    PE(lambda e, fb=fb, k=k: e.transpose(tpP.t[:, k * 128:(k + 1) * 128], xin.t[:, fb * 128:(fb + 1) * 128], idf.t[:]), [xin, idf], [tpP])
                if g4 % 2 == 0:
                    A(lambda e, g4=g4, st=st: e.activation(out=hT.t[:, g4 * 4:(g4 + 1) * 4, st * 128:(st + 1) * 128],
                                                           in_=tpP.t[:, :].rearrange("p (a b) -> p a b", b=128), func=AF.Copy), [tpP], [hT])
                else:
                    V(lambda e, g4=g4, st=st: e.tensor_copy(out=hT.t[:, g4 * 4:(g4 + 1) * 4, st * 128:(st + 1) * 128],
                                                            in_=tpP.t[:, :].rearrange("p (a b) -> p a b", b=128)), [tpP], [hT])
            if fused:
                gi = ycount[0] % 2
                ycount[0] += 1
                sti = tile * NSTP + st
                slab = sti // 8
                for half, (Gt, gbl) in enumerate(((fused["GS"], fused["gsB"]), (fused["GR"], fused["grB"]))):
                    for r in range(4):
                        c0 = half * 2048 + r * 512
                        S.dma_fn("gpsimd", f"q_g{gi}",
                                 lambda e, gi=gi, c0=c0, Gt=Gt, sti=sti, r=r: e.indirect_dma_start(
                                     out=ymg[gi].t[:, c0:c0 + 512], out_offset=None, in_=Gt[:, :],
                                     in_offset=bass.IndirectOffsetOnAxis(ap=idx.t[:, sti:sti + 1], axis=0),
                                     element_offset=r * 1024 * 512),
                                 reads=[idx.b] + list(gbl), writes=[ymg[gi].b])
            for pc in range(4):
                if fused:
                    src = ymg[gi]
                    cb0 = pc * 1024
                else:
                    i = ycount[0] % 2
                    ycount[0] += 1
                    S.dma("sync", f"q_y{i}", ymin[i].t[:], dram["ymix"][tok:tok + 128, pc * 1024:(pc + 1) * 1024], writes=[ymin[i].b])
                    G(lambda e, i=i: e.tensor_copy(out=ymb[i].t[:], in_=ymin[i].t[:]), [ymin[i]], [ymb[i]])
                    src = ymb[i]
                    cb0 = 0
                tpb = tpP.t[:].bitcast(BF16).rearrange("p (a b) -> p a b", b=128)
                for k in range(8):
                    PE(lambda e, src=src, cb0=cb0, k=k, tpb=tpb: e.transpose(tpb[:, k, :], src.t[:, cb0 + k * 128:cb0 + (k + 1) * 128], idb.t[:]), [src, idb], [tpP])
                V(lambda e, pc=pc, st=st, tpb=tpb: e.tensor_copy(out=ymT3[:, pc * 8:(pc + 1) * 8, st * 128:(st + 1) * 128], in_=tpb), [tpP], [ymT])
        for fg in range(4):
            for kc in range(32):
                wb = wload(dram["w_out"][kc * 128:(kc + 1) * 128, fg * 512:(fg + 1) * 512])
                for fi in range(4):
                    PE(lambda e, wb=wb, fi=fi, kc=kc: e.matmul(acc[fi].t[:, :], lhsT=wb.t[:, fi * 128:(fi + 1) * 128], rhs=ymT3[:, kc, :],
                                                               start=(kc == 0), stop=(kc == 31)), [wb, ymT], [acc[fi]])
            for fi in range(4):
                V(lambda e, fi=fi, fg=fg: e.tensor_tensor(out=hT.t[:, fg * 4 + fi, :], in0=hT.t[:, fg * 4 + fi, :], in1=acc[fi].t[:, :], op=ALU.add), [hT, acc[fi]], [hT])
        rmsnorm_scale(0)
        for fb in range(16):
            V(lambda e, fb=fb: e.scalar_tensor_tensor(out=vT.t[:, fb, :], in0=hT.t[:, fb, :], scalar=gv.t[:, fb:fb + 1], in1=rstd.t[:],
                                                      op0=ALU.mult, op1=ALU.mult), [hT, gv, rstd], [vT])
        for gg in range(NFF // 4):
            for kc in range(16):
                wb = wload(dram["w_gate"][kc * 128:(kc + 1) * 128, gg * 512:(gg + 1) * 512])
                for fi in range(4):
                    PE(lambda e, wb=wb, fi=fi, kc=kc: e.matmul(acc[fi].t[:, :], lhsT=wb.t[:, fi * 128:(fi + 1) * 128], rhs=vT.t[:, kc, :],
                                                               start=(kc == 0), stop=(kc == 15)), [wb, vT], [acc[fi]])
            for fi in range(4):
                A(lambda e, fi=fi: e.activation(out=sgt.t[:, fi, :], in_=acc[fi].t[:, :], func=AF.Silu), [acc[fi]], [sgt])
            for kc in range(16):
                wb = wload(dram["w_up"][kc * 128:(kc + 1) * 128, gg * 512:(gg + 1) * 512])
                for fi in range(4):
                    PE(lambda e, wb=wb, fi=fi, kc=kc: e.matmul(acc[fi].t[:, :], lhsT=wb.t[:, fi * 128:(fi + 1) * 128], rhs=vT.t[:, kc, :],
                                                               start=(kc == 0), stop=(kc == 15)), [wb, vT], [acc[fi]])
            for fi in range(4):
                V(lambda e, fi=fi, gg=gg: e.tensor_tensor(out=aT.t[:, gg * 4 + fi, :], in0=sgt.t[:, fi, :], in1=acc[fi].t[:, :], op=ALU.mult), [sgt, acc[fi]], [aT])
        for fg in range(4):
            for kc in range(NFF):
                wb = wload(dram["w_down"][kc * 128:(kc + 1) * 128, fg * 512:(fg + 1) * 512])
                for fi in range(4):
                    PE(lambda e, wb=wb, fi=fi, kc=kc: e.matmul(acc[fi].t[:, :], lhsT=wb.t[:, fi * 128:(fi + 1) * 128], rhs=aT.t[:, kc, :],
                                                               start=(kc == 0), stop=(kc == NFF - 1)), [wb, aT], [acc[fi]])
            for fi in range(4):
                V(lambda e, fi=fi, fg=fg: e.tensor_tensor(out=hT.t[:, fg * 4 + fi, :], in0=hT.t[:, fg * 4 + fi, :], in1=acc[fi].t[:, :], op=ALU.add), [hT, acc[fi]], [hT])
        rmsnorm_scale(16)
        for fb in range(16):
            o = oT[fb % 2]
            V(lambda e, fb=fb, o=o: e.scalar_tensor_tensor(out=o.t[:], in0=hT.t[:, fb, :], scalar=gv.t[:, 16 + fb:17 + fb], in1=rstd.t[:],
                                                           op0=ALU.mult, op1=ALU.mult), [hT, gv, rstd], [o])
            for st in range(NSTP):
                PE(lambda e, st=st, o=o: e.transpose(tpP.t[:, st * 128:(st + 1) * 128], o.t[:, st * 128:(st + 1) * 128], idf.t[:]), [o, idf], [tpP])
            A(lambda e, fb=fb: e.activation(out=ostg[:, :, fb * 128:(fb + 1) * 128], in_=tpP.t[:, :].rearrange("p (a b) -> p a b", b=128), func=AF.Copy), [tpP], [ymT])
        for st in range(NSTP):
            tok = t0 + st * 128
            S.dma("sync", f"q_o{st}", dram["out"][tok:tok + 128, :], ostg[:, st, :], reads=[ymT.b], writes=[])
    for key, ent in S.dma_sems.items():
        if key.startswith("q_o"):
            S._wait("sync", ("dma", ent[0], ent[1], ("dma", key)))


def prep_core_p1(inp, b, q, NT):
    f = np.float32
    w_in = inp["w_in"][0]
    d = {}
    d["x"] = np.ascontiguousarray(inp["x"][b, :NT, :])
    zc = w_in[:, q * 512:(q + 1) * 512]
    dtc = w_in[:, 5120 + q * 8:5120 + (q + 1) * 8]
    xsc = w_in[:, 2048 + q * 512:2048 + (q + 1) * 512]
    Bc = w_in[:, 4096 + q * 128:4096 + (q + 1) * 128]
    Cc = w_in[:, 4608 + q * 128:4608 + (q + 1) * 128]
    d["w1"] = np.ascontiguousarray(np.concatenate([zc, dtc, xsc, Bc, Cc], axis=1))
    rw = w_in[:, 5152:]
    cols = [rw[:, 6144:6240], rw[:, 6240:6336], rw[:, 6336:6592]]
    for j in range(4):
        o = q * 512 + j * 128
        cols += [rw[:, o:o + 128], rw[:, 2048 + o:2048 + o + 128], rw[:, 4096 + o:4096 + o + 128]]
    d["w2"] = np.ascontiguousarray(np.concatenate(cols, axis=1))
    assert d["w1"].shape[1] == N1 and d["w2"].shape[1] == N2
    pv = np.zeros((128, NPV), f)
    cw = inp["ssd_conv_w"][0]; cbias = inp["ssd_conv_b"][0]
    chs = [q * 512 + blk * 128 for blk in range(4)] + [2048 + q * 128, 2560 + q * 128]
    for blk, ch in enumerate(chs):
        for j in range(4):
            pv[:, PV_CW + blk * 4 + j] = cw[j, ch:ch + 128]
        pv[:, PV_CB + blk] = cbias[ch:ch + 128]
    mu = inp["rwkv_mu"][0]
    pv[:96, PV_MU + 0] = mu[6144:6240]
    pv[:96, PV_MU + 1] = mu[6240:6336]
    pv[:, PV_MU + 2] = mu[6336:6464]
    pv[:, PV_MU + 3] = mu[6464:6592]
    for j in range(4):
        o = q * 512 + j * 128
        pv[:, PV_MU + 4 + j * 3 + 0] = mu[o:o + 128]
        pv[:, PV_MU + 4 + j * 3 + 1] = mu[2048 + o:2048 + o + 128]
        pv[:, PV_MU + 4 + j * 3 + 2] = mu[4096 + o:4096 + o + 128]
        pv[:, PV_W0 + j] = inp["rwkv_w0"][0][o:o + 128]
        pv[:, PV_A0 + j] = inp["rwkv_a0"][0][o:o + 128]
        pv[:, PV_KK + j] = inp["rwkv_k_k"][0][o:o + 128]
        pv[:, PV_KA + j] = inp["rwkv_k_a"][0][o:o + 128]
        pv[:, PV_RK + j] = inp["rwkv_r_k"][0][o:o + 128]
    pv[:, PV_G1:PV_G1 + 16] = inp["norm1_g"][0].reshape(16, 128).T
    d["pv"] = pv
    bc = np.zeros((128, NBC), f)
    bc[:, BC_NG:BC_NG + 512] = inp["ssd_norm_g"][0][q * 512:(q + 1) * 512][None]
    bc[:, BC_GNW:BC_GNW + 512] = inp["rwkv_gn_w"][0][q * 512:(q + 1) * 512][None]
    bc[:, BC_GNB:BC_GNB + 512] = inp["rwkv_gn_b"][0][q * 512:(q + 1) * 512][None]
    bc[:, BC_DTB:BC_DTB + 8] = inp["ssd_dt_bias"][0][q * 8:(q + 1) * 8][None]
    bc[:, BC_ALOG:BC_ALOG + 8] = inp["ssd_A_log"][0][q * 8:(q + 1) * 8][None]
    bc[:, BC_D:BC_D + 8] = inp["ssd_D"][0][q * 8:(q + 1) * 8][None]
    d["bc"] = bc
    d["cst"] = make_consts()
    d["lw2"] = np.ascontiguousarray(inp["rwkv_w2"][0][:, q * 512:(q + 1) * 512])
    d["la2"] = np.ascontiguousarray(inp["rwkv_a2"][0][:, q * 512:(q + 1) * 512])
    d["lg2"] = np.ascontiguousarray(inp["rwkv_g2"][0][:, q * 512:(q + 1) * 512])
    return d

from concourse.bass_utils import run_bass_kernel_spmd

SEQ = 8192
NCORES = 8


def _build_prog1(NT):
    nc = bass.Bass("TRN2", target_bir_lowering=False)
    dram = {}

    def din(name, shape):
        dram[name] = nc.dram_tensor(name, list(shape), F32, kind="ExternalInput").ap()
    din("x", [NT, 2048]); din("w1", [2048, N1]); din("w2", [2048, N2]); din("pv", [128, NPV]); din("bc", [128, NBC])
    din("cst", [128, NCS]); din("lw2", [96, 512]); din("la2", [96, 512]); din("lg2", [256, 512])
    dram["ymix"] = nc.dram_tensor("ymix", [NT, 1024], F32, kind="ExternalOutput").ap()
    S, es = build_p1(nc, NT, dram)
    with nc.Block() as block:
        S.emit(block)
    es.close()
    return nc


def _build_prog2(NT2):
    nc = bass.Bass("TRN2", target_bir_lowering=False)
    dram = {}

    def din(name, shape):
        dram[name] = nc.dram_tensor(name, list(shape), F32, kind="ExternalInput").ap()
    din("xres", [NT2, 2048]); din("ymix", [NT2, 4096]); din("w_out", [4096, 2048]); din("w_gate", [2048, 5632])
    din("w_up", [2048, 5632]); din("w_down", [5632, 2048]); din("gv", [128, 32]); din("idf", [128, 128])
    dram["out"] = nc.dram_tensor("out", [NT2, 2048], F32, kind="ExternalOutput").ap()
    es = ExitStack()
    S = Sched(nc, es)
    build_p2(nc, S, es, NT2, dram)
    with nc.Block() as block:
        S.emit(block)
    es.close()
    return nc


def _build_fused(NT, NT2):
    nc = bass.Bass("TRN2", target_bir_lowering=False)
    dram = {}

    def din(name, shape, dt=F32):
        dram[name] = nc.dram_tensor(name, list(shape), dt, kind="ExternalInput").ap()
    din("x", [NT, 2048]); din("w1", [2048, N1]); din("w2", [2048, N2]); din("pv", [128, NPV]); din("bc", [128, NBC])
    din("cst", [128, NCS]); din("lw2", [96, 512]); din("la2", [96, 512]); din("lg2", [256, 512])
    din("xres", [NT2, 2048]); din("w_out", [4096, 2048]); din("w_gate", [2048, 5632])
    din("w_up", [2048, 5632]); din("w_down", [5632, 2048]); din("gv", [128, 32]); din("idf", [128, 128])
    din("idx", [128, NT2 // 128], mybir.dt.uint32)
    dram["out"] = nc.dram_tensor("out", [NT2, 2048], F32, kind="ExternalOutput").ap()
    nslab = NT // 1024
    fused = {
        "YS": nc.dram_tensor("YS", [NT, 512], BF16), "YR": nc.dram_tensor("YR", [NT, 512], BF16),
        "GS": nc.dram_tensor("GS", [4 * NT, 512], BF16), "GR": nc.dram_tensor("GR", [4 * NT, 512], BF16),
        "UT": nc.dram_tensor("UT", [NT // TT, 128, 16 * TT], BF16),
        "gsB": [Buf(f"gs{k}") for k in range(nslab)], "grB": [Buf(f"gr{k}") for k in range(nslab)],
        "groups": [[0, 1, 2, 3], [4, 5, 6, 7]],
    }
    ses = ExitStack()
    S = Sched(nc, ses)
    es1 = ExitStack()
    build_p1(nc, NT, dram, S=S, es=es1, fused=fused)
    S.barrier()
    es1.close()
    es2 = ExitStack()
    build_p2(nc, S, es2, NT2, dram, fused=fused)
    with nc.Block() as block:
        S.emit(block)
    es2.close()
    ses.close()
    return nc


def kernel(**inp):
    inp = {k: np.asarray(v) for k, v in inp.items()}
    B = inp["x"].shape[0]
    NT = inp["x"].shape[1]
    NT2 = B * NT // NCORES
    nc = _build_fused(NT, NT2)
    gv = np.ascontiguousarray(np.concatenate([inp["norm2_g"][0].reshape(16, 128).T, inp["norm_f_g"].reshape(16, 128).T], axis=1).astype(np.float32))
    idf = np.eye(128, dtype=np.float32)
    shared = {"w_out": np.ascontiguousarray(inp["w_out"][0]), "w_gate": np.ascontiguousarray(inp["w_gate"][0]),
              "w_up": np.ascontiguousarray(inp["w_up"][0]), "w_down": np.ascontiguousarray(inp["w_down"][0]),
              "gv": gv, "idf": idf}
    maps = []
    for c in range(NCORES):
        b, j = c // 4, c % 4
        m = prep_core_p1(inp, b, j, NT)
        m.update(shared)
        m["xres"] = np.ascontiguousarray(inp["x"][b, j * NT2:(j + 1) * NT2, :])
        g = j * NT2 + np.arange(NT2 // 128)[None, :] * 128 + np.arange(128)[:, None]
        m["idx"] = ((g // 1024) * 4096 + (g % 1024)).astype(np.uint32)
        maps.append(m)
    res = run_bass_kernel_spmd(nc, maps, core_ids=list(range(NCORES)))
    out = np.stack([np.concatenate([res.results[b * 4 + j]["out"] for j in range(4)], axis=0) for b in range(B)], axis=0)
    return out.astype(np.float32)
```
